# Optimizing a Trainium2 kernel written in Bass

```python
import math
import jax
import jax.numpy as jnp
from jax import lax
import numpy as np

D_MODEL = 2048
BATCH = 4
SEQ = 2048
DEPTH = 4
DEC_BATCH = 32
DEC_SEQ = 64
PAST_LEN = 2048

CHUNK = 64
N_MIXERS = 3
N_LAYERS_A = (DEPTH + 2) // 3
N_LAYERS_B = (DEPTH + 1) // 3
N_LAYERS_C = DEPTH // 3
DEEPNORM_ALPHA = (2.0 * DEPTH) ** 0.25
DEEPNORM_BETA = (8.0 * DEPTH) ** -0.25
LN_EPS = 1e-5
RMS_EPS = 1e-6
NEG_INF = -1e30

SSM_GROUP = 16
SSM_GROUPS = D_MODEL // SSM_GROUP
SSM_STATE = 64
SSM_DT_MIN = 1e-3
SSM_DT_MAX = 1e-1

N_HEADS_B = 16
HEAD_DIM_B = D_MODEL // N_HEADS_B
LEFT_CHUNKS = 8
BAND_CHUNKS = LEFT_CHUNKS + 1
BAND_ROWS_MAX = LEFT_CHUNKS * CHUNK
MAX_REL_DIST = 128

N_HEADS_C = 16
HGRN_KDIM = D_MODEL // N_HEADS_C
HGRN_VDIM = D_MODEL // N_HEADS_C
HGRN_BLOCK = 16

D_FF = 4 * D_MODEL

BAND_ROWS = min(BAND_ROWS_MAX, PAST_LEN)

kernel_name = 'hybrid_s5_chunkattn_hgrn2_stream_step'


def layer_norm(x, g, b):
    xf = x.astype(jnp.float32)
    mu = jnp.mean(xf, axis=-1, keepdims=True)
    var = jnp.mean(jnp.square(xf - mu), axis=-1, keepdims=True)
    y = (xf - mu) * lax.rsqrt(var + LN_EPS) * g.astype(jnp.float32) + b.astype(jnp.float32)
    return y.astype(x.dtype)


def sq_relu_mlp(x, w1, w2):
    return jnp.square(jax.nn.relu(x @ w1)) @ w2


def _linear_combine(e1, e2):
    a1, h1 = e1
    a2, h2 = e2
    return a1 * a2, a2 * h1 + h2


def s5_mixer(x, h0_re, h0_im, a_re, a_im, log_dt, b_re, b_im, c_re, c_im, d_skip, w_out, w_gate):
    n, t, _ = x.shape
    f32 = jnp.float32
    u = x.astype(f32).reshape(n, t, SSM_GROUPS, SSM_GROUP)
    lam = lax.complex(a_re.astype(f32), a_im.astype(f32))
    dt = jnp.exp(log_dt.astype(f32))[:, None]
    a_bar = jnp.exp(lam * dt)
    b_bar = ((a_bar - 1.0) / lam)[..., None] * lax.complex(b_re.astype(f32), b_im.astype(f32))
    c = lax.complex(c_re.astype(f32), c_im.astype(f32))
    bu = jnp.einsum('gpq,ntgq->ntgp', b_bar, u.astype(jnp.complex64))
    a_seq = jnp.broadcast_to(a_bar, (1, t) + a_bar.shape)
    a_cum, h = lax.associative_scan(_linear_combine, (a_seq, bu), axis=1)
    if h0_re is not None:
        h0 = lax.complex(h0_re.astype(f32), h0_im.astype(f32))
        h = h + a_cum * h0[:, None]
    y = jnp.real(jnp.einsum('gqp,ntgp->ntgq', c, h)) + d_skip.astype(f32).reshape(SSM_GROUPS, SSM_GROUP) * u
    y = jax.nn.gelu(y.reshape(n, t, D_MODEL)).astype(x.dtype)
    out = (y @ w_out) * jax.nn.sigmoid(y @ w_gate)
    h_last = h[:, -1]
    return out.astype(x.dtype), jnp.real(h_last), jnp.imag(h_last)


def _rel_bias(table, qpos, kpos):
    rel = jnp.clip(qpos[:, None] - kpos[None, :], -MAX_REL_DIST, MAX_REL_DIST) + MAX_REL_DIST
    return table.astype(jnp.float32)[:, rel]


def _qkv_heads(x, w_qkv):
    n, t, _ = x.shape
    q, k, v = jnp.split(x @ w_qkv, 3, axis=-1)
    shp = (n, t, N_HEADS_B, HEAD_DIM_B)
    return q.reshape(shp), k.reshape(shp), v.reshape(shp)


def chunk_attn_prompt(x, w_qkv, rel_table, w_o):
    n, t, _ = x.shape
    nc = t // CHUNK
    q, k, v = _qkv_heads(x, w_qkv)
    cshape = (n, nc, CHUNK, N_HEADS_B, HEAD_DIM_B)
    pad = jnp.zeros((n, LEFT_CHUNKS, CHUNK, N_HEADS_B, HEAD_DIM_B), k.dtype)
    kp = jnp.concatenate([pad, k.reshape(cshape)], axis=1)
    vp = jnp.concatenate([pad, v.reshape(cshape)], axis=1)
    band = jnp.arange(nc)[:, None] + jnp.arange(BAND_CHUNKS)[None, :]
    kb = kp[:, band].reshape(n, nc, BAND_CHUNKS * CHUNK, N_HEADS_B, HEAD_DIM_B)
    vb = vp[:, band].reshape(n, nc, BAND_CHUNKS * CHUNK, N_HEADS_B, HEAD_DIM_B)
    valid = jnp.repeat(band >= LEFT_CHUNKS, CHUNK, axis=1)
    bias = _rel_bias(rel_table, jnp.arange(CHUNK), jnp.arange(BAND_CHUNKS * CHUNK) - LEFT_CHUNKS * CHUNK)
    s = jnp.einsum('ncqhd,nckhd->nchqk', q.reshape(cshape), kb, preferred_element_type=jnp.float32)
    s = s * (HEAD_DIM_B ** -0.5) + bias[None, None]
    s = jnp.where(valid[None, :, None, None, :], s, NEG_INF)
    p = jax.nn.softmax(s, axis=-1).astype(v.dtype)
    o = jnp.einsum('nchqk,nckhd->ncqhd', p, vb)
    y = o.reshape(n, t, D_MODEL) @ w_o
    rows = min(BAND_ROWS_MAX, t)
    return y, k[:, t - rows:], v[:, t - rows:]


def chunk_attn_sample(x, k_cache, v_cache, w_qkv, rel_table, w_o):
    n, t, _ = x.shape
    rows = k_cache.shape[1]
    q, k, v = _qkv_heads(x, w_qkv)
    kk = jnp.concatenate([k_cache.astype(k.dtype), k], axis=1)
    vv = jnp.concatenate([v_cache.astype(v.dtype), v], axis=1)
    kpos = jnp.concatenate([jnp.arange(rows) - rows, jnp.arange(t)])
    bias = _rel_bias(rel_table, jnp.arange(t), kpos)
    s = jnp.einsum('nqhd,nkhd->nhqk', q, kk, preferred_element_type=jnp.float32) * (HEAD_DIM_B ** -0.5) + bias[None]
    p = jax.nn.softmax(s, axis=-1).astype(v.dtype)
    o = jnp.einsum('nhqk,nkhd->nqhd', p, vv)
    return o.reshape(n, t, D_MODEL) @ w_o, k, v


def hgrn2_mixer(x, s0, w_in, lb, norm_g, w_o):
    n, t, _ = x.shape
    f32 = jnp.float32
    q, f_logit, i_val, g = jnp.split(x @ w_in, 4, axis=-1)
    lbf = lb.astype(f32)
    f = lbf + (1.0 - lbf) * jax.nn.sigmoid(f_logit.astype(f32))
    log_f = jnp.log(f)
    k = 1.0 - f
    pad = (-t) % HGRN_BLOCK
    tp = t + pad
    nb = tp // HGRN_BLOCK

    def blocks(a):
        a = jnp.pad(a.astype(f32), ((0, 0), (0, pad), (0, 0)))
        return a.reshape(n, nb, HGRN_BLOCK, N_HEADS_C, -1).transpose(1, 0, 2, 3, 4)

    qb, kb, vb, lfb = blocks(q), blocks(k), blocks(i_val), blocks(log_f)
    if s0 is None:
        s0 = jnp.zeros((n, N_HEADS_C, HGRN_KDIM, HGRN_VDIM), f32)
    else:
        s0 = s0.astype(f32)
    tril = jnp.tril(jnp.ones((HGRN_BLOCK, HGRN_BLOCK), dtype=bool))
    mid = HGRN_BLOCK // 2

    def step(s, blk):
        qc, kc, vc, lfc = blk
        bcum = jnp.cumsum(lfc, axis=1)
        ref = bcum[:, mid - 1:mid]
        intra = jnp.einsum('nthk,nshk->nhts', qc * jnp.exp(bcum - ref), kc * jnp.exp(ref - bcum))
        intra = jnp.where(tril, intra, 0.0)
        o = jnp.einsum('nhts,nshv->nthv', intra, vc) + jnp.einsum('nthk,nhkv->nthv', qc * jnp.exp(bcum), s)
        blast = bcum[:, -1:]
        s_new = s * jnp.exp(blast[:, 0])[..., None] + jnp.einsum('nshk,nshv->nhkv', kc * jnp.exp(blast - bcum), vc)
        return s_new, o

    s_fin, o = lax.scan(step, s0, (qb, kb, vb, lfb))
    o = o.transpose(1, 0, 2, 3, 4).reshape(n, tp, N_HEADS_C, HGRN_VDIM)[:, :t]
    o = o * lax.rsqrt(jnp.mean(jnp.square(o), axis=-1, keepdims=True) + RMS_EPS) * norm_g.astype(f32)
    o = (o.reshape(n, t, D_MODEL) * jax.nn.silu(g.astype(f32))).astype(x.dtype)
    return o @ w_o, s_fin


def setup_inputs(seed: int = 0) -> dict:
    key = jax.random.key(seed)
    keys = jax.random.split(key, 32)
    counter = iter(range(32))
    f32 = jnp.float32

    def nrm(shape, scale):
        return scale * jax.random.normal(keys[next(counter)], shape, f32)

    d_inv = D_MODEL ** -0.5
    x_prompt = nrm((BATCH, SEQ, D_MODEL), 1.0)
    x_sample = nrm((DEC_BATCH, DEC_SEQ, D_MODEL), 1.0)
    state_ssm_re = nrm((N_LAYERS_A, DEC_BATCH, SSM_GROUPS, SSM_STATE), 0.1)
    state_ssm_im = nrm((N_LAYERS_A, DEC_BATCH, SSM_GROUPS, SSM_STATE), 0.1)
    cache_attn_k = nrm((N_LAYERS_B, DEC_BATCH, BAND_ROWS, N_HEADS_B, HEAD_DIM_B), 1.0)
    cache_attn_v = nrm((N_LAYERS_B, DEC_BATCH, BAND_ROWS, N_HEADS_B, HEAD_DIM_B), 1.0)
    state_hgrn = nrm((N_LAYERS_C, DEC_BATCH, N_HEADS_C, HGRN_KDIM, HGRN_VDIM), 0.5)

    n_idx = jnp.arange(SSM_STATE, dtype=f32)
    ssm_a_re = -0.5 + nrm((N_LAYERS_A, SSM_GROUPS, SSM_STATE), 0.01)
    ssm_a_im = math.pi * n_idx + nrm((N_LAYERS_A, SSM_GROUPS, SSM_STATE), 0.01)
    ssm_log_dt = jax.random.uniform(keys[next(counter)], (N_LAYERS_A, SSM_GROUPS), f32,
                                    math.log(SSM_DT_MIN), math.log(SSM_DT_MAX))
    ssm_b_re = nrm((N_LAYERS_A, SSM_GROUPS, SSM_STATE, SSM_GROUP), SSM_GROUP ** -0.5)
    ssm_b_im = nrm((N_LAYERS_A, SSM_GROUPS, SSM_STATE, SSM_GROUP), SSM_GROUP ** -0.5)
    ssm_c_re = nrm((N_LAYERS_A, SSM_GROUPS, SSM_GROUP, SSM_STATE), 0.5)
    ssm_c_im = nrm((N_LAYERS_A, SSM_GROUPS, SSM_GROUP, SSM_STATE), 0.5)
    ssm_d = nrm((N_LAYERS_A, D_MODEL), 1.0)
    ssm_w_out = nrm((N_LAYERS_A, D_MODEL, D_MODEL), d_inv * DEEPNORM_BETA)
    ssm_w_gate = nrm((N_LAYERS_A, D_MODEL, D_MODEL), d_inv)

    qkv_scale = jnp.concatenate([jnp.ones((2 * D_MODEL,), f32), jnp.full((D_MODEL,), DEEPNORM_BETA, f32)])
    attn_w_qkv = nrm((N_LAYERS_B, D_MODEL, 3 * D_MODEL), d_inv) * qkv_scale
    attn_rel_bias = nrm((N_LAYERS_B, N_HEADS_B, 2 * MAX_REL_DIST + 1), 0.1)
    attn_w_o = nrm((N_LAYERS_B, D_MODEL, D_MODEL), d_inv * DEEPNORM_BETA)

    in_scale = jnp.concatenate([jnp.ones((2 * D_MODEL,), f32), jnp.full((D_MODEL,), DEEPNORM_BETA, f32),
                                jnp.ones((D_MODEL,), f32)])
    hgrn_w_in = nrm((N_LAYERS_C, D_MODEL, 4 * D_MODEL), d_inv) * in_scale
    hgrn_lb_logits = nrm((DEPTH, D_MODEL), 0.1)
    hgrn_norm_g = 1.0 + nrm((N_LAYERS_C, HGRN_VDIM), 0.01)
    hgrn_w_o = nrm((N_LAYERS_C, D_MODEL, D_MODEL), d_inv * DEEPNORM_BETA)

    ln_mix_g = 1.0 + nrm((DEPTH, D_MODEL), 0.01)
    ln_mix_b = nrm((DEPTH, D_MODEL), 0.01)
    ln_ffn_g = 1.0 + nrm((DEPTH, D_MODEL), 0.01)
    ln_ffn_b = nrm((DEPTH, D_MODEL), 0.01)
    ffn_w1 = nrm((DEPTH, D_MODEL, D_FF), d_inv * DEEPNORM_BETA)
    ffn_w2 = nrm((DEPTH, D_FF, D_MODEL), D_FF ** -0.5 * DEEPNORM_BETA)

    return {
        'x_prompt': x_prompt, 'x_sample': x_sample,
        'state_ssm_re': state_ssm_re, 'state_ssm_im': state_ssm_im,
        'cache_attn_k': cache_attn_k, 'cache_attn_v': cache_attn_v,
        'state_hgrn': state_hgrn,
        'ssm_a_re': ssm_a_re, 'ssm_a_im': ssm_a_im, 'ssm_log_dt': ssm_log_dt,
        'ssm_b_re': ssm_b_re, 'ssm_b_im': ssm_b_im, 'ssm_c_re': ssm_c_re, 'ssm_c_im': ssm_c_im,
        'ssm_d': ssm_d, 'ssm_w_out': ssm_w_out, 'ssm_w_gate': ssm_w_gate,
        'attn_w_qkv': attn_w_qkv, 'attn_rel_bias': attn_rel_bias, 'attn_w_o': attn_w_o,
        'hgrn_w_in': hgrn_w_in, 'hgrn_lb_logits': hgrn_lb_logits, 'hgrn_norm_g': hgrn_norm_g,
        'hgrn_w_o': hgrn_w_o,
        'ln_mix_g': ln_mix_g, 'ln_mix_b': ln_mix_b, 'ln_ffn_g': ln_ffn_g, 'ln_ffn_b': ln_ffn_b,
        'ffn_w1': ffn_w1, 'ffn_w2': ffn_w2,
    }


def reference(x_prompt, x_sample, state_ssm_re, state_ssm_im, cache_attn_k, cache_attn_v, state_hgrn,
              ssm_a_re, ssm_a_im, ssm_log_dt, ssm_b_re, ssm_b_im, ssm_c_re, ssm_c_im, ssm_d,
              ssm_w_out, ssm_w_gate, attn_w_qkv, attn_rel_bias, attn_w_o,
              hgrn_w_in, hgrn_lb_logits, hgrn_norm_g, hgrn_w_o,
              ln_mix_g, ln_mix_b, ln_ffn_g, ln_ffn_b, ffn_w1, ffn_w2):
    lb_cum = jnp.cumsum(jax.nn.softmax(hgrn_lb_logits.astype(jnp.float32), axis=0), axis=0)
    lower_bounds = lb_cum - lb_cum[:1]

    yp, ys = x_prompt, x_sample
    ssm_re_p, ssm_im_p, ssm_re_s, ssm_im_s = [], [], [], []
    k_p, v_p, k_s, v_s = [], [], [], []
    hg_p, hg_s = [], []
    for layer in range(DEPTH):
        kind = layer % N_MIXERS
        j = layer // N_MIXERS
        if kind == 0:
            w = (ssm_a_re[j], ssm_a_im[j], ssm_log_dt[j], ssm_b_re[j], ssm_b_im[j],
                 ssm_c_re[j], ssm_c_im[j], ssm_d[j], ssm_w_out[j], ssm_w_gate[j])
            mp, re_p, im_p = s5_mixer(yp, None, None, *w)
            ms, re_s, im_s = s5_mixer(ys, state_ssm_re[j], state_ssm_im[j], *w)
            ssm_re_p.append(re_p)
            ssm_im_p.append(im_p)
            ssm_re_s.append(re_s)
            ssm_im_s.append(im_s)
        elif kind == 1:
            mp, kp_new, vp_new = chunk_attn_prompt(yp, attn_w_qkv[j], attn_rel_bias[j], attn_w_o[j])
            ms, ks_new, vs_new = chunk_attn_sample(ys, cache_attn_k[j], cache_attn_v[j],
                                                   attn_w_qkv[j], attn_rel_bias[j], attn_w_o[j])
            k_p.append(kp_new)
            v_p.append(vp_new)
            k_s.append(ks_new)
            v_s.append(vs_new)
        else:
            w = (hgrn_w_in[j], lower_bounds[layer], hgrn_norm_g[j], hgrn_w_o[j])
            mp, sp_new = hgrn2_mixer(yp, None, *w)
            ms, ss_new = hgrn2_mixer(ys, state_hgrn[j], *w)
            hg_p.append(sp_new)
            hg_s.append(ss_new)
        yp = layer_norm(DEEPNORM_ALPHA * yp + mp, ln_mix_g[layer], ln_mix_b[layer])
        ys = layer_norm(DEEPNORM_ALPHA * ys + ms, ln_mix_g[layer], ln_mix_b[layer])
        yp = layer_norm(DEEPNORM_ALPHA * yp + sq_relu_mlp(yp, ffn_w1[layer], ffn_w2[layer]),
                        ln_ffn_g[layer], ln_ffn_b[layer])
        ys = layer_norm(DEEPNORM_ALPHA * ys + sq_relu_mlp(ys, ffn_w1[layer], ffn_w2[layer]),
                        ln_ffn_g[layer], ln_ffn_b[layer])

    return (yp, ys,
            jnp.stack(ssm_re_p), jnp.stack(ssm_im_p), jnp.stack(k_p), jnp.stack(v_p), jnp.stack(hg_p),
            jnp.stack(ssm_re_s), jnp.stack(ssm_im_s), jnp.stack(k_s), jnp.stack(v_s), jnp.stack(hg_s))
```

```python
import numpy as np
import concourse.bass as bass
import concourse.mybir as mybir
from concourse.bass_utils import run_bass_kernel_spmd

F32 = mybir.dt.float32
BF16 = mybir.dt.bfloat16
AF = mybir.ActivationFunctionType
ALU = mybir.AluOpType
AX = mybir.AxisListType

D = 2048
NCH = 16
SEQ = 2048
NS = 8
DS = 64
T = SEQ + NS * DS
G = 512
NTG = T // G
DFF = 8192
DEPTH = 4
ALPHA = (2.0 * DEPTH) ** 0.25
LN_EPS = 1e-5

SAME_ENGINE_SYNC = True


class Sched:
    ENGS = ("pe", "act", "dve", "pool", "sp")

    def __init__(self, nc, ndma=10):
        self.nc = nc
        self.q = {e: [] for e in self.ENGS}
        self.ticks = {e: 0 for e in self.ENGS}
        self.pending = {e: False for e in self.ENGS}
        self.last_w = {}
        self.readers = {}
        self.waited = {e: {} for e in self.ENGS}
        self.ndma = ndma
        self.dma_val = {}
        self.dma_rr = {e: 0 for e in self.ENGS}
        self.all_dma = []

    def _need(self, eng, tk, waits):
        if tk is None:
            return
        if tk[0] == "e":
            _, e2, n = tk
            if e2 == eng and (eng == "pe" or not SAME_ENGINE_SYNC):
                return
            key = ("e", e2)
            val = n
        else:
            _, qn, idx, val = tk
            key = ("d", qn, idx)
        if self.waited[eng].get(key, 0) >= val:
            return
        cur = waits.get(key, 0)
        if val > cur:
            waits[key] = val

    def op(self, eng, fn, reads=(), writes=(), tick=True, dma=False):
        waits = {}
        for r in reads:
            self._need(eng, self.last_w.get(r), waits)
        for w in writes:
            self._need(eng, self.last_w.get(w), waits)
            for tk in self.readers.get(w, ()):
                self._need(eng, tk, waits)
        if dma:
            idx = self.dma_rr[eng]
            self.dma_rr[eng] = (idx + 1) % self.ndma
            prev = self.dma_val.get((eng, idx), 0)
            if prev:
                self._need(eng, ("d", eng, idx, prev), waits)
            val = prev + 16
            self.dma_val[(eng, idx)] = val
            tk = ("d", eng, idx, val)
            inc = ("d", eng, idx)
        else:
            if tick:
                self.ticks[eng] += 1
                tk = ("e", eng, self.ticks[eng])
                inc = ("e", eng)
                self.pending[eng] = False
            else:
                tk = ("e", eng, self.ticks[eng] + 1)
                inc = None
                self.pending[eng] = True
        for key, val in waits.items():
            self.waited[eng][key] = val
        self.q[eng].append((list(waits.items()), fn, inc))
        for w in writes:
            self.last_w[w] = tk
            self.readers[w] = []
        for r in reads:
            lst = self.readers.setdefault(r, [])
            if tk[0] == "e":
                lst[:] = [x for x in lst if not (x[0] == "e" and x[1] == tk[1])]
            lst.append(tk)
        return tk

    def barrier(self):
        for e in self.ENGS:
            waits = {}
            for e2 in self.ENGS:
                if e2 != e and self.ticks[e2] > 0:
                    self._need(e, ("e", e2, self.ticks[e2]), waits)
            for (qn, idx), val in self.dma_val.items():
                self._need(e, ("d", qn, idx, val), waits)
            for key, val in waits.items():
                self.waited[e][key] = val
            self.q[e].append((list(waits.items()), None, None))

    def finish(self, eng="sp"):
        waits = {}
        for (qn, idx), val in self.dma_val.items():
            self._need(eng, ("d", qn, idx, val), waits)
        self.q[eng].append((list(waits.items()), None, None))

    def simulate(self):
        sem = {}
        pc = {e: 0 for e in self.ENGS}
        progress = True
        while progress:
            progress = False
            for e in self.ENGS:
                while pc[e] < len(self.q[e]):
                    waits, fn, inc = self.q[e][pc[e]]
                    ok = all(sem.get(key, 0) >= val for key, val in waits)
                    if not ok:
                        break
                    if inc is not None:
                        k = ("e", inc[1]) if inc[0] == "e" else ("d", inc[1], inc[2])
                        sem[k] = sem.get(k, 0) + (1 if inc[0] == "e" else 16)
                    pc[e] += 1
                    progress = True
        stuck = {e: (pc[e], len(self.q[e])) for e in self.ENGS if pc[e] < len(self.q[e])}
        if stuck:
            msg = []
            for e, (p, n) in stuck.items():
                waits, fn, inc = self.q[e][p]
                msg.append("%s stuck at %d/%d waits=%s have=%s" % (e, p, n, waits, [sem.get(k, 0) for k, v in waits]))
            raise RuntimeError("DEADLOCK: " + " | ".join(msg))

    def emit(self):
        self.simulate()
        nc = self.nc
        for e in self.ENGS:
            assert not self.pending[e], e
            assert self.ticks[e] < 60000, (e, self.ticks[e])
        import contextlib
        with contextlib.ExitStack() as st:
            esem = {e: st.enter_context(nc.semaphore("se_" + e)) for e in self.ENGS}
            dsem = {}
            for (qn, idx) in self.dma_val:
                dsem[(qn, idx)] = st.enter_context(nc.semaphore("sd_%s_%d" % (qn, idx)))
            block = st.enter_context(nc.Block())

            def run(eng_name):
                def body(eng):
                    for waits, fn, inc in self.q[eng_name]:
                        for key, val in waits:
                            s = esem[key[1]] if key[0] == "e" else dsem[(key[1], key[2])]
                            eng.wait_ge(s, val)
                        if fn is None:
                            continue
                        ins = fn(eng)
                        if inc is not None:
                            if inc[0] == "e":
                                ins.then_inc(esem[inc[1]], 1)
                            else:
                                ins.then_inc(dsem[(inc[1], inc[2])], 16)
                return body

            block.tensor(run("pe"))
            block.scalar(run("act"))
            block.vector(run("dve"))
            block.gpsimd(run("pool"))
            block.sync(run("sp"))


class Ctx:
    pass


def R(name, *idx):
    return (name,) + idx


def xr(t0, W, cs=None):
    cs = range(NCH) if cs is None else cs
    return [R("xT", b, c) for b in range(t0 // 256, (t0 + W + 255) // 256) for c in cs]


def build(cfg):
    nc = bass.Bass("TRN2", target_bir_lowering=False)
    S = Sched(nc)
    K = Ctx()
    K.nc, K.S = nc, S
    K.cfg = cfg
    depth = cfg.get("depth", DEPTH)
    kinds = cfg.get("kinds", [l % 3 for l in range(depth)])
    K.na = max(1, sum(1 for k in kinds if k == 0))

    def din(name, shape):
        return nc.dram_tensor(name, list(shape), F32, kind="ExternalInput").ap()

    def dout(name, shape):
        return nc.dram_tensor(name, list(shape), F32, kind="ExternalOutput").ap()

    K.x_in = din("x_in", [T, D])
    K.ffn_w1 = din("ffn_w1", [cfg.get("wdepth", DEPTH), D, DFF])
    K.ffn_w2 = din("ffn_w2", [cfg.get("wdepth", DEPTH), DFF, D])
    K.lnp = din("lnp", [128, 4 * DEPTH * NCH])
    K.y_out = dout("y_out", [T, D])
    if 0 in kinds:
        na = K.na
        K.s5_l1 = din("s5_l1", [na, 3, 128, 128])
        K.s5_rows = din("s5_rows", [na, NCH, 128, 3, 64])
        K.s5_bT = din("s5_bT", [na, NCH, 128, 2, 64])
        K.s5_cT = din("s5_cT", [na, NCH, 128, 8, 16])
        K.s5_d = din("s5_d", [na, 128, NCH])
        K.s5_h0 = din("s5_h0", [na, NS, 128, 128])
        K.s5_wo = din("s5_wo", [na, D, D])
        K.s5_wg = din("s5_wg", [na, D, D])
        K.ssm_p = dout("ssm_p", [na, 128, 128])
        K.ssm_s = dout("ssm_s", [na, NS, 128, 128])

    if 1 in kinds:
        K.at_wqkv = din("at_wqkv", [D, 3 * D])
        K.at_wo = din("at_wo", [D, D])
        K.at_toep = din("at_toep", [128, 16, 2, 128])
        K.at_far = din("at_far", [128, 16])
        K.at_ck = din("at_ck", [NS, 512, D])
        K.at_cv = din("at_cv", [NS, 512, D])
        K.at_kp = dout("at_kp", [512, D])
        K.at_vp = dout("at_vp", [512, D])
        K.at_ks = dout("at_ks", [NS * DS, D])
        K.at_vs = dout("at_vs", [NS * DS, D])
    if 2 in kinds:
        K.hg_win = din("hg_win", [D, 4 * D])
        K.hg_wo = din("hg_wo", [D, D])
        K.hg_lb = din("hg_lb", [128, 4, NCH])
        K.hg_ng = din("hg_ng", [128, 1])
        K.hg_s0 = din("hg_s0", [NS, 16, 128, 128])
        K.hg_sp = dout("hg_sp", [16, 128, 128])
        K.hg_ss = dout("hg_ss", [NS, 16, 128, 128])
    sb = nc.alloc_sbuf_tensor
    K.xT = sb("xT", [128, NCH, T], BF16)
    big0, big1 = nc.bump_sbuf(65536)
    K.big_off = big0
    K.big = nc.alloc_sbuf_tensor_at("big", [128, 32768], BF16, offset=big0)
    K.stg = [sb("stg%d" % i, [128, 2048], F32) for i in range(3)]
    K.wbf = [sb("wbf%d" % i, [128, 2048], BF16) for i in range(2)]
    K.lnp_sb = sb("lnp_sb", [128, 4 * DEPTH * NCH], F32)
    K.ident_f = sb("ident_f", [128, 128], F32)
    K.ident_b = sb("ident_b", [128, 128], BF16)
    K.ones_b = sb("ones_b", [128, 128], BF16)
    K.zsq = [sb("zsq%d" % i, [128, G], BF16) for i in range(2)]
    K.lnA = sb("lnA", [128, G], F32)
    K.lnB = sb("lnB", [128, G], F32)
    K.lnT = [sb("lnT%d" % i, [128, G], F32) for i in range(2)]
    K.otile = [sb("otile%d" % i, [128, 512], F32) for i in range(2)]
    K.eps_t = sb("eps_t", [128, 1], F32)
    sc0, sc1 = nc.bump_sbuf(14336)
    K.scr_off = sc0
    K.scr_size = 14336
    K.ps = [nc.alloc_psum_tensor("ps%d" % i, [128, 512], F32) for i in range(8)]

    K.stg_rr = 0
    K.wbf_rr = 0
    K.ev_rr = 0
    K.uid = 0

    setup_consts(K)
    load_input(K)
    ia = 0
    for li in range(depth):
        kind = kinds[li]
        layer = li + cfg.get("layer0", 0)
        if kind == 0:
            s5_layer(K, layer, ia)
            ia += 1
        elif kind == 1:
            attn_layer(K, layer)
        elif kind == 2:
            hgrn_layer(K, layer)
        else:
            mixer_none(K, layer)
        if cfg.get("ffn", True):
            ffn(K, layer, last=(li == depth - 1) and not cfg.get("dbg_xT"))
    if cfg.get("dbg_xT"):
        dbg = nc.dram_tensor("dbg", [128, NCH * T], BF16, kind="ExternalOutput").ap()
        S.op("sp", lambda e: e.dma_start(out=dbg, in_=K.xT[:, :, :].rearrange("p c t -> p (c t)")),
             reads=xr(0, T), dma=True)
    if cfg.get("dbg_big"):
        dbg2 = nc.dram_tensor("dbg2", [128, 32768], BF16, kind="ExternalOutput").ap()
        S.op("sp", lambda e: e.dma_start(out=dbg2, in_=K.big[:, :]),
             reads=[R("hT", c) for c in range(64)] + [R("yT", c) for c in range(16)], dma=True)
    S.finish("sp")
    S.emit()
    return nc


def alloc_at(K, name, shape, dtype, off):
    K.uid += 1
    return K.nc.alloc_sbuf_tensor_at("%s_%d" % (name, K.uid), list(shape), dtype, offset=off)


class Arena:
    def __init__(self, K, base, size):
        self.K, self.base, self.size, self.cur = K, base, size, 0

    def get(self, name, shape, dtype):
        n = 1
        for d in shape[1:]:
            n *= d
        nbytes = n * (4 if dtype == F32 else 2)
        nbytes = (nbytes + 31) // 32 * 32
        assert self.cur + nbytes <= self.size, (name, self.cur, nbytes, self.size)
        t = alloc_at(self.K, name, shape, dtype, self.base + self.cur)
        self.cur += nbytes
        return t


def setup_consts(K):
    nc, S = K.nc, K.S
    S.op("pool", lambda e: e.memset(K.ident_f[:, :], 0.0), writes=[R("ident_f")])
    S.op("pool", lambda e: e.affine_select(out=K.ident_f[:, :], in_=K.ident_f[:, :], pattern=[[1, 128]],
                                            compare_op=ALU.not_equal, fill=1.0, base=0, channel_multiplier=-1),
         reads=[R("ident_f")], writes=[R("ident_f")])
    S.op("pool", lambda e: e.tensor_copy(out=K.ident_b[:, :], in_=K.ident_f[:, :]), reads=[R("ident_f")], writes=[R("ident_b")])
    S.op("pool", lambda e: e.memset(K.ones_b[:, :], 1.0), writes=[R("ones_b")])
    S.op("pool", lambda e: e.memset(K.eps_t[:, :], LN_EPS), writes=[R("eps_t")])
    S.op("sp", lambda e: e.dma_start(out=K.lnp_sb[:, :], in_=K.lnp), writes=[R("lnp")], dma=True)


def next_stg(K):
    i = K.stg_rr
    K.stg_rr = (i + 1) % len(K.stg)
    return i


def next_wbf(K):
    i = K.wbf_rr
    K.wbf_rr = (i + 1) % len(K.wbf)
    return i


def evac_engine(K):
    K.ev_rr ^= 1
    return "act" if K.ev_rr else "dve"


def copy_op(K, eng, out, in_, reads, writes):
    if eng == "act":
        return K.S.op("act", lambda e: e.activation(out=out, in_=in_, func=AF.Copy), reads=reads, writes=writes)
    return K.S.op(eng, lambda e: e.tensor_copy(out=out, in_=in_), reads=reads, writes=writes)


def load_input(K):
    S = K.S
    for tt in range(T // 128):
        si = next_stg(K)
        stg = K.stg[si]
        S.op("sp", lambda e, stg=stg, tt=tt: e.dma_start(out=stg[:, :], in_=K.x_in[tt * 128:(tt + 1) * 128, :]),
             writes=[R("stg", si)], dma=True)
        for cb in range(4):
            bank = 4 + (tt * 4 + cb) % 4
            ps = K.ps[bank]
            for j in range(4):
                c = cb * 4 + j
                S.op("pe", lambda e, ps=ps, stg=stg, c=c, j=j: e.transpose(out=ps[:, j * 128:(j + 1) * 128],
                                                                       in_=stg[:, c * 128:(c + 1) * 128],
                                                                       identity=K.ident_f[:, :]),
                     reads=[R("stg", si), R("ident_f")], writes=[R("ps", bank)], tick=(j == 3))
            dst = K.xT[:, cb * 4:(cb + 1) * 4, tt * 128:(tt + 1) * 128]
            src = ps[:, :].rearrange("p (j t) -> p j t", j=4)
            copy_op(K, evac_engine(K), dst, src, [R("ps", bank)], xr(tt * 128, 128, range(cb * 4, cb * 4 + 4)))


def load_w_tile(K, dram_ap):
    S = K.S
    si = next_stg(K)
    wi = next_wbf(K)
    stg = K.stg[si]
    wbf = K.wbf[wi]
    S.op("sp", lambda e: e.dma_start(out=stg[:, :].rearrange("p (k n) -> p k n", k=8), in_=dram_ap), writes=[R("stg", si)], dma=True)
    K.cast_rr = getattr(K, "cast_rr", 0) + 1
    copy_op(K, "act" if K.cast_rr % 2 else "dve", wbf[:, :], stg[:, :], [R("stg", si)], [R("wbf", wi)])
    return wi


def wtile(w2d, kt, ob):
    return w2d[kt * 1024:(kt + 1) * 1024, ob * 256:(ob + 1) * 256].rearrange("(k p) n -> p k n", p=128)


def dense_block(K, w2d, ob, KC, rhs_fn, rhs_res, ncols, banks):
    S = K.S
    nkt = KC // 8
    for kt in range(nkt):
        wi = load_w_tile(K, wtile(w2d, kt, ob))
        wbf = K.wbf[wi]
        for o in range(2):
            ps = K.ps[banks[o]]
            for k in range(8):
                kc = kt * 8 + k
                first = (kt == 0 and k == 0)
                last = (kt == nkt - 1 and k == 7)
                rhs_ap = rhs_fn(kc)
                S.op("pe", lambda e, ps=ps, wbf=wbf, k=k, o=o, rhs_ap=rhs_ap, first=first, last=last:
                     e.matmul(ps[:, 0:ncols], lhsT=wbf[:, k * 256 + o * 128:k * 256 + (o + 1) * 128], rhs=rhs_ap,
                              start=first, stop=last),
                     reads=[R("wbf", wi)] + rhs_res(kc), writes=[R("ps", banks[o])], tick=(last or (o == 1 and k == 7)))


def ln_accum(K, t0, W, c):
    S = K.S
    zc = K.xT[:, c, t0:t0 + W]
    zq = K.zsq[c % 2]
    S.op("act", lambda e: e.activation(out=zq[:, 0:W], in_=zc, func=AF.Square), reads=xr(t0, W, [c]), writes=[R("zsq", c % 2)])
    S.op("pe", lambda e: e.matmul(K.ps[6][:, 0:W], lhsT=K.ones_b[:, :], rhs=zc, start=(c == 0), stop=(c == NCH - 1)),
         reads=xr(t0, W, [c]) + [R("ones_b")], writes=[R("ps", 6)], tick=(c == NCH - 1))
    S.op("pe", lambda e: e.matmul(K.ps[7][:, 0:W], lhsT=K.ones_b[:, :], rhs=zq[:, 0:W], start=(c == 0), stop=(c == NCH - 1)),
         reads=[R("zsq", c % 2), R("ones_b")], writes=[R("ps", 7)], tick=True)


def resid_ln_accum(K, t0, W, c, m_ap, m_res):
    S = K.S
    zc = K.xT[:, c, t0:t0 + W]
    S.op("dve", lambda e: e.scalar_tensor_tensor(out=zc, in0=zc, scalar=ALPHA, in1=m_ap, op0=ALU.mult, op1=ALU.add),
         reads=xr(t0, W, [c]) + m_res, writes=xr(t0, W, [c]))
    ln_accum(K, t0, W, c)


def ln_finish(K, layer, kind, t0, W, last):
    S = K.S
    inv = 1.0 / D
    A, B = K.lnA[:, 0:W], K.lnB[:, 0:W]
    S.op("dve", lambda e: e.tensor_scalar(out=A, in0=K.ps[6][:, 0:W], scalar1=inv, scalar2=None, op0=ALU.mult),
         reads=[R("ps", 6)], writes=[R("lnA")])
    S.op("dve", lambda e: e.tensor_tensor(out=B, in0=A, in1=A, op=ALU.mult), reads=[R("lnA")], writes=[R("lnB")])
    S.op("dve", lambda e: e.scalar_tensor_tensor(out=B, in0=K.ps[7][:, 0:W], scalar=inv, in1=B, op0=ALU.mult, op1=ALU.subtract),
         reads=[R("ps", 7), R("lnB")], writes=[R("lnB")])
    S.op("act", lambda e: e.activation(out=B, in_=B, func=AF.Sqrt, bias=K.eps_t[:, 0:1], scale=1.0),
         reads=[R("lnB"), R("eps_t")], writes=[R("lnB")])
    S.op("dve", lambda e: e.reciprocal(out=B, in_=B), reads=[R("lnB")], writes=[R("lnB")])
    S.op("dve", lambda e: e.tensor_tensor(out=A, in0=A, in1=B, op=ALU.mult), reads=[R("lnA"), R("lnB")], writes=[R("lnA")])
    gi = (2 * kind) * DEPTH * NCH + layer * NCH
    bi = (2 * kind + 1) * DEPTH * NCH + layer * NCH
    for c in range(NCH):
        tmp = K.lnT[c % 2][:, 0:W]
        tr = R("lnT", c % 2)
        zc = K.xT[:, c, t0:t0 + W]
        S.op("dve", lambda e, tmp=tmp, zc=zc: e.tensor_tensor(out=tmp, in0=zc, in1=B, op=ALU.mult),
             reads=xr(t0, W, [c]) + [R("lnB")], writes=[tr])
        S.op("dve", lambda e, tmp=tmp: e.tensor_tensor(out=tmp, in0=tmp, in1=A, op=ALU.subtract),
             reads=[tr, R("lnA")], writes=[tr])
        S.op("act", lambda e, tmp=tmp, zc=zc, c=c: e.activation(out=zc, in_=tmp, func=AF.Identity,
                                                                bias=K.lnp_sb[:, bi + c:bi + c + 1],
                                                                scale=K.lnp_sb[:, gi + c:gi + c + 1]),
             reads=[tr, R("lnp")], writes=xr(t0, W, [c]))
        if last:
            ot = K.otile[c % 2]
            otr = R("otile", c % 2)
            S.op("act", lambda e, tmp=tmp, c=c: e.activation(out=tmp, in_=tmp, func=AF.Identity,
                                                          bias=K.lnp_sb[:, bi + c:bi + c + 1],
                                                          scale=K.lnp_sb[:, gi + c:gi + c + 1]),
                 reads=[tr, R("lnp")], writes=[tr])
            bank = 4 + c % 2
            nj = W // 128
            for j in range(nj):
                S.op("pe", lambda e, tmp=tmp, j=j, bank=bank: e.transpose(out=K.ps[bank][:, j * 128:(j + 1) * 128],
                                                                          in_=tmp[:, j * 128:(j + 1) * 128],
                                                                          identity=K.ident_f[:, :]),
                     reads=[tr, R("ident_f")], writes=[R("ps", bank)], tick=(j == nj - 1))
            S.op("dve", lambda e, ot=ot, bank=bank: e.tensor_copy(out=ot[:, 0:W], in_=K.ps[bank][:, 0:W]),
                 reads=[R("ps", bank)], writes=[otr])
            dst = K.y_out[t0:t0 + W, c * 128:(c + 1) * 128].rearrange("(j p) f -> p j f", p=128)
            S.op("pool", lambda e, ot=ot, dst=dst, nj=nj: e.dma_start(out=dst, in_=ot[:, 0:W].rearrange("p (j f) -> p j f", j=nj)),
                 reads=[otr], writes=[R("yout", t0, c)], dma=True)


def mixer_none(K, layer):
    S = K.S
    for tg in range(NTG):
        t0 = tg * G
        for c in range(NCH):
            zc = K.xT[:, c, t0:t0 + G]
            S.op("dve", lambda e, zc=zc: e.tensor_scalar(out=zc, in0=zc, scalar1=ALPHA, scalar2=None, op0=ALU.mult),
                 reads=xr(t0, G, [c]), writes=xr(t0, G, [c]))
            ln_accum(K, t0, G, c)
        ln_finish(K, layer, 0, t0, G, False)


def ffn(K, layer, last):
    S = K.S
    hT = K.big[:, :].rearrange("p (c t) -> p c t", c=64)
    for tg in range(NTG):
        t0 = tg * G
        for ob in range(DFF // 256):
            banks = [2 * (ob % 2), 2 * (ob % 2) + 1]
            dense_block(K, K.ffn_w1[layer], ob, NCH, lambda kc: K.xT[:, kc, t0:t0 + G], lambda kc: xr(t0, G, [kc]), G, banks)
            for o in range(2):
                relu2(K, ob * 2 + o, banks[o], hT)
        for ob in range(D // 256):
            banks = [2 * (ob % 2), 2 * (ob % 2) + 1]
            dense_block(K, K.ffn_w2[layer], ob, 64, lambda kc: hT[:, kc, :], lambda kc: [R("hT", kc)], G, banks)
            for o in range(2):
                resid_ln_accum(K, t0, G, ob * 2 + o, K.ps[banks[o]][:, 0:G], [R("ps", banks[o])])
        ln_finish(K, layer, 1, t0, G, last)


def relu2(K, oc, bank, hT):
    S = K.S
    tmp = K.lnT[oc % 2]
    S.op("act", lambda e: e.activation(out=tmp[:, :], in_=K.ps[bank][:, :], func=AF.Relu),
         reads=[R("ps", bank)], writes=[R("lnT", oc % 2)])
    S.op("act", lambda e: e.activation(out=hT[:, oc, :], in_=tmp[:, :], func=AF.Square),
         reads=[R("lnT", oc % 2)], writes=[R("hT", oc)])


S5_POW = [1, 2, 3, 4, 5, 6, 7, 8, 16, 32, 64, 128, 256, 512, 1024]


def tt(K, eng, out, a, b, op, reads, writes):
    return K.S.op(eng, lambda e: e.tensor_tensor(out=out, in0=a, in1=b, op=op), reads=reads, writes=writes)


def cmul(K, eng, outr, outi, ar, ai, br, bi, t1, t2, rr, ww):
    tr = [R("s5tmp")]
    tt(K, eng, t1, ar, br, ALU.mult, rr, tr)
    tt(K, eng, t2, ai, bi, ALU.mult, rr, tr)
    tt(K, eng, t2, t1, t2, ALU.subtract, tr, tr)
    tt(K, eng, t1, ar, bi, ALU.mult, rr, tr)
    tt(K, eng, outi, ai, br, ALU.mult, rr + tr, ww)
    tt(K, eng, outi, outi, t1, ALU.add, ww + tr, ww)
    K.S.op(eng, lambda e: e.tensor_copy(out=outr, in_=t2), reads=tr, writes=ww)


def cexp_abar(K, eng, ar_t, ai_t, ldt_t, outr, outi, tmps, rr, ww):
    S = K.S
    d, c, s = tmps
    tr = [R("s5tmp2")]
    S.op("act", lambda e: e.activation(out=d, in_=ldt_t, func=AF.Exp), reads=rr, writes=tr)
    tt(K, eng, c, ai_t, d, ALU.mult, rr + tr, tr)
    tt(K, eng, d, ar_t, d, ALU.mult, rr + tr, tr)
    S.op("act", lambda e: e.activation(out=d, in_=d, func=AF.Exp), reads=tr, writes=tr)
    S.op("act", lambda e: e.activation(out=s, in_=c, func=AF.Sin, scale=1.0 / 16), reads=tr, writes=tr)
    S.op("act", lambda e: e.activation(out=c, in_=c, func=AF.Sin, scale=-1.0 / 16, bias=K.halfpi[:, 0:1]), reads=tr + [R("consts5")], writes=tr)
    for _ in range(4):
        tt(K, eng, outr, c, s, ALU.mult, tr, ww)
        tt(K, eng, c, c, c, ALU.mult, tr, tr)
        tt(K, eng, s, s, s, ALU.mult, tr, tr)
        tt(K, eng, c, c, s, ALU.subtract, tr, tr)
        S.op(eng, lambda e: e.tensor_scalar(out=s, in0=outr, scalar1=2.0, scalar2=None, op0=ALU.mult), reads=ww, writes=tr)
    tt(K, eng, outr, c, d, ALU.mult, tr, ww)
    tt(K, eng, outi, s, d, ALU.mult, tr, ww)


def s5_layer(K, layer, ia):
    nc, S = K.nc, K.S
    S.barrier()
    bg = Arena(K, K.big_off, 65536)
    yT = bg.get("yT", [128, NCH, 1024], BF16)
    PW = bg.get("PW", [128, len(S5_POW), 2, 128], F32)
    bT = bg.get("bT", [128, 8, 128], F32)
    H0 = bg.get("H0", [128, NS, 128], F32)
    FINs = bg.get("FINs", [128, NS, 128], F32)
    Hb = [bg.get("Hb%d" % i, [128, 1024], BF16) for i in range(2)]
    sc = Arena(K, K.scr_off, K.scr_size)
    P = sc.get("P", [128, 4, 132], F32)
    Sb = sc.get("Sb", [128, 4, 128], BF16)
    CAR = sc.get("CAR", [128, 128], F32)
    FINp = sc.get("FINp", [128, 128], F32)
    Ct = sc.get("Ct", [128, 8, 128], BF16)
    cst = sc.get("cst", [128, 8, 16], F32)
    rt = [sc.get("rt%d" % i, [128, 64], F32) for i in range(12)]
    tmpm = [sc.get("tmpm%d" % i, [128, 128], F32) for i in range(4)]
    Jsg = sc.get("Jsg", [128, 128], F32)
    rowmask = sc.get("rowmask", [128, 8], F32)
    dcol = sc.get("dcol", [128, NCH], F32)
    K.halfpi = sc.get("halfpi", [128, 1], F32)
    l1 = [K.lnT[0][:, 0:128], K.lnT[0][:, 128:256], K.lnT[0][:, 256:384], K.lnT[1][:, 0:128], K.lnT[1][:, 128:256], K.lnT[1][:, 256:384],
          K.lnT[0][:, 384:512], K.lnT[1][:, 384:512]]
    l1r = [R("lnT", 0), R("lnT", 1)]
    CS = [R("consts5")]

    S.op("pool", lambda e: e.memset(K.halfpi[:, :], float(np.pi / 2)), writes=CS)
    S.op("pool", lambda e: e.memset(Jsg[:, :], 0.0), writes=CS)
    S.op("pool", lambda e: e.affine_select(out=Jsg[:, :], in_=Jsg[:, :], pattern=[[1, 128]], compare_op=ALU.not_equal,
                                            fill=1.0, base=-64, channel_multiplier=-1), reads=CS, writes=CS)
    S.op("pool", lambda e: e.affine_select(out=Jsg[:, :], in_=Jsg[:, :], pattern=[[1, 128]], compare_op=ALU.not_equal,
                                            fill=-1.0, base=64, channel_multiplier=-1), reads=CS, writes=CS)
    S.op("pool", lambda e: e.memset(rowmask[:, :], 1.0), writes=CS)
    S.op("pool", lambda e: e.affine_select(out=rowmask[:, :], in_=rowmask[:, :], pattern=[[-16, 8]], compare_op=ALU.is_ge,
                                            fill=0.0, base=0, channel_multiplier=1), reads=CS, writes=CS)
    S.op("pool", lambda e: e.affine_select(out=rowmask[:, :], in_=rowmask[:, :], pattern=[[16, 8]], compare_op=ALU.is_ge,
                                            fill=0.0, base=15, channel_multiplier=-1), reads=CS, writes=CS)
    S.op("pool", lambda e: e.memset(Ct[:, :, :], 0.0), writes=[R("Ct")])
    S.op("sp", lambda e: e.dma_start(out=dcol[:, :], in_=K.s5_d[ia]), writes=[R("dcol")], dma=True)

    for i in range(3):
        S.op("sp", lambda e, i=i: e.dma_start(out=l1[i], in_=K.s5_l1[ia, i]), writes=l1r, dma=True)

    def pw(n, ri):
        return PW[:, S5_POW.index(n), ri, :]
    PWR = [R("PW")]
    cexp_abar(K, "dve", l1[0], l1[1], l1[2], pw(1, 0), pw(1, 1), [l1[3], l1[4], l1[5]], l1r, PWR)
    for n in range(2, 9):
        cmul(K, "dve", pw(n, 0), pw(n, 1), pw(n - 1, 0), pw(n - 1, 1), pw(1, 0), pw(1, 1), l1[6], l1[7], PWR + l1r, PWR)
    for n in S5_POW[8:]:
        cmul(K, "dve", pw(n, 0), pw(n, 1), pw(n // 2, 0), pw(n // 2, 1), pw(n // 2, 0), pw(n // 2, 1), l1[6], l1[7], PWR + l1r, PWR)

    si = next_stg(K)
    stg = K.stg[si]
    S.op("sp", lambda e: e.dma_start(out=stg[:, 0:NS * 128].rearrange("g (s k) -> g s k", s=NS),
                                     in_=K.s5_h0[ia].rearrange("s g k -> g s k")), writes=[R("stg", si)], dma=True)
    for s in range(NS):
        bank = 4 + s % 2
        S.op("pe", lambda e, s=s, bank=bank: e.transpose(out=K.ps[bank][:, 0:128], in_=stg[:, s * 128:(s + 1) * 128], identity=K.ident_f[:, :]),
             reads=[R("stg", si), R("ident_f")], writes=[R("ps", bank)])
        copy_op(K, evac_engine(K), H0[:, s, :], K.ps[bank][:, 0:128], [R("ps", bank)], [R("H0")])

    passes = [(0, 1024, "A"), (1024, 1024, "B"), (2048, 512, "C")]
    if K.cfg.get("s5_passes"):
        passes = K.cfg["s5_passes"]
    mats_res = [[R("stg", 0)], [R("stg", 1)], [R("stg", 2)], [R("wbf", 0), R("wbf", 1)]]

    def mats(i):
        if i < 3:
            mb = K.stg[i][:, 0:1024].bitcast(BF16).rearrange("p (m c) -> p m c", m=16)
            mf = K.stg[i][:, 1024:2048].rearrange("p (m c) -> p m c", m=8)
        else:
            mb = K.wbf[0][:, :].rearrange("p (m c) -> p m c", m=16)
            mf = K.wbf[1][:, :].bitcast(F32).rearrange("p (m c) -> p m c", m=8)
        return mb, mf

    for (t0, W, pk) in passes:
        nch = W // 8
        ncol = (nch + 1) if pk != "C" else 9
        for fc in range(NCH):
            RT = [R("rt")]
            si = next_stg(K)
            stg2 = K.stg[si]
            S.op("sp", lambda e, stg2=stg2, fc=fc: e.dma_start(out=stg2[:, 0:192].rearrange("p (i k) -> p i k", i=3), in_=K.s5_rows[ia, fc]),
                 writes=[R("stg", si)], dma=True)
            S.op("sp", lambda e, stg2=stg2, fc=fc: e.dma_start(out=stg2[:, 192:320].rearrange("p (i k) -> p i k", i=2), in_=K.s5_bT[ia, fc]),
                 writes=[R("stg", si)], dma=True)
            S.op("sp", lambda e, fc=fc: e.dma_start(out=cst[:, :, :], in_=K.s5_cT[ia, fc]), writes=[R("cst")], dma=True)
            lam_r, lam_i, ldt = stg2[:, 0:64], stg2[:, 64:128], stg2[:, 128:192]
            b_r, b_i = stg2[:, 192:256], stg2[:, 256:320]
            SG = [R("stg", si)]
            abr, abi = rt[0], rt[1]
            cexp_abar(K, "dve", lam_r, lam_i, ldt, abr[:, :], abi[:, :], [rt[2][:, :], rt[3][:, :], rt[4][:, :]], SG, RT)
            nr, den, cr, ci = rt[2][:, :], rt[3][:, :], rt[5][:, :], rt[6][:, :]
            t1, t2 = rt[7][:, :], rt[8][:, :]
            S.op("dve", lambda e: e.tensor_scalar(out=nr, in0=abr[:, :], scalar1=-1.0, scalar2=None, op0=ALU.add), reads=RT, writes=RT)
            tt(K, "dve", den, lam_r, lam_r, ALU.mult, SG, RT)
            tt(K, "dve", t1, lam_i, lam_i, ALU.mult, SG, RT)
            tt(K, "dve", den, den, t1, ALU.add, RT, RT)
            S.op("dve", lambda e: e.reciprocal(out=den, in_=den), reads=RT, writes=RT)
            tt(K, "dve", cr, nr, lam_r, ALU.mult, RT + SG, RT)
            tt(K, "dve", t1, abi[:, :], lam_i, ALU.mult, RT + SG, RT)
            tt(K, "dve", cr, cr, t1, ALU.add, RT, RT)
            tt(K, "dve", cr, cr, den, ALU.mult, RT, RT)
            tt(K, "dve", ci, abi[:, :], lam_r, ALU.mult, RT + SG, RT)
            tt(K, "dve", t1, nr, lam_i, ALU.mult, RT + SG, RT)
            tt(K, "dve", ci, ci, t1, ALU.subtract, RT, RT)
            tt(K, "dve", ci, ci, den, ALU.mult, RT, RT)
            BT = [R("bT")]
            cmul(K, "dve", bT[:, 0, 0:64], bT[:, 0, 64:128], cr, ci, b_r, b_i, t1, t2, RT + SG, BT)
            for tau in range(1, 8):
                cmul(K, "dve", bT[:, tau, 0:64], bT[:, tau, 64:128], bT[:, tau - 1, 0:64], bT[:, tau - 1, 64:128],
                     abr[:, :], abi[:, :], t1, t2, RT + BT, BT)
            for g8 in range(8):
                S.op("pool", lambda e, g8=g8: e.tensor_copy(out=Ct[0:64, g8, 16 * g8:16 * g8 + 16], in_=cst[0:64, g8, :]),
                     reads=[R("cst")], writes=[R("Ct")])
                S.op("pool", lambda e, g8=g8: e.tensor_scalar(out=Ct[64:128, g8, 16 * g8:16 * g8 + 16], in0=cst[64:128, g8, :],
                                                              scalar1=-1.0, scalar2=None, op0=ALU.mult),
                     reads=[R("cst")], writes=[R("Ct")])

            def uT(s, fc=fc, t0=t0, W=W):
                return K.xT[:, fc, t0 + s:t0 + W:8]
            ures = xr(t0, W, [fc])
            ybanks = [5, 6]
            for b4 in range(2):
                g0 = fc * 8 + b4 * 4
                for i in range(4):
                    g = g0 + i
                    mb, mf = mats(i)
                    for n in range(1, 9):
                        tm = tmpm[n % 4]
                        S.op("act", lambda e, tm=tm, n=n, g=g: e.activation(out=tm[:, :], in_=Jsg[:, :], func=AF.Copy, scale=pw(n, 1)[:, g:g + 1]),
                             reads=PWR + CS, writes=[R("tmpm", n % 4)])
                        S.op("dve", lambda e, tm=tm, n=n, g=g, mb=mb: e.scalar_tensor_tensor(out=mb[:, n - 1, :], in0=K.ident_f[:, :], scalar=pw(n, 0)[:, g:g + 1],
                                                                                               in1=tm[:, :], op0=ALU.mult, op1=ALU.add),
                             reads=PWR + [R("tmpm", n % 4), R("ident_f")], writes=mats_res[i])
                    for k in range(8):
                        n = 8 << k
                        tm = tmpm[k % 4]
                        S.op("act", lambda e, tm=tm, n=n, g=g: e.activation(out=tm[:, :], in_=Jsg[:, :], func=AF.Copy, scale=pw(n, 1)[:, g:g + 1]),
                             reads=PWR + CS, writes=[R("tmpm", k % 4)])
                        S.op("dve", lambda e, tm=tm, n=n, g=g, mf=mf, k=k: e.scalar_tensor_tensor(out=mf[:, k, :], in0=K.ident_f[:, :], scalar=pw(n, 0)[:, g:g + 1],
                                                                                                    in1=tm[:, :], op0=ALU.mult, op1=ALU.add),
                             reads=PWR + [R("tmpm", k % 4), R("ident_f")], writes=mats_res[i])
                    for tau in range(8):
                        S.op("act", lambda e, tau=tau, g=g, mb=mb: e.activation(out=mb[:, 8 + tau, :], in_=bT[:, tau, :], func=AF.Copy, scale=rowmask[:, g % 8:g % 8 + 1]),
                             reads=BT + CS, writes=mats_res[i])
                for i in range(4):
                    mb, mf = mats(i)
                    bank = 3 + i // 2
                    off = (i % 2) * 256
                    for s in range(8):
                        u_ap = uT(s)
                        o_ap = K.ps[bank][:, off:off + nch]
                        S.op("pe", lambda e, mb=mb, s=s, u_ap=u_ap, o_ap=o_ap: e.matmul(o_ap, lhsT=mb[:, 8 + 7 - s, :], rhs=u_ap,
                                                                                                 start=(s == 0), stop=(s == 7)),
                             reads=mats_res[i] + ures, writes=[R("ps", bank)], tick=(s == 7))
                for b in range(2):
                    bank = 3 + b
                    src = K.ps[bank][:, :].rearrange("p (i c) -> p i c", i=2)[:, :, 0:nch]
                    if pk != "C":
                        dst = P[:, 2 * b:2 * b + 2, 1:1 + nch]
                    else:
                        dst = P[:, 2 * b:2 * b + 2, 0:72].rearrange("p i (s c) -> p i s c", c=9)[:, :, :, 1:9]
                        src = K.ps[bank][:, :].rearrange("p (i s c) -> p i s c", i=2, c=8)[:, :, 0:8, :]
                    copy_op(K, "act", dst, src, [R("ps", bank)], [R("P")])
                if pk == "A":
                    S.op("pool", lambda e: e.memset(P[:, :, 0:1], 0.0), writes=[R("P")])
                elif pk == "B":
                    S.op("pool", lambda e, g0=g0: e.tensor_copy(out=P[:, :, 0], in_=CAR[:, g0:g0 + 4]), reads=[R("CAR")], writes=[R("P")])
                else:
                    S.op("pool", lambda e, g0=g0: e.tensor_copy(out=P[:, :, 0:72].rearrange("p i (s c) -> p i s c", c=9)[:, :, :, 0],
                                                               in_=H0[:, :, g0:g0 + 4].rearrange("p s i -> p i s")), reads=[R("H0")], writes=[R("P")])
                k = 0
                while (1 << k) < ncol:
                    sh = 1 << k
                    for i in range(4):
                        mb, mf = mats(i)
                        bank = 3 + i // 2
                        off = (i % 2) * 256
                        if pk != "C":
                            rhs = P[:, i, 0:ncol - sh]
                            out = K.ps[bank][:, off:off + ncol - sh]
                        else:
                            rhs = P[:, i, 0:72].rearrange("p (s c) -> p s c", c=9)[:, :, 0:9 - sh]
                            out = K.ps[bank][:, off:off + 8 * (9 - sh)].rearrange("p (s c) -> p s c", s=8)
                        S.op("pe", lambda e, mf=mf, k=k, rhs=rhs, out=out: e.matmul(out, lhsT=mf[:, k, :], rhs=rhs, start=True, stop=True),
                             reads=mats_res[i] + [R("P")], writes=[R("ps", bank)], tick=(i % 2 == 1))
                    for b in range(2):
                        bank = 3 + b
                        if pk != "C":
                            dst = P[:, 2 * b:2 * b + 2, sh:ncol]
                            src = K.ps[bank][:, :].rearrange("p (i c) -> p i c", i=2)[:, :, 0:ncol - sh]
                        else:
                            dst = P[:, 2 * b:2 * b + 2, 0:72].rearrange("p i (s c) -> p i s c", c=9)[:, :, :, sh:9]
                            src = K.ps[bank][:, :].rearrange("p (i x) -> p i x", i=2)[:, :, 0:8 * (9 - sh)].rearrange("p i (s c) -> p i s c", s=8)
                        S.op("dve", lambda e, dst=dst, src=src: e.tensor_tensor(out=dst, in0=dst, in1=src, op=ALU.add),
                             reads=[R("P"), R("ps", bank)], writes=[R("P")])
                    k += 1
                if pk != "C":
                    copy_op(K, "act", Sb[:, :, 0:nch], P[:, :, 0:nch], [R("P")], [R("Sb")])
                    dstc = (CAR if pk == "A" else FINp)[:, g0:g0 + 4]
                    copy_op(K, "pool", dstc, P[:, :, nch], [R("P")], [R("CAR")] if pk == "A" else [R("FINp")])
                else:
                    pv = P[:, :, 0:72].rearrange("p i (s c) -> p i s c", c=9)
                    copy_op(K, "act", Sb[:, :, 0:64].rearrange("p i (s c) -> p i s c", c=8), pv[:, :, :, 0:8], [R("P")], [R("Sb")])
                    copy_op(K, "pool", FINs[:, :, g0:g0 + 4].rearrange("p s i -> p i s"), pv[:, :, :, 8], [R("P")], [R("FINs")])
                for i in range(4):
                    g = g0 + i
                    g8 = g % 8
                    mb, mf = mats(i)
                    hb = Hb[i % 2]
                    hres = [R("Hb", i % 2)]
                    hv = hb[:, 0:W].rearrange("p (c j) -> p j c", j=8)
                    for j in range(8):
                        bank = j // 3
                        off = (j % 3) * 128
                        for s in range(j + 1):
                            u_ap = uT(s)
                            o_ap = K.ps[bank][:, off:off + nch]
                            S.op("pe", lambda e, mb=mb, j=j, s=s, u_ap=u_ap, o_ap=o_ap: e.matmul(o_ap, lhsT=mb[:, 8 + j - s, :], rhs=u_ap,
                                                                                                           start=(s == 0), stop=False),
                                 reads=mats_res[i] + ures, writes=[R("ps", bank)], tick=False)
                        o_ap = K.ps[bank][:, off:off + nch]
                        sb_ap = Sb[:, i, 0:nch]
                        S.op("pe", lambda e, mb=mb, j=j, o_ap=o_ap, sb_ap=sb_ap: e.matmul(o_ap, lhsT=mb[:, j, :], rhs=sb_ap,
                                                                                           start=False, stop=True),
                             reads=mats_res[i] + [R("Sb")], writes=[R("ps", bank)], tick=(j % 3 == 2 or j == 7))
                        if j % 3 == 2 or j == 7:
                            j0 = (j // 3) * 3
                            nj = j - j0 + 1
                            src = K.ps[bank][:, 0:384].rearrange("p (j c) -> p j c", j=3)[:, 0:nj, 0:nch]
                            copy_op(K, evac_engine(K), hv[:, j0:j0 + nj, :], src, [R("ps", bank)], hres)
                    for tb in range(W // 512):
                        S.op("pe", lambda e, g8=g8, hb=hb, tb=tb: e.matmul(K.ps[ybanks[tb]][:, :], lhsT=Ct[:, g8, :], rhs=hb[:, tb * 512:(tb + 1) * 512],
                                                                          start=(g8 == 0), stop=(g8 == 7)),
                             reads=[R("Ct")] + hres, writes=[R("ps", ybanks[tb])], tick=True)
            for tb in range(W // 512):
                bank = ybanks[tb]
                ua = K.xT[:, fc, t0 + tb * 512:t0 + (tb + 1) * 512]
                y1, t2_ = K.lnT[0][:, :], K.lnT[1][:, :]
                S.op("dve", lambda e, ua=ua, bank=bank, fc=fc: e.scalar_tensor_tensor(out=y1, in0=ua, scalar=dcol[:, fc:fc + 1], in1=K.ps[bank][:, :],
                                                                                      op0=ALU.mult, op1=ALU.add),
                     reads=ures + [R("dcol"), R("ps", bank)], writes=[R("lnT", 0)])
                S.op("act", lambda e: e.activation(out=t2_, in_=y1, func=AF.Square), reads=[R("lnT", 0)], writes=[R("lnT", 1)])
                S.op("dve", lambda e: e.tensor_scalar(out=t2_, in0=t2_, scalar1=0.044715, scalar2=1.0, op0=ALU.mult, op1=ALU.add),
                     reads=[R("lnT", 1)], writes=[R("lnT", 1)])
                tt(K, "dve", t2_, t2_, y1, ALU.mult, [R("lnT", 0), R("lnT", 1)], [R("lnT", 1)])
                S.op("act", lambda e: e.activation(out=t2_, in_=t2_, func=AF.Sigmoid, scale=1.5957691216057308), reads=[R("lnT", 1)], writes=[R("lnT", 1)])
                yo = yT[:, fc, tb * 512:(tb + 1) * 512]
                tt(K, "dve", yo, y1, t2_, ALU.mult, [R("lnT", 0), R("lnT", 1)], [R("yT", fc)])
        for tb in range(W // 512):
            tg0 = t0 + tb * 512
            for ob in range(D // 256):
                def rhs(kc, tb=tb):
                    return yT[:, kc, tb * 512:(tb + 1) * 512]
                dense_block(K, K.s5_wo[ia], ob, NCH, rhs, lambda kc: [R("yT", kc)], 512, [0, 1])
                dense_block(K, K.s5_wg[ia], ob, NCH, rhs, lambda kc: [R("yT", kc)], 512, [2, 3])
                for o in range(2):
                    sg = K.lnT[o][:, :]
                    S.op("act", lambda e, sg=sg, o=o: e.activation(out=sg, in_=K.ps[2 + o][:, :], func=AF.Sigmoid), reads=[R("ps", 2 + o)], writes=[R("lnT", o)])
                    tt(K, "dve", sg, sg, K.ps[o][:, :], ALU.mult, [R("lnT", o), R("ps", o)], [R("lnT", o)])
                    resid_ln_accum(K, tg0, 512, ob * 2 + o, sg, [R("lnT", o)])
            ln_finish(K, layer, 0, tg0, 512, False)
    def out_state(src_ap, src_res, dst):
        bank = 4
        S.op("pe", lambda e: e.transpose(out=K.ps[bank][:, 0:128], in_=src_ap, identity=K.ident_f[:, :]),
             reads=src_res + [R("ident_f")], writes=[R("ps", bank)])
        ot = K.otile[0]
        copy_op(K, "dve", ot[:, 0:128], K.ps[bank][:, 0:128], [R("ps", bank)], [R("otile", 0)])
        S.op("pool", lambda e: e.dma_start(out=dst, in_=ot[:, 0:128]), reads=[R("otile", 0)], writes=[R("sout")], dma=True)
    if any(p[2] == "B" for p in passes):
        out_state(FINp[:, :], [R("FINp")], K.ssm_p[ia])
    if any(p[2] == "C" for p in passes):
        for s in range(NS):
            out_state(FINs[:, s, :], [R("FINs")], K.ssm_s[ia, s])
    S.barrier()


AG = 256


def dense_block_tm(K, w2d, ob, t0, banks):
    S = K.S
    for kt in range(2):
        wi = load_w_tile(K, wtile(w2d, kt, ob))
        wbf = K.wbf[wi]
        for t in range(2):
            for k in range(8):
                kc = kt * 8 + k
                first = (kt == 0 and k == 0)
                last = (kt == 1 and k == 7)
                lhs = K.xT[:, kc, t0 + t * 128:t0 + (t + 1) * 128]
                o_ap = K.ps[banks[t]][:, 0:256]
                S.op("pe", lambda e, lhs=lhs, wbf=wbf, k=k, o_ap=o_ap, first=first, last=last:
                     e.matmul(o_ap, lhsT=lhs, rhs=wbf[:, k * 256:(k + 1) * 256], start=first, stop=last),
                     reads=[R("wbf", wi)] + xr(t0 + t * 128, 128, [kc]), writes=[R("ps", banks[t])], tick=(last or (t == 1 and k == 7)))


def attn_layer(K, layer):
    nc, S = K.nc, K.S
    S.barrier()
    bg = Arena(K, K.big_off, 65536)
    KT = bg.get("KT", [128, 16, 768], BF16)
    V = bg.get("V", [128, 6, 2048], BF16)
    oT = bg.get("oT", [128, 16, AG], BF16)
    toep = bg.get("toep", [128, 16, 2, 128], BF16)
    sc = Arena(K, K.scr_off, K.scr_size)
    qT = sc.get("qT", [128, 16, AG], BF16)
    PT = [sc.get("PT%d" % i, [128, 640], BF16) for i in range(2)]
    Ssb = sc.get("Ssb", [128, 256], F32)
    otok = [sc.get("otok%d" % i, [128, 128], BF16) for i in range(2)]
    farcol = sc.get("farcol", [128, 16], F32)
    rc = [sc.get("rc%d" % i, [128, 1], F32) for i in range(2)]
    wqkv = K.at_wqkv
    scale = float(128 ** -0.5)

    si = next_stg(K)
    stg = K.stg[si]
    for half in range(2):
        S.op("sp", lambda e, half=half, stg=stg: e.dma_start(out=stg[:, :].rearrange("p (h r q) -> p h r q", h=8, r=2), in_=K.at_toep[:, half * 8:(half + 1) * 8]),
             writes=[R("stg", si)], dma=True)
        S.op("pool", lambda e, half=half, stg=stg: e.tensor_copy(out=toep[:, half * 8:(half + 1) * 8, :, :], in_=stg[:, :].rearrange("p (h r q) -> p h r q", h=8, r=2)),
             reads=[R("stg", si)], writes=[R("toep")])
    S.op("sp", lambda e: e.dma_start(out=farcol[:, :], in_=K.at_far), writes=[R("farcol")], dma=True)
    cnt = [0]

    def attend(h, nq, q_ap, tiles, sample_i, o_dst, o_dst_res):
        u = cnt[0] % 2
        cnt[0] += 1
        pt = PT[u]
        ptr = [R("PT", u)]
        far = [t for t in tiles if t[5] == "far"]
        near = [t for t in tiles if t[5] != "far"]
        for (r, kt_ap, kt_res, v_ap, v_res, kind) in tiles:
            bank = 4 if kind == "far" else 5
            col = (r if kind == "far" else r - 3) * nq
            o_ap = K.ps[bank][:, col:col + nq]
            S.op("pe", lambda e, o_ap=o_ap, kt_ap=kt_ap: e.matmul(o_ap, lhsT=kt_ap, rhs=q_ap, start=True, stop=True),
                 reads=kt_res + [R("qT")], writes=[R("ps", bank)], tick=True)
        if far:
            r0 = far[0][0]
            src = K.ps[4][:, r0 * nq:3 * nq]
            dst = pt[:, r0 * nq:3 * nq]
            S.op("act", lambda e: e.activation(out=dst, in_=src, func=AF.Exp, bias=farcol[:, h:h + 1], scale=1.0),
                 reads=[R("ps", 4), R("farcol")], writes=ptr)
        r0n = near[0][0]
        for (r, kt_ap, kt_res, v_ap, v_res, kind) in near:
            col = (r - 3) * nq
            if sample_i is None:
                bias_ap = toep[:, h, r - 3, :]
            elif r == 3:
                bias_ap = toep[:, h, 0, 0:64]
            else:
                bias_ap = toep[:, h, 1, (sample_i % 2) * 64:(sample_i % 2) * 64 + 64]
            s_ap = Ssb[:, col:col + nq]
            p_ap = K.ps[5][:, col:col + nq]
            S.op("dve", lambda e, s_ap=s_ap, p_ap=p_ap, bias_ap=bias_ap: e.tensor_tensor(out=s_ap, in0=p_ap, in1=bias_ap, op=ALU.add),
                 reads=[R("ps", 5), R("toep")], writes=[R("Ssb")])
        srcn = Ssb[:, (r0n - 3) * nq:2 * nq]
        dstn = pt[:, r0n * nq:5 * nq]
        S.op("act", lambda e: e.activation(out=dstn, in_=srcn, func=AF.Exp), reads=[R("Ssb")], writes=ptr)
        if sample_i is None:
            if far and far[0][0] == 0:
                S.op("pool", lambda e: e.memset(pt[0:64, 64:128], 0.0), writes=ptr)
            S.op("pool", lambda e: e.memset(pt[64:128, 4 * 128:4 * 128 + 64], 0.0), writes=ptr)
        else:
            oh = (1 - sample_i % 2) * 64
            S.op("pool", lambda e: e.memset(pt[oh:oh + 64, 4 * nq:5 * nq], 0.0), writes=ptr)
        ob_ = K.ps[6]
        for idx, (r, kt_ap, kt_res, v_ap, v_res, kind) in enumerate(tiles):
            l_ap = pt[:, r * nq:(r + 1) * nq]
            S.op("pe", lambda e, l_ap=l_ap, v_ap=v_ap, idx=idx: e.matmul(ob_[0:nq, 0:128], lhsT=l_ap, rhs=v_ap, start=(idx == 0), stop=(idx == len(tiles) - 1)),
                 reads=ptr + v_res, writes=[R("ps", 6)], tick=False)
        for idx, (r, kt_ap, kt_res, v_ap, v_res, kind) in enumerate(tiles):
            l_ap = pt[:, r * nq:(r + 1) * nq]
            S.op("pe", lambda e, l_ap=l_ap, idx=idx: e.matmul(ob_[0:nq, 128:129], lhsT=l_ap, rhs=K.ones_b[:, 0:1], start=(idx == 0), stop=(idx == len(tiles) - 1)),
                 reads=ptr + [R("ones_b")], writes=[R("ps", 6)], tick=(idx == len(tiles) - 1))
        rcu = rc[u]
        S.op("dve", lambda e: e.reciprocal(out=rcu[0:nq, :], in_=ob_[0:nq, 128:129]), reads=[R("ps", 6)], writes=[R("rc", u)])
        ot = otok[u]
        S.op("act", lambda e: e.activation(out=ot[0:nq, :], in_=ob_[0:nq, 0:128], func=AF.Copy, scale=rcu[0:nq, 0:1]),
             reads=[R("ps", 6), R("rc", u)], writes=[R("otok", u)])
        tp = K.ps[7][:, :].bitcast(BF16)
        S.op("pe", lambda e: e.transpose(out=tp[:, 0:nq], in_=ot[0:nq, :], identity=K.ident_b[0:nq, 0:nq]),
             reads=[R("otok", u), R("ident_b")], writes=[R("ps", 7)])
        copy_op(K, "dve", o_dst, tp[:, 0:nq], [R("ps", 7)], o_dst_res)

    nag = T // AG
    ags = K.cfg.get("attn_ags", list(range(nag)))
    for a in ags:
        t0 = a * AG
        is_s = a >= SEQ // AG
        want_out = (is_s or (t0 >= SEQ - 512)) and not K.cfg.get("no_out")
        if not is_s:
            kcol = (a % 3) * 256
        else:
            kcol = 512
        for ob in range(8):
            banks = [2 * (ob % 2), 2 * (ob % 2) + 1]
            dense_block(K, wqkv, 8 + ob, NCH, lambda kc: K.xT[:, kc, t0:t0 + AG], lambda kc: xr(t0, AG, [kc]), AG, banks)
            for o in range(2):
                hh = ob * 2 + o
                copy_op(K, evac_engine(K), KT[:, hh, kcol:kcol + AG], K.ps[banks[o]][:, 0:AG], [R("ps", banks[o])], [R("KT", kcol // 256)])
        for which in (K.cfg.get("whichs", [1, 2]) if want_out else [2]):
            for ob in range(8):
                banks = [2 * (ob % 2), 2 * (ob % 2) + 1]
                dense_block_tm(K, wqkv, which * 8 + ob, t0, banks)
                for t in range(2):
                    if not is_s:
                        slot = (2 * a + t) % 6
                    else:
                        slot = 4 + t
                    src = K.ps[banks[t]][:, 0:256]
                    if not want_out:
                        copy_op(K, "act", V[:, slot, ob * 256:(ob + 1) * 256], src, [R("ps", banks[t])], [R("V", slot)])
                    else:
                        ot = K.otile[t]
                        copy_op(K, "dve", ot[:, 0:256], src, [R("ps", banks[t])], [R("otile", t)])
                        if which == 2:
                            copy_op(K, "act", V[:, slot, ob * 256:(ob + 1) * 256], ot[:, 0:256], [R("otile", t)], [R("V", slot)])
                        if is_s:
                            dr = (a - SEQ // AG) * AG + t * 128
                            dst = (K.at_ks if which == 1 else K.at_vs)[dr:dr + 128, ob * 256:(ob + 1) * 256]
                        else:
                            dr = t0 - (SEQ - 512) + t * 128
                            dst = (K.at_kp if which == 1 else K.at_vp)[dr:dr + 128, ob * 256:(ob + 1) * 256]
                        if not K.cfg.get("no_dma"):
                            S.op("sp", lambda e, dst=dst, ot=ot: e.dma_start(out=dst, in_=ot[:, 0:256]), reads=[R("otile", t)], writes=[R("kvout")], dma=True)
        for ob in range(8):
            banks = [2 * (ob % 2), 2 * (ob % 2) + 1]
            dense_block(K, wqkv, ob, NCH, lambda kc: K.xT[:, kc, t0:t0 + AG], lambda kc: xr(t0, AG, [kc]), AG, banks)
            for o in range(2):
                hh = ob * 2 + o
                src = K.ps[banks[o]][:, 0:AG]
                dst = qT[:, hh, :]
                S.op("act", lambda e, src=src, dst=dst: e.activation(out=dst, in_=src, func=AF.Copy, scale=scale), reads=[R("ps", banks[o])], writes=[R("qT")])
        stage = K.cfg.get("attn_stage", 3)
        if stage < 2:
            continue
        if not is_s:
            for h in range(16):
                for qi in range(2):
                    qt = 2 * a + qi
                    tiles = []
                    for r in range(5):
                        j = qt - 4 + r
                        if j < 0:
                            continue
                        blk = (j // 2) % 3
                        col = blk * 256 + (j % 2) * 128
                        tiles.append((r, KT[:, h, col:col + 128], [R("KT", blk)], V[:, j % 6, h * 128:(h + 1) * 128], [R("V", j % 6)],
                                      "far" if r < 3 else "near"))
                    attend(h, 128, qT[:, h, qi * 128:(qi + 1) * 128], tiles, None, oT[:, h, qi * 128:(qi + 1) * 128], [R("oT", h)])
        else:
            sa = a - SEQ // AG
            for i in range(4):
                sq = sa * 4 + i
                for jt in range(4):
                    si = next_stg(K)
                    stg = K.stg[si]
                    S.op("sp", lambda e, stg=stg, sq=sq, jt=jt: e.dma_start(out=stg[:, :], in_=K.at_ck[sq, jt * 128:(jt + 1) * 128, :]), writes=[R("stg", si)], dma=True)
                    for cb in range(4):
                        bank = cb % 4
                        for jj in range(4):
                            c = cb * 4 + jj
                            o_ap = K.ps[bank][:, jj * 128:(jj + 1) * 128]
                            S.op("pe", lambda e, stg=stg, c=c, o_ap=o_ap: e.transpose(out=o_ap, in_=stg[:, c * 128:(c + 1) * 128], identity=K.ident_f[:, :]),
                                 reads=[R("stg", si), R("ident_f")], writes=[R("ps", bank)], tick=(jj == 3))
                        copy_op(K, evac_engine(K), KT[:, cb * 4:(cb + 1) * 4, jt * 128:(jt + 1) * 128],
                                K.ps[bank][:, :].rearrange("p (j t) -> p j t", j=4), [R("ps", bank)], [R("KT", jt // 2)])
                    si = next_stg(K)
                    stg = K.stg[si]
                    S.op("sp", lambda e, stg=stg, sq=sq, jt=jt: e.dma_start(out=stg[:, :], in_=K.at_cv[sq, jt * 128:(jt + 1) * 128, :]), writes=[R("stg", si)], dma=True)
                    copy_op(K, "act" if jt % 2 else "dve", V[:, jt, :], stg[:, :], [R("stg", si)], [R("V", jt)])
                for h in range(16):
                    tiles = []
                    for r in range(4):
                        tiles.append((r, KT[:, h, r * 128:(r + 1) * 128], [R("KT", r // 2)], V[:, r, h * 128:(h + 1) * 128], [R("V", r)],
                                      "far" if r < 3 else "near"))
                    pc = 512 + (i // 2) * 128
                    tiles.append((4, KT[:, h, pc:pc + 128], [R("KT", 2)], V[:, 4 + i // 2, h * 128:(h + 1) * 128], [R("V", 4 + i // 2)], "near"))
                    attend(h, 64, qT[:, h, i * 64:(i + 1) * 64], tiles, i, oT[:, h, i * 64:(i + 1) * 64], [R("oT", h)])
        if stage < 3:
            continue
        for ob in range(D // 256):
            banks = [2 * (ob % 2), 2 * (ob % 2) + 1]
            dense_block(K, K.at_wo, ob, NCH, lambda kc: oT[:, kc, :], lambda kc: [R("oT", kc)], AG, banks)
            for o in range(2):
                resid_ln_accum(K, t0, AG, ob * 2 + o, K.ps[banks[o]][:, 0:AG], [R("ps", banks[o])])
        ln_finish(K, layer, 0, t0, AG, False)
    S.barrier()


HG = 256
RMS_EPS = 1e-6


def dense_block_tm64(K, w2d, ob, t0, banks):
    S = K.S
    wis = [load_w_tile(K, wtile(w2d, kt, ob)) for kt in range(2)]
    for c in range(4):
        o_ap = K.ps[banks[c // 2]][0:64, (c % 2) * 256:(c % 2) * 256 + 256]
        for kt in range(2):
            wbf = K.wbf[wis[kt]]
            for k in range(8):
                kc = kt * 8 + k
                first = (kt == 0 and k == 0)
                last = (kt == 1 and k == 7)
                lhs = K.xT[:, kc, t0 + c * 64:t0 + (c + 1) * 64]
                S.op("pe", lambda e, lhs=lhs, wbf=wbf, k=k, o_ap=o_ap, first=first, last=last:
                     e.matmul(o_ap, lhsT=lhs, rhs=wbf[:, k * 256:(k + 1) * 256], start=first, stop=last),
                     reads=[R("wbf", wis[kt])] + xr(t0 + c * 64, 64, [kc]), writes=[R("ps", banks[c // 2])], tick=(k == 7))


def hgrn_layer(K, layer):
    nc, S = K.nc, K.S
    S.barrier()
    bg = Arena(K, K.big_off, 65536)
    qhT = bg.get("qhT", [128, 16, HG], BF16)
    ktT = bg.get("ktT", [128, 16, HG], BF16)
    mT = bg.get("mT", [128, 16, HG], BF16)
    GsT = bg.get("GsT", [128, 16, HG], BF16)
    Vc = bg.get("Vc", [64, 4, 2048], BF16)
    Kc2 = [bg.get("Kc%d" % i, [64, 2048], BF16) for i in range(2)]
    Sbf = bg.get("Sbf", [128, 16, 128], BF16)
    ATm = [bg.get("ATm%d" % i, [64, 4, 64], BF16) for i in range(2)]
    omb = [bg.get("omb%d" % i, [64, 4, 128], BF16) for i in range(2)]
    sc = Arena(K, K.scr_off, K.scr_size)
    Sm = sc.get("Sm", [128, 16, 128], F32)
    ft = [K.lnT[0][:, 0:256], K.lnT[0][:, 256:512], K.lnT[1][:, 0:256], K.lnT[1][:, 256:512]]
    ebuf = [K.otile[0][:, 0:256], K.otile[1][:, 0:256]]
    sq = K.lnA[0:64, :]
    eL = sc.get("eL", [128, 16, 4], F32)
    ss = sc.get("ss", [64, 16], F32)
    ones64 = sc.get("ones64", [128, 64], F32)
    M64 = sc.get("M64", [64, 4, 64], F32)
    lbt = sc.get("lbt", [128, 16], F32)
    oml = sc.get("oml", [128, 16], F32)
    ngt = sc.get("ngt", [128, 1], F32)
    lg = sc.get("lg", [128, 4, 16], F32)
    den = sc.get("den", [128, 16], F32)
    epsr = sc.get("epsr", [64, 1], F32)
    w_in = K.hg_win
    C = [R("hgc")]

    S.op("pool", lambda e: e.memset(ones64[:, :], 1.0), writes=C)
    S.op("pool", lambda e: e.memset(epsr[:, :], RMS_EPS), writes=C)
    S.op("pool", lambda e: e.memset(M64[:, :, :], 1.0), writes=C)
    S.op("pool", lambda e: e.affine_select(out=M64[:, :, :], in_=M64[:, :, :], pattern=[[0, 4], [1, 64]], compare_op=ALU.is_ge,
                                            fill=0.0, base=0, channel_multiplier=-1), reads=C, writes=C)
    S.op("sp", lambda e: e.dma_start(out=ngt[:, :], in_=K.hg_ng), writes=C, dma=True)
    S.op("sp", lambda e: e.dma_start(out=lg[:, :, :], in_=K.hg_lb), writes=C, dma=True)
    S.op("act", lambda e: e.activation(out=lg[:, :, :], in_=lg[:, :, :], func=AF.Exp), reads=C, writes=C)
    tt(K, "dve", den[:, :], lg[:, 0, :], lg[:, 1, :], ALU.add, C, C)
    tt(K, "dve", den[:, :], den[:, :], lg[:, 2, :], ALU.add, C, C)
    tt(K, "dve", den[:, :], den[:, :], lg[:, 3, :], ALU.add, C, C)
    S.op("dve", lambda e: e.reciprocal(out=den[:, :], in_=den[:, :]), reads=C, writes=C)
    S.op("pool", lambda e: e.memset(lbt[:, :], 0.0), writes=C)
    for l in range(1, layer + 1):
        tt(K, "dve", lbt[:, :], lbt[:, :], lg[:, l, :], ALU.add, C, C)
    tt(K, "dve", lbt[:, :], lbt[:, :], den[:, :], ALU.mult, C, C)
    S.op("dve", lambda e: e.tensor_scalar(out=oml[:, :], in0=lbt[:, :], scalar1=-1.0, scalar2=1.0, op0=ALU.mult, op1=ALU.add), reads=C, writes=C)
    S.op("pool", lambda e: e.memset(Sm[:, :, :], 0.0), writes=[R("Sm", h) for h in range(16)])
    S.op("pool", lambda e: e.memset(Sbf[:, :, :], 0.0), writes=[R("Sbf", h) for h in range(16)])

    ntg = T // HG
    tgs = K.cfg.get("hgrn_tgs", list(range(ntg)))
    ucnt = [0]
    for a in tgs:
        t0 = a * HG
        is_s = a >= SEQ // HG
        for ob in range(8):
            banks = [2 * (ob % 2), 2 * (ob % 2) + 1]
            dense_block_tm64(K, w_in, 2 * 8 + ob, t0, banks)
            for b in range(2):
                src = K.ps[banks[b]][0:64, :].rearrange("p (c n) -> p c n", c=2)
                dst = Vc[:, 2 * b:2 * b + 2, ob * 256:(ob + 1) * 256]
                copy_op(K, "act", dst, src, [R("ps", banks[b])], [R("Vc")])
        for hp in range(8):
            banks = [2 * (hp % 2), 2 * (hp % 2) + 1]
            dense_block(K, w_in, 24 + hp, NCH, lambda kc: K.xT[:, kc, t0:t0 + HG], lambda kc: xr(t0, HG, [kc]), HG, banks)
            for o in range(2):
                h = hp * 2 + o
                tmp = K.lnT[o][:, 0:HG]
                psg = K.ps[banks[o]][:, 0:HG]
                S.op("act", lambda e, tmp=tmp, psg=psg: e.activation(out=tmp, in_=psg, func=AF.Sigmoid), reads=[R("ps", banks[o])], writes=[R("lnT", o)])
                tt(K, "dve", GsT[:, h, :], tmp, psg, ALU.mult, [R("lnT", o), R("ps", banks[o])], [R("GsT", h)])
        for hp in range(8):
            dense_block(K, w_in, 8 + hp, NCH, lambda kc: K.xT[:, kc, t0:t0 + HG], lambda kc: xr(t0, HG, [kc]), HG, [0, 1])
            for o in range(2):
                h = hp * 2 + o
                psf = K.ps[o][:, 0:HG]
                f_, omf, b_, en = ft[0], ft[1], ft[2], ft[3]
                eb = ebuf[o]
                FT = [R("lnT", 0), R("lnT", 1)]
                S.op("act", lambda e, psf=psf: e.activation(out=f_, in_=psf, func=AF.Sigmoid), reads=[R("ps", o)], writes=FT)
                S.op("dve", lambda e, h=h: e.tensor_scalar(out=f_, in0=f_, scalar1=oml[:, h:h + 1], scalar2=lbt[:, h:h + 1], op0=ALU.mult, op1=ALU.add),
                     reads=FT + C, writes=FT)
                S.op("dve", lambda e: e.tensor_scalar(out=omf, in0=f_, scalar1=-1.0, scalar2=1.0, op0=ALU.mult, op1=ALU.add), reads=FT, writes=FT)
                S.op("act", lambda e: e.activation(out=f_, in_=f_, func=AF.Ln), reads=FT, writes=FT)
                for c in range(4):
                    S.op("dve", lambda e, c=c: e.tensor_tensor_scan(out=b_[:, c * 64:(c + 1) * 64], data0=ones64[:, :], data1=f_[:, c * 64:(c + 1) * 64],
                                                                    initial=0.0, op0=ALU.mult, op1=ALU.add), reads=FT + C, writes=FT)
                S.op("act", lambda e, eb=eb: e.activation(out=eb, in_=b_, func=AF.Exp), reads=FT, writes=[R("otile", o)])
                S.op("act", lambda e: e.activation(out=en, in_=b_, func=AF.Exp, scale=-1.0), reads=FT, writes=FT)
                tt(K, "dve", ktT[:, h, :], omf, en, ALU.mult, FT, [R("ktT", h)])
                copy_op(K, "pool", eL[:, h, :], ebuf[o][:, 63:HG:64], [R("otile", o)], [R("eL")])
            dense_block(K, w_in, hp, NCH, lambda kc: K.xT[:, kc, t0:t0 + HG], lambda kc: xr(t0, HG, [kc]), HG, [2, 3])
            for o in range(2):
                h = hp * 2 + o
                tt(K, "dve", qhT[:, h, :], K.ps[2 + o][:, 0:HG], ebuf[o], ALU.mult, [R("ps", 2 + o), R("otile", o)], [R("qhT", h)])
        for c in range(4):
            if is_s:
                sq_i = (a - SEQ // HG) * 4 + c
                S.op("sp", lambda e, sq_i=sq_i: e.dma_start(out=Sm[:, :, :], in_=K.hg_s0[sq_i].rearrange("h k v -> k h v")),
                     writes=[R("Sm", h) for h in range(16)], dma=True)
                copy_op(K, "act", Sbf[:, :, :], Sm[:, :, :], [R("Sm", h) for h in range(16)], [R("Sbf", h) for h in range(16)])
            csl = slice(c * 64, (c + 1) * 64)
            Kc = Kc2[c % 2]
            KR = [R("Kc", c % 2)]
            for hg in range(4):
                tp = K.ps[7][:, :].bitcast(BF16)
                for hh in range(4):
                    h = hg * 4 + hh
                    i_ap = ktT[:, h, c * 64:(c + 1) * 64]
                    o_ap = tp[0:64, hh * 128:(hh + 1) * 128]
                    S.op("pe", lambda e, i_ap=i_ap, o_ap=o_ap: e.transpose(out=o_ap, in_=i_ap, identity=K.ident_b[:, :]),
                         reads=[R("ktT", h), R("ident_b")], writes=[R("ps", 7)], tick=(hh == 3))
                copy_op(K, evac_engine(K), Kc[:, hg * 512:(hg + 1) * 512], tp[0:64, 0:512], [R("ps", 7)], KR)
            for hg in range(4):
                u = ucnt[0] % 2
                ucnt[0] += 1
                at, ob_ = ATm[u], omb[u]
                for hh in range(4):
                    h = hg * 4 + hh
                    o_ap = K.ps[4][0:64, hh * 64:(hh + 1) * 64]
                    l_ap, r_ap = ktT[:, h, csl], qhT[:, h, csl]
                    S.op("pe", lambda e, o_ap=o_ap, l_ap=l_ap, r_ap=r_ap: e.matmul(o_ap, lhsT=l_ap, rhs=r_ap, start=True, stop=True),
                         reads=[R("ktT", h), R("qhT", h)], writes=[R("ps", 4)], tick=(hh == 3))
                tt(K, "dve", at[:, :, :], K.ps[4][0:64, 0:256].rearrange("p (h t) -> p h t", h=4), M64[:, :, :], ALU.mult, [R("ps", 4)] + C, [R("ATm", u)])
                for hh in range(4):
                    h = hg * 4 + hh
                    o_ap = K.ps[5][0:64, hh * 128:(hh + 1) * 128]
                    v_ap = Vc[:, c, h * 128:(h + 1) * 128]
                    a_ap = at[:, hh, :]
                    q_ap = qhT[:, h, csl]
                    s_ap = Sbf[:, h, :]
                    S.op("pe", lambda e, o_ap=o_ap, a_ap=a_ap, v_ap=v_ap: e.matmul(o_ap, lhsT=a_ap, rhs=v_ap, start=True, stop=False),
                         reads=[R("ATm", u), R("Vc")], writes=[R("ps", 5)], tick=False)
                    S.op("pe", lambda e, o_ap=o_ap, q_ap=q_ap, s_ap=s_ap: e.matmul(o_ap, lhsT=q_ap, rhs=s_ap, start=False, stop=True),
                         reads=[R("qhT", h), R("Sbf", h)], writes=[R("ps", 5)], tick=(hh == 3))
                for hh in range(4):
                    h = hg * 4 + hh
                    o_ap = K.ps[6][:, hh * 128:(hh + 1) * 128]
                    k_ap = Kc[:, h * 128:(h + 1) * 128]
                    v_ap = Vc[:, c, h * 128:(h + 1) * 128]
                    S.op("pe", lambda e, o_ap=o_ap, k_ap=k_ap, v_ap=v_ap: e.matmul(o_ap, lhsT=k_ap, rhs=v_ap, start=True, stop=True),
                         reads=KR + [R("Vc")], writes=[R("ps", 6)], tick=(hh == 3))
                for hh in range(4):
                    h = hg * 4 + hh
                    sm = Sm[:, h, :]
                    e_ap = eL[:, h, c:c + 1]
                    S.op("dve", lambda e, sm=sm, e_ap=e_ap: e.tensor_scalar(out=sm, in0=sm, scalar1=e_ap, scalar2=None, op0=ALU.mult),
                         reads=[R("Sm", h), R("eL")], writes=[R("Sm", h)])
                    p_ap = K.ps[6][:, hh * 128:(hh + 1) * 128]
                    S.op("dve", lambda e, sm=sm, e_ap=e_ap, p_ap=p_ap: e.scalar_tensor_tensor(out=sm, in0=p_ap, scalar=e_ap, in1=sm, op0=ALU.mult, op1=ALU.add),
                         reads=[R("Sm", h), R("eL"), R("ps", 6)], writes=[R("Sm", h)])
                    copy_op(K, "act", Sbf[:, h, :], sm, [R("Sm", h)], [R("Sbf", h)])
                S.op("act", lambda e: e.activation(out=sq[:, :], in_=K.ps[5][0:64, :], func=AF.Square), reads=[R("ps", 5)], writes=[R("lnA")])
                ssg = ss[:, hg * 4:(hg + 1) * 4]
                S.op("dve", lambda e, ssg=ssg: e.tensor_reduce(out=ssg, in_=sq[:, :].rearrange("p (h v) -> p h v", h=4), axis=AX.X, op=ALU.add),
                     reads=[R("lnA")], writes=[R("ss")])
                S.op("act", lambda e, ssg=ssg: e.activation(out=ssg, in_=ssg, func=AF.Sqrt, bias=epsr[:, 0:1], scale=1.0 / 128), reads=[R("ss")] + C, writes=[R("ss")])
                S.op("dve", lambda e, ssg=ssg: e.reciprocal(out=ssg, in_=ssg), reads=[R("ss")], writes=[R("ss")])
                for hh in range(4):
                    h = hg * 4 + hh
                    p_ap = K.ps[5][0:64, hh * 128:(hh + 1) * 128]
                    r_ap = ss[:, h:h + 1]
                    d_ap = ob_[:, hh, :]
                    S.op("act", lambda e, p_ap=p_ap, r_ap=r_ap, d_ap=d_ap: e.activation(out=d_ap, in_=p_ap, func=AF.Copy, scale=r_ap),
                         reads=[R("ps", 5), R("ss")], writes=[R("omb", u)])
                tp = K.ps[7][:, :].bitcast(BF16)
                for hh in range(4):
                    i_ap = ob_[:, hh, :]
                    o_ap = tp[:, hh * 64:(hh + 1) * 64]
                    S.op("pe", lambda e, i_ap=i_ap, o_ap=o_ap: e.transpose(out=o_ap, in_=i_ap, identity=K.ident_b[0:64, 0:64]),
                         reads=[R("omb", u), R("ident_b")], writes=[R("ps", 7)], tick=(hh == 3))
                m_dst = mT[:, hg * 4:(hg + 1) * 4, csl]
                g_src = GsT[:, hg * 4:(hg + 1) * 4, csl]
                t_src = tp[:, 0:256].rearrange("p (h t) -> p h t", h=4)
                S.op("dve", lambda e, m_dst=m_dst, g_src=g_src, t_src=t_src: e.scalar_tensor_tensor(out=m_dst, in0=t_src, scalar=ngt[:, 0:1], in1=g_src, op0=ALU.mult, op1=ALU.mult),
                     reads=[R("ps", 7)] + C + [R("GsT", hg * 4 + hh) for hh in range(4)], writes=[R("mT", hg * 4 + hh) for hh in range(4)])
            last_prompt = (not is_s) and (a == SEQ // HG - 1) and c == 3
            if is_s or last_prompt:
                dst = K.hg_ss[(a - SEQ // HG) * 4 + c] if is_s else K.hg_sp
                S.op("sp", lambda e, dst=dst: e.dma_start(out=dst.rearrange("h k v -> k h v"), in_=Sm[:, :, :]),
                     reads=[R("Sm", h) for h in range(16)], writes=[R("hgout")], dma=True)
        for ob in range(D // 256):
            banks = [2 * (ob % 2), 2 * (ob % 2) + 1]
            dense_block(K, K.hg_wo, ob, NCH, lambda kc: mT[:, kc, :], lambda kc: [R("mT", kc)], HG, banks)
            for o in range(2):
                resid_ln_accum(K, t0, HG, ob * 2 + o, K.ps[banks[o]][:, 0:HG], [R("ps", banks[o])])
        ln_finish(K, layer, 0, t0, HG, False)
    S.barrier()


def s5_host_layout(inp, b):
    a_re, a_im, ldt = inp["ssm_a_re"], inp["ssm_a_im"], inp["ssm_log_dt"]
    na = a_re.shape[0]
    out = {}
    are_t = a_re.transpose(0, 2, 1)
    aim_t = a_im.transpose(0, 2, 1)
    l1 = np.stack([np.concatenate([are_t, are_t], 1), np.concatenate([aim_t, aim_t], 1),
                   np.broadcast_to(ldt[:, None, :], (na, 128, 128))], 1)
    out["s5_l1"] = l1
    rows = np.stack([np.repeat(a_re, 16, axis=1), np.repeat(a_im, 16, axis=1),
                     np.broadcast_to(np.repeat(ldt, 16, axis=1)[:, :, None], (na, D, 64))], 2)
    out["s5_rows"] = rows.reshape(na, NCH, 128, 3, 64)
    bre = inp["ssm_b_re"].transpose(0, 1, 3, 2).reshape(na, D, 64)
    bim = inp["ssm_b_im"].transpose(0, 1, 3, 2).reshape(na, D, 64)
    out["s5_bT"] = np.stack([bre, bim], 2).reshape(na, NCH, 128, 2, 64)
    cre = inp["ssm_c_re"].transpose(0, 1, 3, 2)
    cim = inp["ssm_c_im"].transpose(0, 1, 3, 2)
    cc = np.concatenate([cre, cim], 2)
    out["s5_cT"] = cc.reshape(na, NCH, 8, 128, 16).transpose(0, 1, 3, 2, 4)
    out["s5_d"] = inp["ssm_d"].reshape(na, NCH, 128).transpose(0, 2, 1)
    h0 = np.concatenate([inp["state_ssm_re"][:, 8 * b:8 * b + 8], inp["state_ssm_im"][:, 8 * b:8 * b + 8]], -1)
    out["s5_h0"] = h0
    out["s5_wo"] = inp["ssm_w_out"]
    out["s5_wg"] = inp["ssm_w_gate"]
    return {k: np.ascontiguousarray(v, dtype=np.float32) for k, v in out.items()}


def attn_host_layout(inp, b):
    tab = inp["attn_rel_bias"][0]
    kp = np.arange(128)[:, None, None]
    r = np.array([-1, 0])[None, :, None]
    q = np.arange(128)[None, None, :]
    idx = np.clip(q - (128 * r + kp), -128, 128) + 128
    toep = tab[:, idx].transpose(1, 0, 2, 3)
    far = np.broadcast_to(tab[None, :, 256], (128, 16))
    out = {"at_wqkv": inp["attn_w_qkv"][0], "at_wo": inp["attn_w_o"][0], "at_toep": toep, "at_far": far,
           "at_ck": inp["cache_attn_k"][0, 8 * b:8 * b + 8].reshape(NS, 512, D),
           "at_cv": inp["cache_attn_v"][0, 8 * b:8 * b + 8].reshape(NS, 512, D)}
    return {k: np.ascontiguousarray(v, dtype=np.float32) for k, v in out.items()}


def hgrn_host_layout(inp, b):
    out = {"hg_win": inp["hgrn_w_in"][0], "hg_wo": inp["hgrn_w_o"][0],
           "hg_lb": inp["hgrn_lb_logits"].reshape(4, NCH, 128).transpose(2, 0, 1),
           "hg_ng": inp["hgrn_norm_g"][0].reshape(128, 1),
           "hg_s0": inp["state_hgrn"][0, 8 * b:8 * b + 8]}
    return {k: np.ascontiguousarray(v, dtype=np.float32) for k, v in out.items()}


def make_core_inputs(inp, c):
    b = c % 4
    x_in = np.concatenate([inp["x_prompt"][b], inp["x_sample"][8 * b:8 * b + 8].reshape(NS * DS, D)], axis=0)
    lnp = np.stack([inp["ln_mix_g"], inp["ln_mix_b"], inp["ln_ffn_g"], inp["ln_ffn_b"]], 0)
    lnp = lnp.reshape(4, DEPTH, NCH, 128).transpose(3, 0, 1, 2).reshape(128, 4 * DEPTH * NCH)
    m = {
        "x_in": np.ascontiguousarray(x_in, dtype=np.float32),
        "ffn_w1": np.ascontiguousarray(inp["ffn_w1"], dtype=np.float32),
        "ffn_w2": np.ascontiguousarray(inp["ffn_w2"], dtype=np.float32),
        "lnp": np.ascontiguousarray(lnp, dtype=np.float32),
    }
    if "ssm_a_re" in inp:
        m.update(s5_host_layout(inp, b))
    if "attn_w_qkv" in inp:
        m.update(attn_host_layout(inp, b))
    if "hgrn_w_in" in inp:
        m.update(hgrn_host_layout(inp, b))
    return m


_NC_CACHE = {}


def kernel(**inp):
    inp = {k: np.asarray(v) for k, v in inp.items()}
    if "nc" not in _NC_CACHE:
        _NC_CACHE["nc"] = build({})
    nc = _NC_CACHE["nc"]
    maps4 = [make_core_inputs(inp, c) for c in range(4)]
    in_maps = [maps4[c % 4] for c in range(8)]
    res = run_bass_kernel_spmd(nc, in_maps, core_ids=list(range(8)))
    r = res.results
    f32 = np.float32
    yp = np.stack([r[c]["y_out"][:SEQ] for c in range(4)]).astype(f32)
    ys = np.concatenate([r[c]["y_out"][SEQ:].reshape(NS, DS, D) for c in range(4)]).astype(f32)
    ssm_p = np.stack([r[c]["ssm_p"] for c in range(4)], 1)
    ssm_s = np.concatenate([r[c]["ssm_s"] for c in range(4)], 1)
    kp = np.stack([r[c]["at_kp"] for c in range(4)]).reshape(1, 4, 512, 16, 128)
    vp = np.stack([r[c]["at_vp"] for c in range(4)]).reshape(1, 4, 512, 16, 128)
    ks = np.concatenate([r[c]["at_ks"].reshape(NS, DS, 16, 128) for c in range(4)])[None]
    vs = np.concatenate([r[c]["at_vs"].reshape(NS, DS, 16, 128) for c in range(4)])[None]
    hp = np.stack([r[c]["hg_sp"] for c in range(4)])[None]
    hs = np.concatenate([r[c]["hg_ss"] for c in range(4)])[None]
    out = (yp, ys, ssm_p[..., :64], ssm_p[..., 64:], kp, vp, hp, ssm_s[..., :64], ssm_s[..., 64:], ks, vs, hs)
    return tuple(np.ascontiguousarray(o, dtype=f32) for o in out)
```

```python
import numpy as np
import concourse.bass as bass
import concourse.mybir as mybir
from concourse.bass_utils import run_bass_kernel_spmd

F32 = mybir.dt.float32
BF16 = mybir.dt.bfloat16
AF = mybir.ActivationFunctionType
ALU = mybir.AluOpType
AX = mybir.AxisListType

D = 2048
NCH = 16
SEQ = 2048
NS = 8
DS = 64
T = SEQ + NS * DS
G = 512
NTG = T // G
DFF = 8192
DEPTH = 4
ALPHA = (2.0 * DEPTH) ** 0.25
LN_EPS = 1e-5

SAME_ENGINE_SYNC = True


class Sched:
    ENGS = ("pe", "act", "dve", "pool", "sp")

    def __init__(self, nc, ndma=10):
        self.nc = nc
        self.q = {e: [] for e in self.ENGS}
        self.ticks = {e: 0 for e in self.ENGS}
        self.pending = {e: False for e in self.ENGS}
        self.last_w = {}
        self.readers = {}
        self.waited = {e: {} for e in self.ENGS}
        self.ndma = ndma
        self.dma_val = {}
        self.dma_rr = {e: 0 for e in self.ENGS}
        self.all_dma = []

    def _need(self, eng, tk, waits):
        if tk is None:
            return
        if tk[0] == "e":
            _, e2, n = tk
            if e2 == eng and (eng == "pe" or not SAME_ENGINE_SYNC):
                return
            key = ("e", e2)
            val = n
        else:
            _, qn, idx, val = tk
            key = ("d", qn, idx)
        if self.waited[eng].get(key, 0) >= val:
            return
        cur = waits.get(key, 0)
        if val > cur:
            waits[key] = val

    def op(self, eng, fn, reads=(), writes=(), tick=True, dma=False):
        waits = {}
        for r in reads:
            self._need(eng, self.last_w.get(r), waits)
        for w in writes:
            self._need(eng, self.last_w.get(w), waits)
            for tk in self.readers.get(w, ()):
                self._need(eng, tk, waits)
        if dma:
            idx = self.dma_rr[eng]
            self.dma_rr[eng] = (idx + 1) % self.ndma
            prev = self.dma_val.get((eng, idx), 0)
            if prev:
                self._need(eng, ("d", eng, idx, prev), waits)
            val = prev + 16
            self.dma_val[(eng, idx)] = val
            tk = ("d", eng, idx, val)
            inc = ("d", eng, idx)
        else:
            if tick:
                self.ticks[eng] += 1
                tk = ("e", eng, self.ticks[eng])
                inc = ("e", eng)
                self.pending[eng] = False
            else:
                tk = ("e", eng, self.ticks[eng] + 1)
                inc = None
                self.pending[eng] = True
        for key, val in waits.items():
            self.waited[eng][key] = val
        self.q[eng].append((list(waits.items()), fn, inc))
        for w in writes:
            self.last_w[w] = tk
            self.readers[w] = []
        for r in reads:
            lst = self.readers.setdefault(r, [])
            if tk[0] == "e":
                lst[:] = [x for x in lst if not (x[0] == "e" and x[1] == tk[1])]
            lst.append(tk)
        return tk

    def barrier(self):
        for e in self.ENGS:
            waits = {}
            for e2 in self.ENGS:
                if e2 != e and self.ticks[e2] > 0:
                    self._need(e, ("e", e2, self.ticks[e2]), waits)
            for (qn, idx), val in self.dma_val.items():
                self._need(e, ("d", qn, idx, val), waits)
            for key, val in waits.items():
                self.waited[e][key] = val
            self.q[e].append((list(waits.items()), None, None))

    def finish(self, eng="sp"):
        waits = {}
        for (qn, idx), val in self.dma_val.items():
            self._need(eng, ("d", qn, idx, val), waits)
        self.q[eng].append((list(waits.items()), None, None))

    def simulate(self):
        sem = {}
        pc = {e: 0 for e in self.ENGS}
        progress = True
        while progress:
            progress = False
            for e in self.ENGS:
                while pc[e] < len(self.q[e]):
                    waits, fn, inc = self.q[e][pc[e]]
                    ok = all(sem.get(key, 0) >= val for key, val in waits)
                    if not ok:
                        break
                    if inc is not None:
                        k = ("e", inc[1]) if inc[0] == "e" else ("d", inc[1], inc[2])
                        sem[k] = sem.get(k, 0) + (1 if inc[0] == "e" else 16)
                    pc[e] += 1
                    progress = True
        stuck = {e: (pc[e], len(self.q[e])) for e in self.ENGS if pc[e] < len(self.q[e])}
        if stuck:
            msg = []
            for e, (p, n) in stuck.items():
                waits, fn, inc = self.q[e][p]
                msg.append("%s stuck at %d/%d waits=%s have=%s" % (e, p, n, waits, [sem.get(k, 0) for k, v in waits]))
            raise RuntimeError("DEADLOCK: " + " | ".join(msg))

    def emit(self):
        self.simulate()
        nc = self.nc
        for e in self.ENGS:
            assert not self.pending[e], e
            assert self.ticks[e] < 60000, (e, self.ticks[e])
        import contextlib
        with contextlib.ExitStack() as st:
            esem = {e: st.enter_context(nc.semaphore("se_" + e)) for e in self.ENGS}
            dsem = {}
            for (qn, idx) in self.dma_val:
                dsem[(qn, idx)] = st.enter_context(nc.semaphore("sd_%s_%d" % (qn, idx)))
            block = st.enter_context(nc.Block())

            def run(eng_name):
                def body(eng):
                    for waits, fn, inc in self.q[eng_name]:
                        for key, val in waits:
                            s = esem[key[1]] if key[0] == "e" else dsem[(key[1], key[2])]
                            eng.wait_ge(s, val)
                        if fn is None:
                            continue
                        ins = fn(eng)
                        if inc is not None:
                            if inc[0] == "e":
                                ins.then_inc(esem[inc[1]], 1)
                            else:
                                ins.then_inc(dsem[(inc[1], inc[2])], 16)
                return body

            block.tensor(run("pe"))
            block.scalar(run("act"))
            block.vector(run("dve"))
            block.gpsimd(run("pool"))
            block.sync(run("sp"))


class Ctx:
    pass


def R(name, *idx):
    return (name,) + idx


def xr(t0, W, cs=None):
    cs = range(NCH) if cs is None else cs
    return [R("xT", b, c) for b in range(t0 // 256, (t0 + W + 255) // 256) for c in cs]


def build(cfg):
    nc = bass.Bass("TRN2", target_bir_lowering=False)
    S = Sched(nc)
    K = Ctx()
    K.nc, K.S = nc, S
    K.cfg = cfg
    depth = cfg.get("depth", DEPTH)
    kinds = cfg.get("kinds", [l % 3 for l in range(depth)])
    K.na = max(1, sum(1 for k in kinds if k == 0))

    def din(name, shape):
        return nc.dram_tensor(name, list(shape), F32, kind="ExternalInput").ap()

    def dout(name, shape):
        return nc.dram_tensor(name, list(shape), F32, kind="ExternalOutput").ap()

    K.x_in = din("x_in", [T, D])
    K.ffn_w1 = din("ffn_w1", [cfg.get("wdepth", DEPTH), D, DFF])
    K.ffn_w2 = din("ffn_w2", [cfg.get("wdepth", DEPTH), DFF, D])
    K.lnp = din("lnp", [128, 4 * DEPTH * NCH])
    K.y_out = dout("y_out", [T, D])
    if 0 in kinds:
        na = K.na
        K.s5_l1 = din("s5_l1", [na, 3, 128, 128])
        K.s5_rows = din("s5_rows", [na, NCH, 128, 3, 64])
        K.s5_bT = din("s5_bT", [na, NCH, 128, 2, 64])
        K.s5_cT = din("s5_cT", [na, NCH, 128, 8, 16])
        K.s5_d = din("s5_d", [na, 128, NCH])
        K.s5_h0 = din("s5_h0", [na, NS, 128, 128])
        K.s5_wo = din("s5_wo", [na, D, D])
        K.s5_wg = din("s5_wg", [na, D, D])
        K.ssm_p = dout("ssm_p", [na, 128, 128])
        K.ssm_s = dout("ssm_s", [na, NS, 128, 128])

    if 1 in kinds:
        K.at_wqkv = din("at_wqkv", [D, 3 * D])
        K.at_wo = din("at_wo", [D, D])
        K.at_toep = din("at_toep", [128, 16, 2, 128])
        K.at_far = din("at_far", [128, 16])
        K.at_ck = din("at_ck", [NS, 512, D])
        K.at_cv = din("at_cv", [NS, 512, D])
        K.at_kp = dout("at_kp", [512, D])
        K.at_vp = dout("at_vp", [512, D])
        K.at_ks = dout("at_ks", [NS * DS, D])
        K.at_vs = dout("at_vs", [NS * DS, D])
    if 2 in kinds:
        K.hg_win = din("hg_win", [D, 4 * D])
        K.hg_wo = din("hg_wo", [D, D])
        K.hg_lb = din("hg_lb", [128, 4, NCH])
        K.hg_ng = din("hg_ng", [128, 1])
        K.hg_s0 = din("hg_s0", [NS, 16, 128, 128])
        K.hg_sp = dout("hg_sp", [16, 128, 128])
        K.hg_ss = dout("hg_ss", [NS, 16, 128, 128])
    sb = nc.alloc_sbuf_tensor
    K.xT = sb("xT", [128, NCH, T], BF16)
    big0, big1 = nc.bump_sbuf(65536)
    K.big_off = big0
    K.big = nc.alloc_sbuf_tensor_at("big", [128, 32768], BF16, offset=big0)
    K.stg = [sb("stg%d" % i, [128, 2048], F32) for i in range(3)]
    K.wbf = [sb("wbf%d" % i, [128, 2048], BF16) for i in range(2)]
    K.lnp_sb = sb("lnp_sb", [128, 4 * DEPTH * NCH], F32)
    K.ident_f = sb("ident_f", [128, 128], F32)
    K.ident_b = sb("ident_b", [128, 128], BF16)
    K.ones_b = sb("ones_b", [128, 128], BF16)
    K.zsq = [sb("zsq%d" % i, [128, G], BF16) for i in range(2)]
    K.lnA = sb("lnA", [128, G], F32)
    K.lnB = sb("lnB", [128, G], F32)
    K.lnT = [sb("lnT%d" % i, [128, G], F32) for i in range(2)]
    K.otile = [sb("otile%d" % i, [128, 512], F32) for i in range(2)]
    K.eps_t = sb("eps_t", [128, 1], F32)
    sc0, sc1 = nc.bump_sbuf(14336)
    K.scr_off = sc0
    K.scr_size = 14336
    K.ps = [nc.alloc_psum_tensor("ps%d" % i, [128, 512], F32) for i in range(8)]

    K.stg_rr = 0
    K.wbf_rr = 0
    K.ev_rr = 0
    K.uid = 0

    setup_consts(K)
    load_input(K)
    ia = 0
    for li in range(depth):
        kind = kinds[li]
        layer = li + cfg.get("layer0", 0)
        if kind == 0:
            s5_layer(K, layer, ia)
            ia += 1
        elif kind == 1:
            attn_layer(K, layer)
        elif kind == 2:
            hgrn_layer(K, layer)
        else:
            mixer_none(K, layer)
        if cfg.get("ffn", True):
            ffn(K, layer, last=(li == depth - 1) and not cfg.get("dbg_xT"))
    if cfg.get("dbg_xT"):
        dbg = nc.dram_tensor("dbg", [128, NCH * T], BF16, kind="ExternalOutput").ap()
        S.op("sp", lambda e: e.dma_start(out=dbg, in_=K.xT[:, :, :].rearrange("p c t -> p (c t)")),
             reads=xr(0, T), dma=True)
    if cfg.get("dbg_big"):
        dbg2 = nc.dram_tensor("dbg2", [128, 32768], BF16, kind="ExternalOutput").ap()
        S.op("sp", lambda e: e.dma_start(out=dbg2, in_=K.big[:, :]),
             reads=[R("hT", c) for c in range(64)] + [R("yT", c) for c in range(16)], dma=True)
    S.finish("sp")
    S.emit()
    return nc


def alloc_at(K, name, shape, dtype, off):
    K.uid += 1
    return K.nc.alloc_sbuf_tensor_at("%s_%d" % (name, K.uid), list(shape), dtype, offset=off)


class Arena:
    def __init__(self, K, base, size):
        self.K, self.base, self.size, self.cur = K, base, size, 0

    def get(self, name, shape, dtype):
        n = 1
        for d in shape[1:]:
            n *= d
        nbytes = n * (4 if dtype == F32 else 2)
        nbytes = (nbytes + 31) // 32 * 32
        assert self.cur + nbytes <= self.size, (name, self.cur, nbytes, self.size)
        t = alloc_at(self.K, name, shape, dtype, self.base + self.cur)
        self.cur += nbytes
        return t


def setup_consts(K):
    nc, S = K.nc, K.S
    S.op("pool", lambda e: e.memset(K.ident_f[:, :], 0.0), writes=[R("ident_f")])
    S.op("pool", lambda e: e.affine_select(out=K.ident_f[:, :], in_=K.ident_f[:, :], pattern=[[1, 128]],
                                            compare_op=ALU.not_equal, fill=1.0, base=0, channel_multiplier=-1),
         reads=[R("ident_f")], writes=[R("ident_f")])
    S.op("pool", lambda e: e.tensor_copy(out=K.ident_b[:, :], in_=K.ident_f[:, :]), reads=[R("ident_f")], writes=[R("ident_b")])
    S.op("pool", lambda e: e.memset(K.ones_b[:, :], 1.0), writes=[R("ones_b")])
    S.op("pool", lambda e: e.memset(K.eps_t[:, :], LN_EPS), writes=[R("eps_t")])
    S.op("sp", lambda e: e.dma_start(out=K.lnp_sb[:, :], in_=K.lnp), writes=[R("lnp")], dma=True)


def next_stg(K):
    i = K.stg_rr
    K.stg_rr = (i + 1) % len(K.stg)
    return i


def next_wbf(K):
    i = K.wbf_rr
    K.wbf_rr = (i + 1) % len(K.wbf)
    return i


def evac_engine(K):
    K.ev_rr ^= 1
    return "act" if K.ev_rr else "dve"


def copy_op(K, eng, out, in_, reads, writes):
    if eng == "act":
        return K.S.op("act", lambda e: e.activation(out=out, in_=in_, func=AF.Copy), reads=reads, writes=writes)
    return K.S.op(eng, lambda e: e.tensor_copy(out=out, in_=in_), reads=reads, writes=writes)


def load_input(K):
    S = K.S
    for tt in range(T // 128):
        si = next_stg(K)
        stg = K.stg[si]
        S.op("sp", lambda e, stg=stg, tt=tt: e.dma_start(out=stg[:, :], in_=K.x_in[tt * 128:(tt + 1) * 128, :]),
             writes=[R("stg", si)], dma=True)
        for cb in range(4):
            bank = 4 + (tt * 4 + cb) % 4
            ps = K.ps[bank]
            for j in range(4):
                c = cb * 4 + j
                S.op("pe", lambda e, ps=ps, stg=stg, c=c, j=j: e.transpose(out=ps[:, j * 128:(j + 1) * 128],
                                                                       in_=stg[:, c * 128:(c + 1) * 128],
                                                                       identity=K.ident_f[:, :]),
                     reads=[R("stg", si), R("ident_f")], writes=[R("ps", bank)], tick=(j == 3))
            dst = K.xT[:, cb * 4:(cb + 1) * 4, tt * 128:(tt + 1) * 128]
            src = ps[:, :].rearrange("p (j t) -> p j t", j=4)
            copy_op(K, evac_engine(K), dst, src, [R("ps", bank)], xr(tt * 128, 128, range(cb * 4, cb * 4 + 4)))


def load_w_tile(K, dram_ap):
    S = K.S
    si = next_stg(K)
    wi = next_wbf(K)
    stg = K.stg[si]
    wbf = K.wbf[wi]
    S.op("sp", lambda e: e.dma_start(out=stg[:, :].rearrange("p (k n) -> p k n", k=8), in_=dram_ap), writes=[R("stg", si)], dma=True)
    K.cast_rr = getattr(K, "cast_rr", 0) + 1
    copy_op(K, "act" if K.cast_rr % 2 else "dve", wbf[:, :], stg[:, :], [R("stg", si)], [R("wbf", wi)])
    return wi


def wtile(w2d, kt, ob):
    return w2d[kt * 1024:(kt + 1) * 1024, ob * 256:(ob + 1) * 256].rearrange("(k p) n -> p k n", p=128)


def dense_block(K, w2d, ob, KC, rhs_fn, rhs_res, ncols, banks):
    S = K.S
    nkt = KC // 8
    for kt in range(nkt):
        wi = load_w_tile(K, wtile(w2d, kt, ob))
        wbf = K.wbf[wi]
        for o in range(2):
            ps = K.ps[banks[o]]
            for k in range(8):
                kc = kt * 8 + k
                first = (kt == 0 and k == 0)
                last = (kt == nkt - 1 and k == 7)
                rhs_ap = rhs_fn(kc)
                S.op("pe", lambda e, ps=ps, wbf=wbf, k=k, o=o, rhs_ap=rhs_ap, first=first, last=last:
                     e.matmul(ps[:, 0:ncols], lhsT=wbf[:, k * 256 + o * 128:k * 256 + (o + 1) * 128], rhs=rhs_ap,
                              start=first, stop=last),
                     reads=[R("wbf", wi)] + rhs_res(kc), writes=[R("ps", banks[o])], tick=(last or (o == 1 and k == 7)))


def ln_accum(K, t0, W, c):
    S = K.S
    zc = K.xT[:, c, t0:t0 + W]
    zq = K.zsq[c % 2]
    S.op("act", lambda e: e.activation(out=zq[:, 0:W], in_=zc, func=AF.Square), reads=xr(t0, W, [c]), writes=[R("zsq", c % 2)])
    S.op("pe", lambda e: e.matmul(K.ps[6][:, 0:W], lhsT=K.ones_b[:, :], rhs=zc, start=(c == 0), stop=(c == NCH - 1)),
         reads=xr(t0, W, [c]) + [R("ones_b")], writes=[R("ps", 6)], tick=(c == NCH - 1))
    S.op("pe", lambda e: e.matmul(K.ps[7][:, 0:W], lhsT=K.ones_b[:, :], rhs=zq[:, 0:W], start=(c == 0), stop=(c == NCH - 1)),
         reads=[R("zsq", c % 2), R("ones_b")], writes=[R("ps", 7)], tick=True)


def resid_ln_accum(K, t0, W, c, m_ap, m_res):
    S = K.S
    zc = K.xT[:, c, t0:t0 + W]
    S.op("dve", lambda e: e.scalar_tensor_tensor(out=zc, in0=zc, scalar=ALPHA, in1=m_ap, op0=ALU.mult, op1=ALU.add),
         reads=xr(t0, W, [c]) + m_res, writes=xr(t0, W, [c]))
    ln_accum(K, t0, W, c)


def ln_finish(K, layer, kind, t0, W, last):
    S = K.S
    inv = 1.0 / D
    A, B = K.lnA[:, 0:W], K.lnB[:, 0:W]
    S.op("dve", lambda e: e.tensor_scalar(out=A, in0=K.ps[6][:, 0:W], scalar1=inv, scalar2=None, op0=ALU.mult),
         reads=[R("ps", 6)], writes=[R("lnA")])
    S.op("dve", lambda e: e.tensor_tensor(out=B, in0=A, in1=A, op=ALU.mult), reads=[R("lnA")], writes=[R("lnB")])
    S.op("dve", lambda e: e.scalar_tensor_tensor(out=B, in0=K.ps[7][:, 0:W], scalar=inv, in1=B, op0=ALU.mult, op1=ALU.subtract),
         reads=[R("ps", 7), R("lnB")], writes=[R("lnB")])
    S.op("act", lambda e: e.activation(out=B, in_=B, func=AF.Sqrt, bias=K.eps_t[:, 0:1], scale=1.0),
         reads=[R("lnB"), R("eps_t")], writes=[R("lnB")])
    S.op("dve", lambda e: e.reciprocal(out=B, in_=B), reads=[R("lnB")], writes=[R("lnB")])
    S.op("dve", lambda e: e.tensor_tensor(out=A, in0=A, in1=B, op=ALU.mult), reads=[R("lnA"), R("lnB")], writes=[R("lnA")])
    gi = (2 * kind) * DEPTH * NCH + layer * NCH
    bi = (2 * kind + 1) * DEPTH * NCH + layer * NCH
    for c in range(NCH):
        tmp = K.lnT[c % 2][:, 0:W]
        tr = R("lnT", c % 2)
        zc = K.xT[:, c, t0:t0 + W]
        S.op("dve", lambda e, tmp=tmp, zc=zc: e.tensor_tensor(out=tmp, in0=zc, in1=B, op=ALU.mult),
             reads=xr(t0, W, [c]) + [R("lnB")], writes=[tr])
        S.op("dve", lambda e, tmp=tmp: e.tensor_tensor(out=tmp, in0=tmp, in1=A, op=ALU.subtract),
             reads=[tr, R("lnA")], writes=[tr])
        S.op("act", lambda e, tmp=tmp, zc=zc, c=c: e.activation(out=zc, in_=tmp, func=AF.Identity,
                                                                bias=K.lnp_sb[:, bi + c:bi + c + 1],
                                                                scale=K.lnp_sb[:, gi + c:gi + c + 1]),
             reads=[tr, R("lnp")], writes=xr(t0, W, [c]))
        if last:
            ot = K.otile[c % 2]
            otr = R("otile", c % 2)
            S.op("act", lambda e, tmp=tmp, c=c: e.activation(out=tmp, in_=tmp, func=AF.Identity,
                                                          bias=K.lnp_sb[:, bi + c:bi + c + 1],
                                                          scale=K.lnp_sb[:, gi + c:gi + c + 1]),
                 reads=[tr, R("lnp")], writes=[tr])
            bank = 4 + c % 2
            nj = W // 128
            for j in range(nj):
                S.op("pe", lambda e, tmp=tmp, j=j, bank=bank: e.transpose(out=K.ps[bank][:, j * 128:(j + 1) * 128],
                                                                          in_=tmp[:, j * 128:(j + 1) * 128],
                                                                          identity=K.ident_f[:, :]),
                     reads=[tr, R("ident_f")], writes=[R("ps", bank)], tick=(j == nj - 1))
            S.op("dve", lambda e, ot=ot, bank=bank: e.tensor_copy(out=ot[:, 0:W], in_=K.ps[bank][:, 0:W]),
                 reads=[R("ps", bank)], writes=[otr])
            dst = K.y_out[t0:t0 + W, c * 128:(c + 1) * 128].rearrange("(j p) f -> p j f", p=128)
            S.op("pool", lambda e, ot=ot, dst=dst, nj=nj: e.dma_start(out=dst, in_=ot[:, 0:W].rearrange("p (j f) -> p j f", j=nj)),
                 reads=[otr], writes=[R("yout", t0, c)], dma=True)


def mixer_none(K, layer):
    S = K.S
    for tg in range(NTG):
        t0 = tg * G
        for c in range(NCH):
            zc = K.xT[:, c, t0:t0 + G]
            S.op("dve", lambda e, zc=zc: e.tensor_scalar(out=zc, in0=zc, scalar1=ALPHA, scalar2=None, op0=ALU.mult),
                 reads=xr(t0, G, [c]), writes=xr(t0, G, [c]))
            ln_accum(K, t0, G, c)
        ln_finish(K, layer, 0, t0, G, False)


def ffn(K, layer, last):
    S = K.S
    hT = K.big[:, :].rearrange("p (c t) -> p c t", c=64)
    for tg in range(NTG):
        t0 = tg * G
        for ob in range(DFF // 256):
            banks = [2 * (ob % 2), 2 * (ob % 2) + 1]
            dense_block(K, K.ffn_w1[layer], ob, NCH, lambda kc: K.xT[:, kc, t0:t0 + G], lambda kc: xr(t0, G, [kc]), G, banks)
            for o in range(2):
                relu2(K, ob * 2 + o, banks[o], hT)
        for ob in range(D // 256):
            banks = [2 * (ob % 2), 2 * (ob % 2) + 1]
            dense_block(K, K.ffn_w2[layer], ob, 64, lambda kc: hT[:, kc, :], lambda kc: [R("hT", kc)], G, banks)
            for o in range(2):
                resid_ln_accum(K, t0, G, ob * 2 + o, K.ps[banks[o]][:, 0:G], [R("ps", banks[o])])
        ln_finish(K, layer, 1, t0, G, last)


def relu2(K, oc, bank, hT):
    S = K.S
    tmp = K.lnT[oc % 2]
    S.op("act", lambda e: e.activation(out=tmp[:, :], in_=K.ps[bank][:, :], func=AF.Relu),
         reads=[R("ps", bank)], writes=[R("lnT", oc % 2)])
    S.op("act", lambda e: e.activation(out=hT[:, oc, :], in_=tmp[:, :], func=AF.Square),
         reads=[R("lnT", oc % 2)], writes=[R("hT", oc)])


S5_POW = [1, 2, 3, 4, 5, 6, 7, 8, 16, 32, 64, 128, 256, 512, 1024]


def tt(K, eng, out, a, b, op, reads, writes):
    return K.S.op(eng, lambda e: e.tensor_tensor(out=out, in0=a, in1=b, op=op), reads=reads, writes=writes)


def cmul(K, eng, outr, outi, ar, ai, br, bi, t1, t2, rr, ww):
    tr = [R("s5tmp")]
    tt(K, eng, t1, ar, br, ALU.mult, rr, tr)
    tt(K, eng, t2, ai, bi, ALU.mult, rr, tr)
    tt(K, eng, t2, t1, t2, ALU.subtract, tr, tr)
    tt(K, eng, t1, ar, bi, ALU.mult, rr, tr)
    tt(K, eng, outi, ai, br, ALU.mult, rr + tr, ww)
    tt(K, eng, outi, outi, t1, ALU.add, ww + tr, ww)
    K.S.op(eng, lambda e: e.tensor_copy(out=outr, in_=t2), reads=tr, writes=ww)


def cexp_abar(K, eng, ar_t, ai_t, ldt_t, outr, outi, tmps, rr, ww):
    S = K.S
    d, c, s = tmps
    tr = [R("s5tmp2")]
    S.op("act", lambda e: e.activation(out=d, in_=ldt_t, func=AF.Exp), reads=rr, writes=tr)
    tt(K, eng, c, ai_t, d, ALU.mult, rr + tr, tr)
    tt(K, eng, d, ar_t, d, ALU.mult, rr + tr, tr)
    S.op("act", lambda e: e.activation(out=d, in_=d, func=AF.Exp), reads=tr, writes=tr)
    S.op("act", lambda e: e.activation(out=s, in_=c, func=AF.Sin, scale=1.0 / 16), reads=tr, writes=tr)
    S.op("act", lambda e: e.activation(out=c, in_=c, func=AF.Sin, scale=-1.0 / 16, bias=K.halfpi[:, 0:1]), reads=tr + [R("consts5")], writes=tr)
    for _ in range(4):
        tt(K, eng, outr, c, s, ALU.mult, tr, ww)
        tt(K, eng, c, c, c, ALU.mult, tr, tr)
        tt(K, eng, s, s, s, ALU.mult, tr, tr)
        tt(K, eng, c, c, s, ALU.subtract, tr, tr)
        S.op(eng, lambda e: e.tensor_scalar(out=s, in0=outr, scalar1=2.0, scalar2=None, op0=ALU.mult), reads=ww, writes=tr)
    tt(K, eng, outr, c, d, ALU.mult, tr, ww)
    tt(K, eng, outi, s, d, ALU.mult, tr, ww)


def s5_layer(K, layer, ia):
    nc, S = K.nc, K.S
    S.barrier()
    bg = Arena(K, K.big_off, 65536)
    yT = bg.get("yT", [128, NCH, 1024], BF16)
    PW = bg.get("PW", [128, len(S5_POW), 2, 128], F32)
    bT = bg.get("bT", [128, 8, 128], F32)
    H0 = bg.get("H0", [128, NS, 128], F32)
    FINs = bg.get("FINs", [128, NS, 128], F32)
    Hb = [bg.get("Hb%d" % i, [128, 1024], BF16) for i in range(2)]
    sc = Arena(K, K.scr_off, K.scr_size)
    P = sc.get("P", [128, 4, 132], F32)
    Sb = sc.get("Sb", [128, 4, 128], BF16)
    CAR = sc.get("CAR", [128, 128], F32)
    FINp = sc.get("FINp", [128, 128], F32)
    Ct = sc.get("Ct", [128, 8, 128], BF16)
    cst = sc.get("cst", [128, 8, 16], F32)
    rt = [sc.get("rt%d" % i, [128, 64], F32) for i in range(12)]
    tmpm = [sc.get("tmpm%d" % i, [128, 128], F32) for i in range(4)]
    Jsg = sc.get("Jsg", [128, 128], F32)
    rowmask = sc.get("rowmask", [128, 8], F32)
    dcol = sc.get("dcol", [128, NCH], F32)
    K.halfpi = sc.get("halfpi", [128, 1], F32)
    l1 = [K.lnT[0][:, 0:128], K.lnT[0][:, 128:256], K.lnT[0][:, 256:384], K.lnT[1][:, 0:128], K.lnT[1][:, 128:256], K.lnT[1][:, 256:384],
          K.lnT[0][:, 384:512], K.lnT[1][:, 384:512]]
    l1r = [R("lnT", 0), R("lnT", 1)]
    CS = [R("consts5")]

    S.op("pool", lambda e: e.memset(K.halfpi[:, :], float(np.pi / 2)), writes=CS)
    S.op("pool", lambda e: e.memset(Jsg[:, :], 0.0), writes=CS)
    S.op("pool", lambda e: e.affine_select(out=Jsg[:, :], in_=Jsg[:, :], pattern=[[1, 128]], compare_op=ALU.not_equal,
                                            fill=1.0, base=-64, channel_multiplier=-1), reads=CS, writes=CS)
    S.op("pool", lambda e: e.affine_select(out=Jsg[:, :], in_=Jsg[:, :], pattern=[[1, 128]], compare_op=ALU.not_equal,
                                            fill=-1.0, base=64, channel_multiplier=-1), reads=CS, writes=CS)
    S.op("pool", lambda e: e.memset(rowmask[:, :], 1.0), writes=CS)
    S.op("pool", lambda e: e.affine_select(out=rowmask[:, :], in_=rowmask[:, :], pattern=[[-16, 8]], compare_op=ALU.is_ge,
                                            fill=0.0, base=0, channel_multiplier=1), reads=CS, writes=CS)
    S.op("pool", lambda e: e.affine_select(out=rowmask[:, :], in_=rowmask[:, :], pattern=[[16, 8]], compare_op=ALU.is_ge,
                                            fill=0.0, base=15, channel_multiplier=-1), reads=CS, writes=CS)
    S.op("pool", lambda e: e.memset(Ct[:, :, :], 0.0), writes=[R("Ct")])
    S.op("sp", lambda e: e.dma_start(out=dcol[:, :], in_=K.s5_d[ia]), writes=[R("dcol")], dma=True)

    for i in range(3):
        S.op("sp", lambda e, i=i: e.dma_start(out=l1[i], in_=K.s5_l1[ia, i]), writes=l1r, dma=True)

    def pw(n, ri):
        return PW[:, S5_POW.index(n), ri, :]
    PWR = [R("PW")]
    cexp_abar(K, "dve", l1[0], l1[1], l1[2], pw(1, 0), pw(1, 1), [l1[3], l1[4], l1[5]], l1r, PWR)
    for n in range(2, 9):
        cmul(K, "dve", pw(n, 0), pw(n, 1), pw(n - 1, 0), pw(n - 1, 1), pw(1, 0), pw(1, 1), l1[6], l1[7], PWR + l1r, PWR)
    for n in S5_POW[8:]:
        cmul(K, "dve", pw(n, 0), pw(n, 1), pw(n // 2, 0), pw(n // 2, 1), pw(n // 2, 0), pw(n // 2, 1), l1[6], l1[7], PWR + l1r, PWR)

    si = next_stg(K)
    stg = K.stg[si]
    S.op("sp", lambda e: e.dma_start(out=stg[:, 0:NS * 128].rearrange("g (s k) -> g s k", s=NS),
                                     in_=K.s5_h0[ia].rearrange("s g k -> g s k")), writes=[R("stg", si)], dma=True)
    for s in range(NS):
        bank = 4 + s % 2
        S.op("pe", lambda e, s=s, bank=bank: e.transpose(out=K.ps[bank][:, 0:128], in_=stg[:, s * 128:(s + 1) * 128], identity=K.ident_f[:, :]),
             reads=[R("stg", si), R("ident_f")], writes=[R("ps", bank)])
        copy_op(K, evac_engine(K), H0[:, s, :], K.ps[bank][:, 0:128], [R("ps", bank)], [R("H0")])

    passes = [(0, 1024, "A"), (1024, 1024, "B"), (2048, 512, "C")]
    if K.cfg.get("s5_passes"):
        passes = K.cfg["s5_passes"]
    mats_res = [[R("stg", 0)], [R("stg", 1)], [R("stg", 2)], [R("wbf", 0), R("wbf", 1)]]

    def mats(i):
        if i < 3:
            mb = K.stg[i][:, 0:1024].bitcast(BF16).rearrange("p (m c) -> p m c", m=16)
            mf = K.stg[i][:, 1024:2048].rearrange("p (m c) -> p m c", m=8)
        else:
            mb = K.wbf[0][:, :].rearrange("p (m c) -> p m c", m=16)
            mf = K.wbf[1][:, :].bitcast(F32).rearrange("p (m c) -> p m c", m=8)
        return mb, mf

    for (t0, W, pk) in passes:
        nch = W // 8
        ncol = (nch + 1) if pk != "C" else 9
        for fc in range(NCH):
            RT = [R("rt")]
            si = next_stg(K)
            stg2 = K.stg[si]
            S.op("sp", lambda e, stg2=stg2, fc=fc: e.dma_start(out=stg2[:, 0:192].rearrange("p (i k) -> p i k", i=3), in_=K.s5_rows[ia, fc]),
                 writes=[R("stg", si)], dma=True)
            S.op("sp", lambda e, stg2=stg2, fc=fc: e.dma_start(out=stg2[:, 192:320].rearrange("p (i k) -> p i k", i=2), in_=K.s5_bT[ia, fc]),
                 writes=[R("stg", si)], dma=True)
            S.op("sp", lambda e, fc=fc: e.dma_start(out=cst[:, :, :], in_=K.s5_cT[ia, fc]), writes=[R("cst")], dma=True)
            lam_r, lam_i, ldt = stg2[:, 0:64], stg2[:, 64:128], stg2[:, 128:192]
            b_r, b_i = stg2[:, 192:256], stg2[:, 256:320]
            SG = [R("stg", si)]
            abr, abi = rt[0], rt[1]
            cexp_abar(K, "dve", lam_r, lam_i, ldt, abr[:, :], abi[:, :], [rt[2][:, :], rt[3][:, :], rt[4][:, :]], SG, RT)
            nr, den, cr, ci = rt[2][:, :], rt[3][:, :], rt[5][:, :], rt[6][:, :]
            t1, t2 = rt[7][:, :], rt[8][:, :]
            S.op("dve", lambda e: e.tensor_scalar(out=nr, in0=abr[:, :], scalar1=-1.0, scalar2=None, op0=ALU.add), reads=RT, writes=RT)
            tt(K, "dve", den, lam_r, lam_r, ALU.mult, SG, RT)
            tt(K, "dve", t1, lam_i, lam_i, ALU.mult, SG, RT)
            tt(K, "dve", den, den, t1, ALU.add, RT, RT)
            S.op("dve", lambda e: e.reciprocal(out=den, in_=den), reads=RT, writes=RT)
            tt(K, "dve", cr, nr, lam_r, ALU.mult, RT + SG, RT)
            tt(K, "dve", t1, abi[:, :], lam_i, ALU.mult, RT + SG, RT)
            tt(K, "dve", cr, cr, t1, ALU.add, RT, RT)
            tt(K, "dve", cr, cr, den, ALU.mult, RT, RT)
            tt(K, "dve", ci, abi[:, :], lam_r, ALU.mult, RT + SG, RT)
            tt(K, "dve", t1, nr, lam_i, ALU.mult, RT + SG, RT)
            tt(K, "dve", ci, ci, t1, ALU.subtract, RT, RT)
            tt(K, "dve", ci, ci, den, ALU.mult, RT, RT)
            BT = [R("bT")]
            cmul(K, "dve", bT[:, 0, 0:64], bT[:, 0, 64:128], cr, ci, b_r, b_i, t1, t2, RT + SG, BT)
            for tau in range(1, 8):
                cmul(K, "dve", bT[:, tau, 0:64], bT[:, tau, 64:128], bT[:, tau - 1, 0:64], bT[:, tau - 1, 64:128],
                     abr[:, :], abi[:, :], t1, t2, RT + BT, BT)
            for g8 in range(8):
                S.op("pool", lambda e, g8=g8: e.tensor_copy(out=Ct[0:64, g8, 16 * g8:16 * g8 + 16], in_=cst[0:64, g8, :]),
                     reads=[R("cst")], writes=[R("Ct")])
                S.op("pool", lambda e, g8=g8: e.tensor_scalar(out=Ct[64:128, g8, 16 * g8:16 * g8 + 16], in0=cst[64:128, g8, :],
                                                              scalar1=-1.0, scalar2=None, op0=ALU.mult),
                     reads=[R("cst")], writes=[R("Ct")])

            def uT(s, fc=fc, t0=t0, W=W):
                return K.xT[:, fc, t0 + s:t0 + W:8]
            ures = xr(t0, W, [fc])
            ybanks = [5, 6]
            for b4 in range(2):
                g0 = fc * 8 + b4 * 4
                for i in range(4):
                    g = g0 + i
                    mb, mf = mats(i)
                    for n in range(1, 9):
                        tm = tmpm[n % 4]
                        S.op("act", lambda e, tm=tm, n=n, g=g: e.activation(out=tm[:, :], in_=Jsg[:, :], func=AF.Copy, scale=pw(n, 1)[:, g:g + 1]),
                             reads=PWR + CS, writes=[R("tmpm", n % 4)])
                        S.op("dve", lambda e, tm=tm, n=n, g=g, mb=mb: e.scalar_tensor_tensor(out=mb[:, n - 1, :], in0=K.ident_f[:, :], scalar=pw(n, 0)[:, g:g + 1],
                                                                                               in1=tm[:, :], op0=ALU.mult, op1=ALU.add),
                             reads=PWR + [R("tmpm", n % 4), R("ident_f")], writes=mats_res[i])
                    for k in range(8):
                        if (1 << k) >= ncol:
                            continue
                        n = 8 << k
                        tm = tmpm[k % 4]
                        S.op("act", lambda e, tm=tm, n=n, g=g: e.activation(out=tm[:, :], in_=Jsg[:, :], func=AF.Copy, scale=pw(n, 1)[:, g:g + 1]),
                             reads=PWR + CS, writes=[R("tmpm", k % 4)])
                        S.op("dve", lambda e, tm=tm, n=n, g=g, mf=mf, k=k: e.scalar_tensor_tensor(out=mf[:, k, :], in0=K.ident_f[:, :], scalar=pw(n, 0)[:, g:g + 1],
                                                                                                    in1=tm[:, :], op0=ALU.mult, op1=ALU.add),
                             reads=PWR + [R("tmpm", k % 4), R("ident_f")], writes=mats_res[i])
                    S.op("act", lambda e, g=g, mb=mb: e.activation(out=mb[:, 8:16, :], in_=bT[:, :, :], func=AF.Copy, scale=rowmask[:, g % 8:g % 8 + 1]),
                         reads=BT + CS, writes=mats_res[i])
                for i in range(4):
                    mb, mf = mats(i)
                    bank = 3 + i // 2
                    off = (i % 2) * 256
                    for s in range(8):
                        u_ap = uT(s)
                        o_ap = K.ps[bank][:, off:off + nch]
                        S.op("pe", lambda e, mb=mb, s=s, u_ap=u_ap, o_ap=o_ap: e.matmul(o_ap, lhsT=mb[:, 8 + 7 - s, :], rhs=u_ap,
                                                                                                 start=(s == 0), stop=(s == 7)),
                             reads=mats_res[i] + ures, writes=[R("ps", bank)], tick=(s == 7))
                for b in range(2):
                    bank = 3 + b
                    src = K.ps[bank][:, :].rearrange("p (i c) -> p i c", i=2)[:, :, 0:nch]
                    if pk != "C":
                        dst = P[:, 2 * b:2 * b + 2, 1:1 + nch]
                    else:
                        dst = P[:, 2 * b:2 * b + 2, 0:72].rearrange("p i (s c) -> p i s c", c=9)[:, :, :, 1:9]
                        src = K.ps[bank][:, :].rearrange("p (i s c) -> p i s c", i=2, c=8)[:, :, 0:8, :]
                    copy_op(K, "act", dst, src, [R("ps", bank)], [R("P")])
                if pk == "A":
                    S.op("pool", lambda e: e.memset(P[:, :, 0:1], 0.0), writes=[R("P")])
                elif pk == "B":
                    S.op("pool", lambda e, g0=g0: e.tensor_copy(out=P[:, :, 0], in_=CAR[:, g0:g0 + 4]), reads=[R("CAR")], writes=[R("P")])
                else:
                    S.op("pool", lambda e, g0=g0: e.tensor_copy(out=P[:, :, 0:72].rearrange("p i (s c) -> p i s c", c=9)[:, :, :, 0],
                                                               in_=H0[:, :, g0:g0 + 4].rearrange("p s i -> p i s")), reads=[R("H0")], writes=[R("P")])
                k = 0
                while (1 << k) < ncol:
                    sh = 1 << k
                    for i in range(4):
                        mb, mf = mats(i)
                        bank = 3 + i // 2
                        off = (i % 2) * 256
                        if pk != "C":
                            rhs = P[:, i, 0:ncol - sh]
                            out = K.ps[bank][:, off:off + ncol - sh]
                        else:
                            rhs = P[:, i, 0:72].rearrange("p (s c) -> p s c", c=9)[:, :, 0:9 - sh]
                            out = K.ps[bank][:, off:off + 8 * (9 - sh)].rearrange("p (s c) -> p s c", s=8)
                        S.op("pe", lambda e, mf=mf, k=k, rhs=rhs, out=out: e.matmul(out, lhsT=mf[:, k, :], rhs=rhs, start=True, stop=True),
                             reads=mats_res[i] + [R("P")], writes=[R("ps", bank)], tick=(i % 2 == 1))
                    for b in range(2):
                        bank = 3 + b
                        if pk != "C":
                            dst = P[:, 2 * b:2 * b + 2, sh:ncol]
                            src = K.ps[bank][:, :].rearrange("p (i c) -> p i c", i=2)[:, :, 0:ncol - sh]
                        else:
                            dst = P[:, 2 * b:2 * b + 2, 0:72].rearrange("p i (s c) -> p i s c", c=9)[:, :, :, sh:9]
                            src = K.ps[bank][:, :].rearrange("p (i x) -> p i x", i=2)[:, :, 0:8 * (9 - sh)].rearrange("p i (s c) -> p i s c", s=8)
                        S.op("dve", lambda e, dst=dst, src=src: e.tensor_tensor(out=dst, in0=dst, in1=src, op=ALU.add),
                             reads=[R("P"), R("ps", bank)], writes=[R("P")])
                    k += 1
                if pk != "C":
                    copy_op(K, "act", Sb[:, :, 0:nch], P[:, :, 0:nch], [R("P")], [R("Sb")])
                    dstc = (CAR if pk == "A" else FINp)[:, g0:g0 + 4]
                    copy_op(K, "pool", dstc, P[:, :, nch], [R("P")], [R("CAR")] if pk == "A" else [R("FINp")])
                else:
                    pv = P[:, :, 0:72].rearrange("p i (s c) -> p i s c", c=9)
                    copy_op(K, "act", Sb[:, :, 0:64].rearrange("p i (s c) -> p i s c", c=8), pv[:, :, :, 0:8], [R("P")], [R("Sb")])
                    copy_op(K, "pool", FINs[:, :, g0:g0 + 4].rearrange("p s i -> p i s"), pv[:, :, :, 8], [R("P")], [R("FINs")])
                for i in range(4):
                    g = g0 + i
                    g8 = g % 8
                    mb, mf = mats(i)
                    hb = Hb[i % 2]
                    hres = [R("Hb", i % 2)]
                    hv = hb[:, 0:W].rearrange("p (c j) -> p j c", j=8)
                    xv = K.xT[:, fc, t0:t0 + W].rearrange("p (c j) -> p j c", j=8)
                    bank2 = [(0, 1), (2, 7)][i % 2]
                    for hlf in range(2):
                        bank = bank2[hlf]
                        pv = K.ps[bank][:, :].rearrange("p (j c) -> p j c", j=4)
                        jlo = 4 * hlf
                        for d in range(jlo + 4):
                            ja = max(jlo, d)
                            rhs_ap = xv[:, ja - d:jlo + 4 - d, :]
                            o_ap = pv[:, ja - jlo:4, 0:nch]
                            S.op("pe", lambda e, mb=mb, d=d, rhs_ap=rhs_ap, o_ap=o_ap: e.matmul(o_ap, lhsT=mb[:, 8 + d, :], rhs=rhs_ap,
                                                                                                     start=(d == 0), stop=False),
                                 reads=mats_res[i] + ures, writes=[R("ps", bank)], tick=False)
                        for j in range(jlo, jlo + 4):
                            o_ap = pv[:, j - jlo, 0:nch]
                            sb_ap = Sb[:, i, 0:nch]
                            S.op("pe", lambda e, mb=mb, j=j, o_ap=o_ap, sb_ap=sb_ap: e.matmul(o_ap, lhsT=mb[:, j, :], rhs=sb_ap,
                                                                                               start=False, stop=True),
                                 reads=mats_res[i] + [R("Sb")], writes=[R("ps", bank)], tick=(j == jlo + 3))
                        copy_op(K, evac_engine(K), hv[:, jlo:jlo + 4, :], pv[:, :, 0:nch], [R("ps", bank)], hres)
                    for tb in range(W // 512):
                        S.op("pe", lambda e, g8=g8, hb=hb, tb=tb: e.matmul(K.ps[ybanks[tb]][:, :], lhsT=Ct[:, g8, :], rhs=hb[:, tb * 512:(tb + 1) * 512],
                                                                          start=(g8 == 0), stop=(g8 == 7)),
                             reads=[R("Ct")] + hres, writes=[R("ps", ybanks[tb])], tick=True)
            for tb in range(W // 512):
                bank = ybanks[tb]
                ua = K.xT[:, fc, t0 + tb * 512:t0 + (tb + 1) * 512]
                y1, t2_ = K.lnT[0][:, :], K.lnT[1][:, :]
                S.op("dve", lambda e, ua=ua, bank=bank, fc=fc: e.scalar_tensor_tensor(out=y1, in0=ua, scalar=dcol[:, fc:fc + 1], in1=K.ps[bank][:, :],
                                                                                      op0=ALU.mult, op1=ALU.add),
                     reads=ures + [R("dcol"), R("ps", bank)], writes=[R("lnT", 0)])
                S.op("act", lambda e: e.activation(out=t2_, in_=y1, func=AF.Square), reads=[R("lnT", 0)], writes=[R("lnT", 1)])
                S.op("dve", lambda e: e.tensor_scalar(out=t2_, in0=t2_, scalar1=0.044715, scalar2=1.0, op0=ALU.mult, op1=ALU.add),
                     reads=[R("lnT", 1)], writes=[R("lnT", 1)])
                tt(K, "dve", t2_, t2_, y1, ALU.mult, [R("lnT", 0), R("lnT", 1)], [R("lnT", 1)])
                S.op("act", lambda e: e.activation(out=t2_, in_=t2_, func=AF.Sigmoid, scale=1.5957691216057308), reads=[R("lnT", 1)], writes=[R("lnT", 1)])
                yo = yT[:, fc, tb * 512:(tb + 1) * 512]
                tt(K, "dve", yo, y1, t2_, ALU.mult, [R("lnT", 0), R("lnT", 1)], [R("yT", fc)])
        for tb in range(W // 512):
            tg0 = t0 + tb * 512
            for ob in range(D // 256):
                def rhs(kc, tb=tb):
                    return yT[:, kc, tb * 512:(tb + 1) * 512]
                dense_block(K, K.s5_wo[ia], ob, NCH, rhs, lambda kc: [R("yT", kc)], 512, [0, 1])
                dense_block(K, K.s5_wg[ia], ob, NCH, rhs, lambda kc: [R("yT", kc)], 512, [2, 3])
                for o in range(2):
                    sg = K.lnT[o][:, :]
                    S.op("act", lambda e, sg=sg, o=o: e.activation(out=sg, in_=K.ps[2 + o][:, :], func=AF.Sigmoid), reads=[R("ps", 2 + o)], writes=[R("lnT", o)])
                    tt(K, "dve", sg, sg, K.ps[o][:, :], ALU.mult, [R("lnT", o), R("ps", o)], [R("lnT", o)])
                    resid_ln_accum(K, tg0, 512, ob * 2 + o, sg, [R("lnT", o)])
            ln_finish(K, layer, 0, tg0, 512, False)
    def out_state(src_ap, src_res, dst):
        bank = 4
        S.op("pe", lambda e: e.transpose(out=K.ps[bank][:, 0:128], in_=src_ap, identity=K.ident_f[:, :]),
             reads=src_res + [R("ident_f")], writes=[R("ps", bank)])
        ot = K.otile[0]
        copy_op(K, "dve", ot[:, 0:128], K.ps[bank][:, 0:128], [R("ps", bank)], [R("otile", 0)])
        S.op("pool", lambda e: e.dma_start(out=dst, in_=ot[:, 0:128]), reads=[R("otile", 0)], writes=[R("sout")], dma=True)
    if any(p[2] == "B" for p in passes):
        out_state(FINp[:, :], [R("FINp")], K.ssm_p[ia])
    if any(p[2] == "C" for p in passes):
        for s in range(NS):
            out_state(FINs[:, s, :], [R("FINs")], K.ssm_s[ia, s])
    S.barrier()


AG = 256


def dense_block_tm(K, w2d, ob, t0, banks):
    S = K.S
    for kt in range(2):
        wi = load_w_tile(K, wtile(w2d, kt, ob))
        wbf = K.wbf[wi]
        for t in range(2):
            for k in range(8):
                kc = kt * 8 + k
                first = (kt == 0 and k == 0)
                last = (kt == 1 and k == 7)
                lhs = K.xT[:, kc, t0 + t * 128:t0 + (t + 1) * 128]
                o_ap = K.ps[banks[t]][:, 0:256]
                S.op("pe", lambda e, lhs=lhs, wbf=wbf, k=k, o_ap=o_ap, first=first, last=last:
                     e.matmul(o_ap, lhsT=lhs, rhs=wbf[:, k * 256:(k + 1) * 256], start=first, stop=last),
                     reads=[R("wbf", wi)] + xr(t0 + t * 128, 128, [kc]), writes=[R("ps", banks[t])], tick=(last or (t == 1 and k == 7)))


def attn_layer(K, layer):
    nc, S = K.nc, K.S
    S.barrier()
    bg = Arena(K, K.big_off, 65536)
    KT = bg.get("KT", [128, 16, 768], BF16)
    V = bg.get("V", [128, 6, 2048], BF16)
    oT = bg.get("oT", [128, 16, AG], BF16)
    toep = bg.get("toep", [128, 16, 2, 128], BF16)
    sc = Arena(K, K.scr_off, K.scr_size)
    qT = sc.get("qT", [128, 16, AG], BF16)
    PT = [sc.get("PT%d" % i, [128, 640], BF16) for i in range(2)]
    Ssb = sc.get("Ssb", [128, 256], F32)
    otok = [sc.get("otok%d" % i, [128, 128], BF16) for i in range(2)]
    farcol = sc.get("farcol", [128, 16], F32)
    rc = [sc.get("rc%d" % i, [128, 1], F32) for i in range(2)]
    wqkv = K.at_wqkv
    scale = float(128 ** -0.5)

    si = next_stg(K)
    stg = K.stg[si]
    for half in range(2):
        S.op("sp", lambda e, half=half, stg=stg: e.dma_start(out=stg[:, :].rearrange("p (h r q) -> p h r q", h=8, r=2), in_=K.at_toep[:, half * 8:(half + 1) * 8]),
             writes=[R("stg", si)], dma=True)
        S.op("pool", lambda e, half=half, stg=stg: e.tensor_copy(out=toep[:, half * 8:(half + 1) * 8, :, :], in_=stg[:, :].rearrange("p (h r q) -> p h r q", h=8, r=2)),
             reads=[R("stg", si)], writes=[R("toep")])
    S.op("sp", lambda e: e.dma_start(out=farcol[:, :], in_=K.at_far), writes=[R("farcol")], dma=True)
    cnt = [0]

    def attend(h, nq, q_ap, tiles, sample_i, o_dst, o_dst_res):
        u = cnt[0] % 2
        cnt[0] += 1
        pt = PT[u]
        ptr = [R("PT", u)]
        far = [t for t in tiles if t[5] == "far"]
        near = [t for t in tiles if t[5] != "far"]
        for (r, kt_ap, kt_res, v_ap, v_res, kind) in tiles:
            bank = 4 if kind == "far" else 5
            col = (r if kind == "far" else r - 3) * nq
            o_ap = K.ps[bank][:, col:col + nq]
            S.op("pe", lambda e, o_ap=o_ap, kt_ap=kt_ap: e.matmul(o_ap, lhsT=kt_ap, rhs=q_ap, start=True, stop=True),
                 reads=kt_res + [R("qT")], writes=[R("ps", bank)], tick=True)
        if far:
            r0 = far[0][0]
            src = K.ps[4][:, r0 * nq:3 * nq]
            dst = pt[:, r0 * nq:3 * nq]
            S.op("act", lambda e: e.activation(out=dst, in_=src, func=AF.Exp, bias=farcol[:, h:h + 1], scale=1.0),
                 reads=[R("ps", 4), R("farcol")], writes=ptr)
        r0n = near[0][0]
        for (r, kt_ap, kt_res, v_ap, v_res, kind) in near:
            col = (r - 3) * nq
            if sample_i is None:
                bias_ap = toep[:, h, r - 3, :]
            elif r == 3:
                bias_ap = toep[:, h, 0, 0:64]
            else:
                bias_ap = toep[:, h, 1, (sample_i % 2) * 64:(sample_i % 2) * 64 + 64]
            s_ap = Ssb[:, col:col + nq]
            p_ap = K.ps[5][:, col:col + nq]
            S.op("dve", lambda e, s_ap=s_ap, p_ap=p_ap, bias_ap=bias_ap: e.tensor_tensor(out=s_ap, in0=p_ap, in1=bias_ap, op=ALU.add),
                 reads=[R("ps", 5), R("toep")], writes=[R("Ssb")])
        srcn = Ssb[:, (r0n - 3) * nq:2 * nq]
        dstn = pt[:, r0n * nq:5 * nq]
        S.op("act", lambda e: e.activation(out=dstn, in_=srcn, func=AF.Exp), reads=[R("Ssb")], writes=ptr)
        if sample_i is None:
            if far and far[0][0] == 0:
                S.op("pool", lambda e: e.memset(pt[0:64, 64:128], 0.0), writes=ptr)
            S.op("pool", lambda e: e.memset(pt[64:128, 4 * 128:4 * 128 + 64], 0.0), writes=ptr)
        else:
            oh = (1 - sample_i % 2) * 64
            S.op("pool", lambda e: e.memset(pt[oh:oh + 64, 4 * nq:5 * nq], 0.0), writes=ptr)
        ob_ = K.ps[6]
        for idx, (r, kt_ap, kt_res, v_ap, v_res, kind) in enumerate(tiles):
            l_ap = pt[:, r * nq:(r + 1) * nq]
            S.op("pe", lambda e, l_ap=l_ap, v_ap=v_ap, idx=idx: e.matmul(ob_[0:nq, 0:128], lhsT=l_ap, rhs=v_ap, start=(idx == 0), stop=(idx == len(tiles) - 1)),
                 reads=ptr + v_res, writes=[R("ps", 6)], tick=False)
        for idx, (r, kt_ap, kt_res, v_ap, v_res, kind) in enumerate(tiles):
            l_ap = pt[:, r * nq:(r + 1) * nq]
            S.op("pe", lambda e, l_ap=l_ap, idx=idx: e.matmul(ob_[0:nq, 128:129], lhsT=l_ap, rhs=K.ones_b[:, 0:1], start=(idx == 0), stop=(idx == len(tiles) - 1)),
                 reads=ptr + [R("ones_b")], writes=[R("ps", 6)], tick=(idx == len(tiles) - 1))
        rcu = rc[u]
        S.op("dve", lambda e: e.reciprocal(out=rcu[0:nq, :], in_=ob_[0:nq, 128:129]), reads=[R("ps", 6)], writes=[R("rc", u)])
        ot = otok[u]
        S.op("act", lambda e: e.activation(out=ot[0:nq, :], in_=ob_[0:nq, 0:128], func=AF.Copy, scale=rcu[0:nq, 0:1]),
             reads=[R("ps", 6), R("rc", u)], writes=[R("otok", u)])
        tp = K.ps[7][:, :].bitcast(BF16)
        S.op("pe", lambda e: e.transpose(out=tp[:, 0:nq], in_=ot[0:nq, :], identity=K.ident_b[0:nq, 0:nq]),
             reads=[R("otok", u), R("ident_b")], writes=[R("ps", 7)])
        copy_op(K, "dve", o_dst, tp[:, 0:nq], [R("ps", 7)], o_dst_res)

    nag = T // AG
    ags = K.cfg.get("attn_ags", list(range(nag)))
    for a in ags:
        t0 = a * AG
        is_s = a >= SEQ // AG
        want_out = (is_s or (t0 >= SEQ - 512)) and not K.cfg.get("no_out")
        if not is_s:
            kcol = (a % 3) * 256
        else:
            kcol = 512
        for ob in range(8):
            banks = [2 * (ob % 2), 2 * (ob % 2) + 1]
            dense_block(K, wqkv, 8 + ob, NCH, lambda kc: K.xT[:, kc, t0:t0 + AG], lambda kc: xr(t0, AG, [kc]), AG, banks)
            for o in range(2):
                hh = ob * 2 + o
                copy_op(K, evac_engine(K), KT[:, hh, kcol:kcol + AG], K.ps[banks[o]][:, 0:AG], [R("ps", banks[o])], [R("KT", kcol // 256)])
        for which in (K.cfg.get("whichs", [1, 2]) if want_out else [2]):
            for ob in range(8):
                banks = [2 * (ob % 2), 2 * (ob % 2) + 1]
                dense_block_tm(K, wqkv, which * 8 + ob, t0, banks)
                for t in range(2):
                    if not is_s:
                        slot = (2 * a + t) % 6
                    else:
                        slot = 4 + t
                    src = K.ps[banks[t]][:, 0:256]
                    if not want_out:
                        copy_op(K, "act", V[:, slot, ob * 256:(ob + 1) * 256], src, [R("ps", banks[t])], [R("V", slot)])
                    else:
                        ot = K.otile[t]
                        copy_op(K, "dve", ot[:, 0:256], src, [R("ps", banks[t])], [R("otile", t)])
                        if which == 2:
                            copy_op(K, "act", V[:, slot, ob * 256:(ob + 1) * 256], ot[:, 0:256], [R("otile", t)], [R("V", slot)])
                        if is_s:
                            dr = (a - SEQ // AG) * AG + t * 128
                            dst = (K.at_ks if which == 1 else K.at_vs)[dr:dr + 128, ob * 256:(ob + 1) * 256]
                        else:
                            dr = t0 - (SEQ - 512) + t * 128
                            dst = (K.at_kp if which == 1 else K.at_vp)[dr:dr + 128, ob * 256:(ob + 1) * 256]
                        if not K.cfg.get("no_dma"):
                            S.op("sp", lambda e, dst=dst, ot=ot: e.dma_start(out=dst, in_=ot[:, 0:256]), reads=[R("otile", t)], writes=[R("kvout")], dma=True)
        for ob in range(8):
            banks = [2 * (ob % 2), 2 * (ob % 2) + 1]
            dense_block(K, wqkv, ob, NCH, lambda kc: K.xT[:, kc, t0:t0 + AG], lambda kc: xr(t0, AG, [kc]), AG, banks)
            for o in range(2):
                hh = ob * 2 + o
                src = K.ps[banks[o]][:, 0:AG]
                dst = qT[:, hh, :]
                S.op("act", lambda e, src=src, dst=dst: e.activation(out=dst, in_=src, func=AF.Copy, scale=scale), reads=[R("ps", banks[o])], writes=[R("qT")])
        stage = K.cfg.get("attn_stage", 3)
        if stage < 2:
            continue
        if not is_s:
            for h in range(16):
                for qi in range(2):
                    qt = 2 * a + qi
                    tiles = []
                    for r in range(5):
                        j = qt - 4 + r
                        if j < 0:
                            continue
                        blk = (j // 2) % 3
                        col = blk * 256 + (j % 2) * 128
                        tiles.append((r, KT[:, h, col:col + 128], [R("KT", blk)], V[:, j % 6, h * 128:(h + 1) * 128], [R("V", j % 6)],
                                      "far" if r < 3 else "near"))
                    attend(h, 128, qT[:, h, qi * 128:(qi + 1) * 128], tiles, None, oT[:, h, qi * 128:(qi + 1) * 128], [R("oT", h)])
        else:
            sa = a - SEQ // AG
            for i in range(4):
                sq = sa * 4 + i
                for jt in range(4):
                    si = next_stg(K)
                    stg = K.stg[si]
                    S.op("sp", lambda e, stg=stg, sq=sq, jt=jt: e.dma_start(out=stg[:, :], in_=K.at_ck[sq, jt * 128:(jt + 1) * 128, :]), writes=[R("stg", si)], dma=True)
                    for cb in range(4):
                        bank = cb % 4
                        for jj in range(4):
                            c = cb * 4 + jj
                            o_ap = K.ps[bank][:, jj * 128:(jj + 1) * 128]
                            S.op("pe", lambda e, stg=stg, c=c, o_ap=o_ap: e.transpose(out=o_ap, in_=stg[:, c * 128:(c + 1) * 128], identity=K.ident_f[:, :]),
                                 reads=[R("stg", si), R("ident_f")], writes=[R("ps", bank)], tick=(jj == 3))
                        copy_op(K, evac_engine(K), KT[:, cb * 4:(cb + 1) * 4, jt * 128:(jt + 1) * 128],
                                K.ps[bank][:, :].rearrange("p (j t) -> p j t", j=4), [R("ps", bank)], [R("KT", jt // 2)])
                    si = next_stg(K)
                    stg = K.stg[si]
                    S.op("sp", lambda e, stg=stg, sq=sq, jt=jt: e.dma_start(out=stg[:, :], in_=K.at_cv[sq, jt * 128:(jt + 1) * 128, :]), writes=[R("stg", si)], dma=True)
                    copy_op(K, "act" if jt % 2 else "dve", V[:, jt, :], stg[:, :], [R("stg", si)], [R("V", jt)])
                for h in range(16):
                    tiles = []
                    for r in range(4):
                        tiles.append((r, KT[:, h, r * 128:(r + 1) * 128], [R("KT", r // 2)], V[:, r, h * 128:(h + 1) * 128], [R("V", r)],
                                      "far" if r < 3 else "near"))
                    pc = 512 + (i // 2) * 128
                    tiles.append((4, KT[:, h, pc:pc + 128], [R("KT", 2)], V[:, 4 + i // 2, h * 128:(h + 1) * 128], [R("V", 4 + i // 2)], "near"))
                    attend(h, 64, qT[:, h, i * 64:(i + 1) * 64], tiles, i, oT[:, h, i * 64:(i + 1) * 64], [R("oT", h)])
        if stage < 3:
            continue
        for ob in range(D // 256):
            banks = [2 * (ob % 2), 2 * (ob % 2) + 1]
            dense_block(K, K.at_wo, ob, NCH, lambda kc: oT[:, kc, :], lambda kc: [R("oT", kc)], AG, banks)
            for o in range(2):
                resid_ln_accum(K, t0, AG, ob * 2 + o, K.ps[banks[o]][:, 0:AG], [R("ps", banks[o])])
        ln_finish(K, layer, 0, t0, AG, False)
    S.barrier()


HG = 256
RMS_EPS = 1e-6


def dense_block_tm64(K, w2d, ob, t0, banks):
    S = K.S
    wis = [load_w_tile(K, wtile(w2d, kt, ob)) for kt in range(2)]
    for c in range(4):
        o_ap = K.ps[banks[c // 2]][0:64, (c % 2) * 256:(c % 2) * 256 + 256]
        for kt in range(2):
            wbf = K.wbf[wis[kt]]
            for k in range(8):
                kc = kt * 8 + k
                first = (kt == 0 and k == 0)
                last = (kt == 1 and k == 7)
                lhs = K.xT[:, kc, t0 + c * 64:t0 + (c + 1) * 64]
                S.op("pe", lambda e, lhs=lhs, wbf=wbf, k=k, o_ap=o_ap, first=first, last=last:
                     e.matmul(o_ap, lhsT=lhs, rhs=wbf[:, k * 256:(k + 1) * 256], start=first, stop=last),
                     reads=[R("wbf", wis[kt])] + xr(t0 + c * 64, 64, [kc]), writes=[R("ps", banks[c // 2])], tick=(k == 7))


def hgrn_layer(K, layer):
    nc, S = K.nc, K.S
    S.barrier()
    bg = Arena(K, K.big_off, 65536)
    qhT = bg.get("qhT", [128, 16, HG], BF16)
    ktT = bg.get("ktT", [128, 16, HG], BF16)
    mT = bg.get("mT", [128, 16, HG], BF16)
    GsT = bg.get("GsT", [128, 16, HG], BF16)
    Vc = bg.get("Vc", [64, 4, 2048], BF16)
    Kc2 = [bg.get("Kc%d" % i, [64, 2048], BF16) for i in range(2)]
    Sbf = bg.get("Sbf", [128, 16, 128], BF16)
    ATm = [bg.get("ATm%d" % i, [64, 4, 64], BF16) for i in range(2)]
    omb = [bg.get("omb%d" % i, [64, 4, 128], BF16) for i in range(2)]
    sc = Arena(K, K.scr_off, K.scr_size)
    Sm = sc.get("Sm", [128, 16, 128], F32)
    ft = [K.lnT[0][:, 0:256], K.lnT[0][:, 256:512], K.lnT[1][:, 0:256], K.lnT[1][:, 256:512]]
    ebuf = [K.otile[0][:, 0:256], K.otile[1][:, 0:256]]
    sq = K.lnA[0:64, :]
    eL = sc.get("eL", [128, 16, 4], F32)
    ss = sc.get("ss", [64, 16], F32)
    ones64 = sc.get("ones64", [128, 64], F32)
    M64 = sc.get("M64", [64, 4, 64], F32)
    lbt = sc.get("lbt", [128, 16], F32)
    oml = sc.get("oml", [128, 16], F32)
    ngt = sc.get("ngt", [128, 1], F32)
    lg = sc.get("lg", [128, 4, 16], F32)
    den = sc.get("den", [128, 16], F32)
    epsr = sc.get("epsr", [64, 1], F32)
    w_in = K.hg_win
    C = [R("hgc")]

    S.op("pool", lambda e: e.memset(ones64[:, :], 1.0), writes=C)
    S.op("pool", lambda e: e.memset(epsr[:, :], RMS_EPS), writes=C)
    S.op("pool", lambda e: e.memset(M64[:, :, :], 1.0), writes=C)
    S.op("pool", lambda e: e.affine_select(out=M64[:, :, :], in_=M64[:, :, :], pattern=[[0, 4], [1, 64]], compare_op=ALU.is_ge,
                                            fill=0.0, base=0, channel_multiplier=-1), reads=C, writes=C)
    S.op("sp", lambda e: e.dma_start(out=ngt[:, :], in_=K.hg_ng), writes=C, dma=True)
    S.op("sp", lambda e: e.dma_start(out=lg[:, :, :], in_=K.hg_lb), writes=C, dma=True)
    S.op("act", lambda e: e.activation(out=lg[:, :, :], in_=lg[:, :, :], func=AF.Exp), reads=C, writes=C)
    tt(K, "dve", den[:, :], lg[:, 0, :], lg[:, 1, :], ALU.add, C, C)
    tt(K, "dve", den[:, :], den[:, :], lg[:, 2, :], ALU.add, C, C)
    tt(K, "dve", den[:, :], den[:, :], lg[:, 3, :], ALU.add, C, C)
    S.op("dve", lambda e: e.reciprocal(out=den[:, :], in_=den[:, :]), reads=C, writes=C)
    S.op("pool", lambda e: e.memset(lbt[:, :], 0.0), writes=C)
    for l in range(1, layer + 1):
        tt(K, "dve", lbt[:, :], lbt[:, :], lg[:, l, :], ALU.add, C, C)
    tt(K, "dve", lbt[:, :], lbt[:, :], den[:, :], ALU.mult, C, C)
    S.op("dve", lambda e: e.tensor_scalar(out=oml[:, :], in0=lbt[:, :], scalar1=-1.0, scalar2=1.0, op0=ALU.mult, op1=ALU.add), reads=C, writes=C)
    S.op("pool", lambda e: e.memset(Sm[:, :, :], 0.0), writes=[R("Sm", h) for h in range(16)])
    S.op("pool", lambda e: e.memset(Sbf[:, :, :], 0.0), writes=[R("Sbf", h) for h in range(16)])

    ntg = T // HG
    tgs = K.cfg.get("hgrn_tgs", list(range(ntg)))
    ucnt = [0]
    for a in tgs:
        t0 = a * HG
        is_s = a >= SEQ // HG
        for ob in range(8):
            banks = [2 * (ob % 2), 2 * (ob % 2) + 1]
            dense_block_tm64(K, w_in, 2 * 8 + ob, t0, banks)
            for b in range(2):
                src = K.ps[banks[b]][0:64, :].rearrange("p (c n) -> p c n", c=2)
                dst = Vc[:, 2 * b:2 * b + 2, ob * 256:(ob + 1) * 256]
                copy_op(K, "act", dst, src, [R("ps", banks[b])], [R("Vc")])
        for hp in range(8):
            banks = [2 * (hp % 2), 2 * (hp % 2) + 1]
            dense_block(K, w_in, 24 + hp, NCH, lambda kc: K.xT[:, kc, t0:t0 + HG], lambda kc: xr(t0, HG, [kc]), HG, banks)
            for o in range(2):
                h = hp * 2 + o
                tmp = K.lnT[o][:, 0:HG]
                psg = K.ps[banks[o]][:, 0:HG]
                S.op("act", lambda e, tmp=tmp, psg=psg: e.activation(out=tmp, in_=psg, func=AF.Sigmoid), reads=[R("ps", banks[o])], writes=[R("lnT", o)])
                tt(K, "dve", GsT[:, h, :], tmp, psg, ALU.mult, [R("lnT", o), R("ps", banks[o])], [R("GsT", h)])
        for hp in range(8):
            dense_block(K, w_in, 8 + hp, NCH, lambda kc: K.xT[:, kc, t0:t0 + HG], lambda kc: xr(t0, HG, [kc]), HG, [0, 1])
            for o in range(2):
                h = hp * 2 + o
                psf = K.ps[o][:, 0:HG]
                f_, omf, b_, en = ft[0], ft[1], ft[2], ft[3]
                eb = ebuf[o]
                FT = [R("lnT", 0), R("lnT", 1)]
                S.op("act", lambda e, psf=psf: e.activation(out=f_, in_=psf, func=AF.Sigmoid), reads=[R("ps", o)], writes=FT)
                S.op("dve", lambda e, h=h: e.tensor_scalar(out=f_, in0=f_, scalar1=oml[:, h:h + 1], scalar2=lbt[:, h:h + 1], op0=ALU.mult, op1=ALU.add),
                     reads=FT + C, writes=FT)
                S.op("dve", lambda e: e.tensor_scalar(out=omf, in0=f_, scalar1=-1.0, scalar2=1.0, op0=ALU.mult, op1=ALU.add), reads=FT, writes=FT)
                S.op("act", lambda e: e.activation(out=f_, in_=f_, func=AF.Ln), reads=FT, writes=FT)
                for c in range(4):
                    S.op("dve", lambda e, c=c: e.tensor_tensor_scan(out=b_[:, c * 64:(c + 1) * 64], data0=ones64[:, :], data1=f_[:, c * 64:(c + 1) * 64],
                                                                    initial=0.0, op0=ALU.mult, op1=ALU.add), reads=FT + C, writes=FT)
                S.op("act", lambda e, eb=eb: e.activation(out=eb, in_=b_, func=AF.Exp), reads=FT, writes=[R("otile", o)])
                S.op("act", lambda e: e.activation(out=en, in_=b_, func=AF.Exp, scale=-1.0), reads=FT, writes=FT)
                tt(K, "dve", ktT[:, h, :], omf, en, ALU.mult, FT, [R("ktT", h)])
                copy_op(K, "pool", eL[:, h, :], ebuf[o][:, 63:HG:64], [R("otile", o)], [R("eL")])
            dense_block(K, w_in, hp, NCH, lambda kc: K.xT[:, kc, t0:t0 + HG], lambda kc: xr(t0, HG, [kc]), HG, [2, 3])
            for o in range(2):
                h = hp * 2 + o
                tt(K, "dve", qhT[:, h, :], K.ps[2 + o][:, 0:HG], ebuf[o], ALU.mult, [R("ps", 2 + o), R("otile", o)], [R("qhT", h)])
        for c in range(4):
            if is_s:
                sq_i = (a - SEQ // HG) * 4 + c
                S.op("sp", lambda e, sq_i=sq_i: e.dma_start(out=Sm[:, :, :], in_=K.hg_s0[sq_i].rearrange("h k v -> k h v")),
                     writes=[R("Sm", h) for h in range(16)], dma=True)
                copy_op(K, "act", Sbf[:, :, :], Sm[:, :, :], [R("Sm", h) for h in range(16)], [R("Sbf", h) for h in range(16)])
            csl = slice(c * 64, (c + 1) * 64)
            Kc = Kc2[c % 2]
            KR = [R("Kc", c % 2)]
            for hg in range(4):
                tp = K.ps[7][:, :].bitcast(BF16)
                for hh in range(4):
                    h = hg * 4 + hh
                    i_ap = ktT[:, h, c * 64:(c + 1) * 64]
                    o_ap = tp[0:64, hh * 128:(hh + 1) * 128]
                    S.op("pe", lambda e, i_ap=i_ap, o_ap=o_ap: e.transpose(out=o_ap, in_=i_ap, identity=K.ident_b[:, :]),
                         reads=[R("ktT", h), R("ident_b")], writes=[R("ps", 7)], tick=(hh == 3))
                copy_op(K, evac_engine(K), Kc[:, hg * 512:(hg + 1) * 512], tp[0:64, 0:512], [R("ps", 7)], KR)
            for hg in range(4):
                u = ucnt[0] % 2
                ucnt[0] += 1
                at, ob_ = ATm[u], omb[u]
                for hh in range(4):
                    h = hg * 4 + hh
                    o_ap = K.ps[4][0:64, hh * 64:(hh + 1) * 64]
                    l_ap, r_ap = ktT[:, h, csl], qhT[:, h, csl]
                    S.op("pe", lambda e, o_ap=o_ap, l_ap=l_ap, r_ap=r_ap: e.matmul(o_ap, lhsT=l_ap, rhs=r_ap, start=True, stop=True),
                         reads=[R("ktT", h), R("qhT", h)], writes=[R("ps", 4)], tick=(hh == 3))
                tt(K, "dve", at[:, :, :], K.ps[4][0:64, 0:256].rearrange("p (h t) -> p h t", h=4), M64[:, :, :], ALU.mult, [R("ps", 4)] + C, [R("ATm", u)])
                for hh in range(4):
                    h = hg * 4 + hh
                    o_ap = K.ps[5][0:64, hh * 128:(hh + 1) * 128]
                    v_ap = Vc[:, c, h * 128:(h + 1) * 128]
                    a_ap = at[:, hh, :]
                    q_ap = qhT[:, h, csl]
                    s_ap = Sbf[:, h, :]
                    S.op("pe", lambda e, o_ap=o_ap, a_ap=a_ap, v_ap=v_ap: e.matmul(o_ap, lhsT=a_ap, rhs=v_ap, start=True, stop=False),
                         reads=[R("ATm", u), R("Vc")], writes=[R("ps", 5)], tick=False)
                    S.op("pe", lambda e, o_ap=o_ap, q_ap=q_ap, s_ap=s_ap: e.matmul(o_ap, lhsT=q_ap, rhs=s_ap, start=False, stop=True),
                         reads=[R("qhT", h), R("Sbf", h)], writes=[R("ps", 5)], tick=(hh == 3))
                for hh in range(4):
                    h = hg * 4 + hh
                    o_ap = K.ps[6][:, hh * 128:(hh + 1) * 128]
                    k_ap = Kc[:, h * 128:(h + 1) * 128]
                    v_ap = Vc[:, c, h * 128:(h + 1) * 128]
                    S.op("pe", lambda e, o_ap=o_ap, k_ap=k_ap, v_ap=v_ap: e.matmul(o_ap, lhsT=k_ap, rhs=v_ap, start=True, stop=True),
                         reads=KR + [R("Vc")], writes=[R("ps", 6)], tick=(hh == 3))
                for hh in range(4):
                    h = hg * 4 + hh
                    sm = Sm[:, h, :]
                    e_ap = eL[:, h, c:c + 1]
                    S.op("dve", lambda e, sm=sm, e_ap=e_ap: e.tensor_scalar(out=sm, in0=sm, scalar1=e_ap, scalar2=None, op0=ALU.mult),
                         reads=[R("Sm", h), R("eL")], writes=[R("Sm", h)])
                    p_ap = K.ps[6][:, hh * 128:(hh + 1) * 128]
                    S.op("dve", lambda e, sm=sm, e_ap=e_ap, p_ap=p_ap: e.scalar_tensor_tensor(out=sm, in0=p_ap, scalar=e_ap, in1=sm, op0=ALU.mult, op1=ALU.add),
                         reads=[R("Sm", h), R("eL"), R("ps", 6)], writes=[R("Sm", h)])
                    copy_op(K, "act", Sbf[:, h, :], sm, [R("Sm", h)], [R("Sbf", h)])
                S.op("act", lambda e: e.activation(out=sq[:, :], in_=K.ps[5][0:64, :], func=AF.Square), reads=[R("ps", 5)], writes=[R("lnA")])
                ssg = ss[:, hg * 4:(hg + 1) * 4]
                S.op("dve", lambda e, ssg=ssg: e.tensor_reduce(out=ssg, in_=sq[:, :].rearrange("p (h v) -> p h v", h=4), axis=AX.X, op=ALU.add),
                     reads=[R("lnA")], writes=[R("ss")])
                S.op("act", lambda e, ssg=ssg: e.activation(out=ssg, in_=ssg, func=AF.Sqrt, bias=epsr[:, 0:1], scale=1.0 / 128), reads=[R("ss")] + C, writes=[R("ss")])
                S.op("dve", lambda e, ssg=ssg: e.reciprocal(out=ssg, in_=ssg), reads=[R("ss")], writes=[R("ss")])
                for hh in range(4):
                    h = hg * 4 + hh
                    p_ap = K.ps[5][0:64, hh * 128:(hh + 1) * 128]
                    r_ap = ss[:, h:h + 1]
                    d_ap = ob_[:, hh, :]
                    S.op("act", lambda e, p_ap=p_ap, r_ap=r_ap, d_ap=d_ap: e.activation(out=d_ap, in_=p_ap, func=AF.Copy, scale=r_ap),
                         reads=[R("ps", 5), R("ss")], writes=[R("omb", u)])
                tp = K.ps[7][:, :].bitcast(BF16)
                for hh in range(4):
                    i_ap = ob_[:, hh, :]
                    o_ap = tp[:, hh * 64:(hh + 1) * 64]
                    S.op("pe", lambda e, i_ap=i_ap, o_ap=o_ap: e.transpose(out=o_ap, in_=i_ap, identity=K.ident_b[0:64, 0:64]),
                         reads=[R("omb", u), R("ident_b")], writes=[R("ps", 7)], tick=(hh == 3))
                m_dst = mT[:, hg * 4:(hg + 1) * 4, csl]
                g_src = GsT[:, hg * 4:(hg + 1) * 4, csl]
                t_src = tp[:, 0:256].rearrange("p (h t) -> p h t", h=4)
                S.op("dve", lambda e, m_dst=m_dst, g_src=g_src, t_src=t_src: e.scalar_tensor_tensor(out=m_dst, in0=t_src, scalar=ngt[:, 0:1], in1=g_src, op0=ALU.mult, op1=ALU.mult),
                     reads=[R("ps", 7)] + C + [R("GsT", hg * 4 + hh) for hh in range(4)], writes=[R("mT", hg * 4 + hh) for hh in range(4)])
            last_prompt = (not is_s) and (a == SEQ // HG - 1) and c == 3
            if is_s or last_prompt:
                dst = K.hg_ss[(a - SEQ // HG) * 4 + c] if is_s else K.hg_sp
                S.op("sp", lambda e, dst=dst: e.dma_start(out=dst.rearrange("h k v -> k h v"), in_=Sm[:, :, :]),
                     reads=[R("Sm", h) for h in range(16)], writes=[R("hgout")], dma=True)
        for ob in range(D // 256):
            banks = [2 * (ob % 2), 2 * (ob % 2) + 1]
            dense_block(K, K.hg_wo, ob, NCH, lambda kc: mT[:, kc, :], lambda kc: [R("mT", kc)], HG, banks)
            for o in range(2):
                resid_ln_accum(K, t0, HG, ob * 2 + o, K.ps[banks[o]][:, 0:HG], [R("ps", banks[o])])
        ln_finish(K, layer, 0, t0, HG, False)
    S.barrier()


def s5_host_layout(inp, b):
    a_re, a_im, ldt = inp["ssm_a_re"], inp["ssm_a_im"], inp["ssm_log_dt"]
    na = a_re.shape[0]
    out = {}
    are_t = a_re.transpose(0, 2, 1)
    aim_t = a_im.transpose(0, 2, 1)
    l1 = np.stack([np.concatenate([are_t, are_t], 1), np.concatenate([aim_t, aim_t], 1),
                   np.broadcast_to(ldt[:, None, :], (na, 128, 128))], 1)
    out["s5_l1"] = l1
    rows = np.stack([np.repeat(a_re, 16, axis=1), np.repeat(a_im, 16, axis=1),
                     np.broadcast_to(np.repeat(ldt, 16, axis=1)[:, :, None], (na, D, 64))], 2)
    out["s5_rows"] = rows.reshape(na, NCH, 128, 3, 64)
    bre = inp["ssm_b_re"].transpose(0, 1, 3, 2).reshape(na, D, 64)
    bim = inp["ssm_b_im"].transpose(0, 1, 3, 2).reshape(na, D, 64)
    out["s5_bT"] = np.stack([bre, bim], 2).reshape(na, NCH, 128, 2, 64)
    cre = inp["ssm_c_re"].transpose(0, 1, 3, 2)
    cim = inp["ssm_c_im"].transpose(0, 1, 3, 2)
    cc = np.concatenate([cre, cim], 2)
    out["s5_cT"] = cc.reshape(na, NCH, 8, 128, 16).transpose(0, 1, 3, 2, 4)
    out["s5_d"] = inp["ssm_d"].reshape(na, NCH, 128).transpose(0, 2, 1)
    h0 = np.concatenate([inp["state_ssm_re"][:, 8 * b:8 * b + 8], inp["state_ssm_im"][:, 8 * b:8 * b + 8]], -1)
    out["s5_h0"] = h0
    out["s5_wo"] = inp["ssm_w_out"]
    out["s5_wg"] = inp["ssm_w_gate"]
    return {k: np.ascontiguousarray(v, dtype=np.float32) for k, v in out.items()}


def attn_host_layout(inp, b):
    tab = inp["attn_rel_bias"][0]
    kp = np.arange(128)[:, None, None]
    r = np.array([-1, 0])[None, :, None]
    q = np.arange(128)[None, None, :]
    idx = np.clip(q - (128 * r + kp), -128, 128) + 128
    toep = tab[:, idx].transpose(1, 0, 2, 3)
    far = np.broadcast_to(tab[None, :, 256], (128, 16))
    out = {"at_wqkv": inp["attn_w_qkv"][0], "at_wo": inp["attn_w_o"][0], "at_toep": toep, "at_far": far,
           "at_ck": inp["cache_attn_k"][0, 8 * b:8 * b + 8].reshape(NS, 512, D),
           "at_cv": inp["cache_attn_v"][0, 8 * b:8 * b + 8].reshape(NS, 512, D)}
    return {k: np.ascontiguousarray(v, dtype=np.float32) for k, v in out.items()}


def hgrn_host_layout(inp, b):
    out = {"hg_win": inp["hgrn_w_in"][0], "hg_wo": inp["hgrn_w_o"][0],
           "hg_lb": inp["hgrn_lb_logits"].reshape(4, NCH, 128).transpose(2, 0, 1),
           "hg_ng": inp["hgrn_norm_g"][0].reshape(128, 1),
           "hg_s0": inp["state_hgrn"][0, 8 * b:8 * b + 8]}
    return {k: np.ascontiguousarray(v, dtype=np.float32) for k, v in out.items()}


def make_core_inputs(inp, c):
    b = c % 4
    x_in = np.concatenate([inp["x_prompt"][b], inp["x_sample"][8 * b:8 * b + 8].reshape(NS * DS, D)], axis=0)
    lnp = np.stack([inp["ln_mix_g"], inp["ln_mix_b"], inp["ln_ffn_g"], inp["ln_ffn_b"]], 0)
    lnp = lnp.reshape(4, DEPTH, NCH, 128).transpose(3, 0, 1, 2).reshape(128, 4 * DEPTH * NCH)
    m = {
        "x_in": np.ascontiguousarray(x_in, dtype=np.float32),
        "ffn_w1": np.ascontiguousarray(inp["ffn_w1"], dtype=np.float32),
        "ffn_w2": np.ascontiguousarray(inp["ffn_w2"], dtype=np.float32),
        "lnp": np.ascontiguousarray(lnp, dtype=np.float32),
    }
    if "ssm_a_re" in inp:
        m.update(s5_host_layout(inp, b))
    if "attn_w_qkv" in inp:
        m.update(attn_host_layout(inp, b))
    if "hgrn_w_in" in inp:
        m.update(hgrn_host_layout(inp, b))
    return m


_NC_CACHE = {}


def kernel(**inp):
    inp = {k: np.asarray(v) for k, v in inp.items()}
    if "nc" not in _NC_CACHE:
        _NC_CACHE["nc"] = build({})
    nc = _NC_CACHE["nc"]
    maps4 = [make_core_inputs(inp, c) for c in range(4)]
    in_maps = [maps4[c % 4] for c in range(8)]
    res = run_bass_kernel_spmd(nc, in_maps, core_ids=list(range(8)))
    r = res.results
    f32 = np.float32
    yp = np.stack([r[c]["y_out"][:SEQ] for c in range(4)]).astype(f32)
    ys = np.concatenate([r[c]["y_out"][SEQ:].reshape(NS, DS, D) for c in range(4)]).astype(f32)
    ssm_p = np.stack([r[c]["ssm_p"] for c in range(4)], 1)
    ssm_s = np.concatenate([r[c]["ssm_s"] for c in range(4)], 1)
    kp = np.stack([r[c]["at_kp"] for c in range(4)]).reshape(1, 4, 512, 16, 128)
    vp = np.stack([r[c]["at_vp"] for c in range(4)]).reshape(1, 4, 512, 16, 128)
    ks = np.concatenate([r[c]["at_ks"].reshape(NS, DS, 16, 128) for c in range(4)])[None]
    vs = np.concatenate([r[c]["at_vs"].reshape(NS, DS, 16, 128) for c in range(4)])[None]
    hp = np.stack([r[c]["hg_sp"] for c in range(4)])[None]
    hs = np.concatenate([r[c]["hg_ss"] for c in range(4)])[None]
    out = (yp, ys, ssm_p[..., :64], ssm_p[..., 64:], kp, vp, hp, ssm_s[..., :64], ssm_s[..., 64:], ks, vs, hs)
    return tuple(np.ascontiguousarray(o, dtype=f32) for o in out)
```

```python
import numpy as np
import concourse.bass as bass
import concourse.mybir as mybir
from concourse.bass_utils import run_bass_kernel_spmd

F32 = mybir.dt.float32
BF16 = mybir.dt.bfloat16
AF = mybir.ActivationFunctionType
ALU = mybir.AluOpType
AX = mybir.AxisListType

D = 2048
NCH = 16
SEQ = 2048
NS = 8
DS = 64
T = SEQ + NS * DS
G = 512
NTG = T // G
DFF = 8192
DEPTH = 4
ALPHA = (2.0 * DEPTH) ** 0.25
LN_EPS = 1e-5

SAME_ENGINE_SYNC = True


class Sched:
    ENGS = ("pe", "act", "dve", "pool", "sp")

    def __init__(self, nc, ndma=10):
        self.nc = nc
        self.q = {e: [] for e in self.ENGS}
        self.ticks = {e: 0 for e in self.ENGS}
        self.pending = {e: False for e in self.ENGS}
        self.last_w = {}
        self.readers = {}
        self.waited = {e: {} for e in self.ENGS}
        self.ndma = ndma
        self.dma_val = {}
        self.dma_rr = {e: 0 for e in self.ENGS}
        self.all_dma = []

    def _need(self, eng, tk, waits):
        if tk is None:
            return
        if tk[0] == "e":
            _, e2, n = tk
            if e2 == eng and (eng == "pe" or not SAME_ENGINE_SYNC):
                return
            key = ("e", e2)
            val = n
        else:
            _, qn, idx, val = tk
            key = ("d", qn, idx)
        if self.waited[eng].get(key, 0) >= val:
            return
        cur = waits.get(key, 0)
        if val > cur:
            waits[key] = val

    def op(self, eng, fn, reads=(), writes=(), tick=True, dma=False):
        waits = {}
        for r in reads:
            self._need(eng, self.last_w.get(r), waits)
        for w in writes:
            self._need(eng, self.last_w.get(w), waits)
            for tk in self.readers.get(w, ()):
                self._need(eng, tk, waits)
        if dma:
            idx = self.dma_rr[eng]
            self.dma_rr[eng] = (idx + 1) % self.ndma
            prev = self.dma_val.get((eng, idx), 0)
            if prev:
                self._need(eng, ("d", eng, idx, prev), waits)
            val = prev + 16
            self.dma_val[(eng, idx)] = val
            tk = ("d", eng, idx, val)
            inc = ("d", eng, idx)
        else:
            if tick:
                self.ticks[eng] += 1
                tk = ("e", eng, self.ticks[eng])
                inc = ("e", eng)
                self.pending[eng] = False
            else:
                tk = ("e", eng, self.ticks[eng] + 1)
                inc = None
                self.pending[eng] = True
        for key, val in waits.items():
            self.waited[eng][key] = val
        self.q[eng].append((list(waits.items()), fn, inc))
        for w in writes:
            self.last_w[w] = tk
            self.readers[w] = []
        for r in reads:
            lst = self.readers.setdefault(r, [])
            if tk[0] == "e":
                lst[:] = [x for x in lst if not (x[0] == "e" and x[1] == tk[1])]
            lst.append(tk)
        return tk

    def barrier(self):
        for e in self.ENGS:
            waits = {}
            for e2 in self.ENGS:
                if e2 != e and self.ticks[e2] > 0:
                    self._need(e, ("e", e2, self.ticks[e2]), waits)
            for (qn, idx), val in self.dma_val.items():
                self._need(e, ("d", qn, idx, val), waits)
            for key, val in waits.items():
                self.waited[e][key] = val
            self.q[e].append((list(waits.items()), None, None))

    def finish(self, eng="sp"):
        waits = {}
        for (qn, idx), val in self.dma_val.items():
            self._need(eng, ("d", qn, idx, val), waits)
        self.q[eng].append((list(waits.items()), None, None))

    def simulate(self):
        sem = {}
        pc = {e: 0 for e in self.ENGS}
        progress = True
        while progress:
            progress = False
            for e in self.ENGS:
                while pc[e] < len(self.q[e]):
                    waits, fn, inc = self.q[e][pc[e]]
                    ok = all(sem.get(key, 0) >= val for key, val in waits)
                    if not ok:
                        break
                    if inc is not None:
                        k = ("e", inc[1]) if inc[0] == "e" else ("d", inc[1], inc[2])
                        sem[k] = sem.get(k, 0) + (1 if inc[0] == "e" else 16)
                    pc[e] += 1
                    progress = True
        stuck = {e: (pc[e], len(self.q[e])) for e in self.ENGS if pc[e] < len(self.q[e])}
        if stuck:
            msg = []
            for e, (p, n) in stuck.items():
                waits, fn, inc = self.q[e][p]
                msg.append("%s stuck at %d/%d waits=%s have=%s" % (e, p, n, waits, [sem.get(k, 0) for k, v in waits]))
            raise RuntimeError("DEADLOCK: " + " | ".join(msg))

    def emit(self):
        self.simulate()
        nc = self.nc
        for e in self.ENGS:
            assert not self.pending[e], e
            assert self.ticks[e] < 60000, (e, self.ticks[e])
        import contextlib
        with contextlib.ExitStack() as st:
            esem = {e: st.enter_context(nc.semaphore("se_" + e)) for e in self.ENGS}
            dsem = {}
            for (qn, idx) in self.dma_val:
                dsem[(qn, idx)] = st.enter_context(nc.semaphore("sd_%s_%d" % (qn, idx)))
            block = st.enter_context(nc.Block())

            def run(eng_name):
                def body(eng):
                    for waits, fn, inc in self.q[eng_name]:
                        for key, val in waits:
                            s = esem[key[1]] if key[0] == "e" else dsem[(key[1], key[2])]
                            eng.wait_ge(s, val)
                        if fn is None:
                            continue
                        ins = fn(eng)
                        if inc is not None:
                            if inc[0] == "e":
                                ins.then_inc(esem[inc[1]], 1)
                            else:
                                ins.then_inc(dsem[(inc[1], inc[2])], 16)
                return body

            block.tensor(run("pe"))
            block.scalar(run("act"))
            block.vector(run("dve"))
            block.gpsimd(run("pool"))
            block.sync(run("sp"))


class Ctx:
    pass


def R(name, *idx):
    return (name,) + idx


def xr(t0, W, cs=None):
    cs = range(NCH) if cs is None else cs
    return [R("xT", b, c) for b in range(t0 // 256, (t0 + W + 255) // 256) for c in cs]


def build(cfg):
    nc = bass.Bass("TRN2", target_bir_lowering=False)
    S = Sched(nc)
    K = Ctx()
    K.nc, K.S = nc, S
    K.cfg = cfg
    depth = cfg.get("depth", DEPTH)
    kinds = cfg.get("kinds", [l % 3 for l in range(depth)])
    K.na = max(1, sum(1 for k in kinds if k == 0))

    def din(name, shape):
        return nc.dram_tensor(name, list(shape), F32, kind="ExternalInput").ap()

    def dout(name, shape):
        return nc.dram_tensor(name, list(shape), F32, kind="ExternalOutput").ap()

    K.x_in = din("x_in", [T, D])
    K.ffn_w1 = din("ffn_w1", [cfg.get("wdepth", DEPTH), D, DFF])
    K.ffn_w2 = din("ffn_w2", [cfg.get("wdepth", DEPTH), DFF, D])
    K.lnp = din("lnp", [128, 4 * DEPTH * NCH])
    K.y_out = dout("y_out", [T, D])
    if 0 in kinds:
        na = K.na
        K.s5_l1 = din("s5_l1", [na, 3, 128, 128])
        K.s5_rows = din("s5_rows", [na, NCH, 128, 3, 64])
        K.s5_bT = din("s5_bT", [na, NCH, 128, 2, 64])
        K.s5_cT = din("s5_cT", [na, NCH, 128, 8, 16])
        K.s5_d = din("s5_d", [na, 128, NCH])
        K.s5_h0 = din("s5_h0", [na, NS, 128, 128])
        K.s5_wo = din("s5_wo", [na, D, D])
        K.s5_wg = din("s5_wg", [na, D, D])
        K.ssm_p = dout("ssm_p", [na, 128, 128])
        K.ssm_s = dout("ssm_s", [na, NS, 128, 128])

    if 1 in kinds:
        K.at_wqkv = din("at_wqkv", [D, 3 * D])
        K.at_wo = din("at_wo", [D, D])
        K.at_toep = din("at_toep", [128, 16, 2, 128])
        K.at_far = din("at_far", [128, 16])
        K.at_ck = din("at_ck", [NS, 512, D])
        K.at_cv = din("at_cv", [NS, 512, D])
        K.at_kp = dout("at_kp", [512, D])
        K.at_vp = dout("at_vp", [512, D])
        K.at_ks = dout("at_ks", [NS * DS, D])
        K.at_vs = dout("at_vs", [NS * DS, D])
    if 2 in kinds:
        K.hg_win = din("hg_win", [D, 4 * D])
        K.hg_wo = din("hg_wo", [D, D])
        K.hg_lb = din("hg_lb", [128, 4, NCH])
        K.hg_ng = din("hg_ng", [128, 1])
        K.hg_s0 = din("hg_s0", [NS, 16, 128, 128])
        K.hg_sp = dout("hg_sp", [16, 128, 128])
        K.hg_ss = dout("hg_ss", [NS, 16, 128, 128])
    sb = nc.alloc_sbuf_tensor
    K.xT = sb("xT", [128, NCH, T], BF16)
    big0, big1 = nc.bump_sbuf(65536)
    K.big_off = big0
    K.big = nc.alloc_sbuf_tensor_at("big", [128, 32768], BF16, offset=big0)
    K.stg = [sb("stg%d" % i, [128, 2048], F32) for i in range(3)]
    K.wbf = [sb("wbf%d" % i, [128, 2048], BF16) for i in range(2)]
    K.lnp_sb = sb("lnp_sb", [128, 4 * DEPTH * NCH], F32)
    K.ident_f = sb("ident_f", [128, 128], F32)
    K.ident_b = sb("ident_b", [128, 128], BF16)
    K.ones_b = sb("ones_b", [128, 128], BF16)
    K.zsq = [sb("zsq%d" % i, [128, G], BF16) for i in range(2)]
    K.lnA = sb("lnA", [128, G], F32)
    K.lnB = sb("lnB", [128, G], F32)
    K.lnT = [sb("lnT%d" % i, [128, G], F32) for i in range(2)]
    K.otile = [sb("otile%d" % i, [128, 512], F32) for i in range(2)]
    K.eps_t = sb("eps_t", [128, 1], F32)
    sc0, sc1 = nc.bump_sbuf(14336)
    K.scr_off = sc0
    K.scr_size = 14336
    K.ps = [nc.alloc_psum_tensor("ps%d" % i, [128, 512], F32) for i in range(8)]

    K.stg_rr = 0
    K.wbf_rr = 0
    K.ev_rr = 0
    K.uid = 0

    setup_consts(K)
    load_input(K)
    ia = 0
    for li in range(depth):
        kind = kinds[li]
        layer = li + cfg.get("layer0", 0)
        if kind == 0:
            s5_layer(K, layer, ia)
            ia += 1
        elif kind == 1:
            attn_layer(K, layer)
        elif kind == 2:
            hgrn_layer(K, layer)
        else:
            mixer_none(K, layer)
        if cfg.get("ffn", True):
            ffn(K, layer, last=(li == depth - 1) and not cfg.get("dbg_xT"))
    if cfg.get("dbg_xT"):
        dbg = nc.dram_tensor("dbg", [128, NCH * T], BF16, kind="ExternalOutput").ap()
        S.op("sp", lambda e: e.dma_start(out=dbg, in_=K.xT[:, :, :].rearrange("p c t -> p (c t)")),
             reads=xr(0, T), dma=True)
    if cfg.get("dbg_big"):
        dbg2 = nc.dram_tensor("dbg2", [128, 32768], BF16, kind="ExternalOutput").ap()
        S.op("sp", lambda e: e.dma_start(out=dbg2, in_=K.big[:, :]),
             reads=[R("hT", c) for c in range(64)] + [R("yT", c) for c in range(16)], dma=True)
    S.finish("sp")
    S.emit()
    return nc


def alloc_at(K, name, shape, dtype, off):
    K.uid += 1
    return K.nc.alloc_sbuf_tensor_at("%s_%d" % (name, K.uid), list(shape), dtype, offset=off)


class Arena:
    def __init__(self, K, base, size):
        self.K, self.base, self.size, self.cur = K, base, size, 0

    def get(self, name, shape, dtype):
        n = 1
        for d in shape[1:]:
            n *= d
        nbytes = n * (4 if dtype == F32 else 2)
        nbytes = (nbytes + 31) // 32 * 32
        assert self.cur + nbytes <= self.size, (name, self.cur, nbytes, self.size)
        t = alloc_at(self.K, name, shape, dtype, self.base + self.cur)
        self.cur += nbytes
        return t


def setup_consts(K):
    nc, S = K.nc, K.S
    S.op("pool", lambda e: e.memset(K.ident_f[:, :], 0.0), writes=[R("ident_f")])
    S.op("pool", lambda e: e.affine_select(out=K.ident_f[:, :], in_=K.ident_f[:, :], pattern=[[1, 128]],
                                            compare_op=ALU.not_equal, fill=1.0, base=0, channel_multiplier=-1),
         reads=[R("ident_f")], writes=[R("ident_f")])
    S.op("pool", lambda e: e.tensor_copy(out=K.ident_b[:, :], in_=K.ident_f[:, :]), reads=[R("ident_f")], writes=[R("ident_b")])
    S.op("pool", lambda e: e.memset(K.ones_b[:, :], 1.0), writes=[R("ones_b")])
    S.op("pool", lambda e: e.memset(K.eps_t[:, :], LN_EPS), writes=[R("eps_t")])
    S.op("sp", lambda e: e.dma_start(out=K.lnp_sb[:, :], in_=K.lnp), writes=[R("lnp")], dma=True)


def next_stg(K):
    i = K.stg_rr
    K.stg_rr = (i + 1) % len(K.stg)
    return i


def next_wbf(K):
    i = K.wbf_rr
    K.wbf_rr = (i + 1) % len(K.wbf)
    return i


def evac_engine(K):
    K.ev_rr ^= 1
    return "act" if K.ev_rr else "dve"


def copy_op(K, eng, out, in_, reads, writes):
    if eng == "act":
        return K.S.op("act", lambda e: e.activation(out=out, in_=in_, func=AF.Copy), reads=reads, writes=writes)
    return K.S.op(eng, lambda e: e.tensor_copy(out=out, in_=in_), reads=reads, writes=writes)


def load_input(K):
    S = K.S
    for tt in range(T // 128):
        si = next_stg(K)
        stg = K.stg[si]
        S.op("sp", lambda e, stg=stg, tt=tt: e.dma_start(out=stg[:, :], in_=K.x_in[tt * 128:(tt + 1) * 128, :]),
             writes=[R("stg", si)], dma=True)
        for cb in range(4):
            bank = 4 + (tt * 4 + cb) % 4
            ps = K.ps[bank]
            for j in range(4):
                c = cb * 4 + j
                S.op("pe", lambda e, ps=ps, stg=stg, c=c, j=j: e.transpose(out=ps[:, j * 128:(j + 1) * 128],
                                                                       in_=stg[:, c * 128:(c + 1) * 128],
                                                                       identity=K.ident_f[:, :]),
                     reads=[R("stg", si), R("ident_f")], writes=[R("ps", bank)], tick=(j == 3))
            dst = K.xT[:, cb * 4:(cb + 1) * 4, tt * 128:(tt + 1) * 128]
            src = ps[:, :].rearrange("p (j t) -> p j t", j=4)
            copy_op(K, evac_engine(K), dst, src, [R("ps", bank)], xr(tt * 128, 128, range(cb * 4, cb * 4 + 4)))


def load_w_tile(K, dram_ap):
    S = K.S
    si = next_stg(K)
    wi = next_wbf(K)
    stg = K.stg[si]
    wbf = K.wbf[wi]
    S.op("sp", lambda e: e.dma_start(out=stg[:, :].rearrange("p (k n) -> p k n", k=8), in_=dram_ap), writes=[R("stg", si)], dma=True)
    K.cast_rr = getattr(K, "cast_rr", 0) + 1
    copy_op(K, "act" if K.cast_rr % 2 else "dve", wbf[:, :], stg[:, :], [R("stg", si)], [R("wbf", wi)])
    return wi


def wtile(w2d, kt, ob):
    return w2d[kt * 1024:(kt + 1) * 1024, ob * 256:(ob + 1) * 256].rearrange("(k p) n -> p k n", p=128)


def dense_block(K, w2d, ob, KC, rhs_fn, rhs_res, ncols, banks):
    S = K.S
    nkt = KC // 8
    for kt in range(nkt):
        wi = load_w_tile(K, wtile(w2d, kt, ob))
        wbf = K.wbf[wi]
        for o in range(2):
            ps = K.ps[banks[o]]
            for k in range(8):
                kc = kt * 8 + k
                first = (kt == 0 and k == 0)
                last = (kt == nkt - 1 and k == 7)
                rhs_ap = rhs_fn(kc)
                S.op("pe", lambda e, ps=ps, wbf=wbf, k=k, o=o, rhs_ap=rhs_ap, first=first, last=last:
                     e.matmul(ps[:, 0:ncols], lhsT=wbf[:, k * 256 + o * 128:k * 256 + (o + 1) * 128], rhs=rhs_ap,
                              start=first, stop=last),
                     reads=[R("wbf", wi)] + rhs_res(kc), writes=[R("ps", banks[o])], tick=(last or (o == 1 and k == 7)))


def ln_accum(K, t0, W, c):
    S = K.S
    zc = K.xT[:, c, t0:t0 + W]
    zq = K.zsq[c % 2]
    S.op("act", lambda e: e.activation(out=zq[:, 0:W], in_=zc, func=AF.Square), reads=xr(t0, W, [c]), writes=[R("zsq", c % 2)])
    S.op("pe", lambda e: e.matmul(K.ps[6][:, 0:W], lhsT=K.ones_b[:, :], rhs=zc, start=(c == 0), stop=(c == NCH - 1)),
         reads=xr(t0, W, [c]) + [R("ones_b")], writes=[R("ps", 6)], tick=(c == NCH - 1))
    S.op("pe", lambda e: e.matmul(K.ps[7][:, 0:W], lhsT=K.ones_b[:, :], rhs=zq[:, 0:W], start=(c == 0), stop=(c == NCH - 1)),
         reads=[R("zsq", c % 2), R("ones_b")], writes=[R("ps", 7)], tick=True)


def resid_ln_accum(K, t0, W, c, m_ap, m_res):
    S = K.S
    zc = K.xT[:, c, t0:t0 + W]
    S.op("dve", lambda e: e.scalar_tensor_tensor(out=zc, in0=zc, scalar=ALPHA, in1=m_ap, op0=ALU.mult, op1=ALU.add),
         reads=xr(t0, W, [c]) + m_res, writes=xr(t0, W, [c]))
    ln_accum(K, t0, W, c)


def ln_finish(K, layer, kind, t0, W, last):
    S = K.S
    inv = 1.0 / D
    A, B = K.lnA[:, 0:W], K.lnB[:, 0:W]
    S.op("dve", lambda e: e.tensor_scalar(out=A, in0=K.ps[6][:, 0:W], scalar1=inv, scalar2=None, op0=ALU.mult),
         reads=[R("ps", 6)], writes=[R("lnA")])
    S.op("dve", lambda e: e.tensor_tensor(out=B, in0=A, in1=A, op=ALU.mult), reads=[R("lnA")], writes=[R("lnB")])
    S.op("dve", lambda e: e.scalar_tensor_tensor(out=B, in0=K.ps[7][:, 0:W], scalar=inv, in1=B, op0=ALU.mult, op1=ALU.subtract),
         reads=[R("ps", 7), R("lnB")], writes=[R("lnB")])
    S.op("act", lambda e: e.activation(out=B, in_=B, func=AF.Sqrt, bias=K.eps_t[:, 0:1], scale=1.0),
         reads=[R("lnB"), R("eps_t")], writes=[R("lnB")])
    S.op("dve", lambda e: e.reciprocal(out=B, in_=B), reads=[R("lnB")], writes=[R("lnB")])
    S.op("dve", lambda e: e.tensor_tensor(out=A, in0=A, in1=B, op=ALU.mult), reads=[R("lnA"), R("lnB")], writes=[R("lnA")])
    gi = (2 * kind) * DEPTH * NCH + layer * NCH
    bi = (2 * kind + 1) * DEPTH * NCH + layer * NCH
    for c in range(NCH):
        tmp = K.lnT[c % 2][:, 0:W]
        tr = R("lnT", c % 2)
        zc = K.xT[:, c, t0:t0 + W]
        S.op("dve", lambda e, tmp=tmp, zc=zc: e.tensor_tensor(out=tmp, in0=zc, in1=B, op=ALU.mult),
             reads=xr(t0, W, [c]) + [R("lnB")], writes=[tr])
        S.op("dve", lambda e, tmp=tmp: e.tensor_tensor(out=tmp, in0=tmp, in1=A, op=ALU.subtract),
             reads=[tr, R("lnA")], writes=[tr])
        S.op("act", lambda e, tmp=tmp, zc=zc, c=c: e.activation(out=zc, in_=tmp, func=AF.Identity,
                                                                bias=K.lnp_sb[:, bi + c:bi + c + 1],
                                                                scale=K.lnp_sb[:, gi + c:gi + c + 1]),
             reads=[tr, R("lnp")], writes=xr(t0, W, [c]))
        if last:
            ot = K.otile[c % 2]
            otr = R("otile", c % 2)
            S.op("act", lambda e, tmp=tmp, c=c: e.activation(out=tmp, in_=tmp, func=AF.Identity,
                                                          bias=K.lnp_sb[:, bi + c:bi + c + 1],
                                                          scale=K.lnp_sb[:, gi + c:gi + c + 1]),
                 reads=[tr, R("lnp")], writes=[tr])
            bank = 4 + c % 2
            nj = W // 128
            for j in range(nj):
                S.op("pe", lambda e, tmp=tmp, j=j, bank=bank: e.transpose(out=K.ps[bank][:, j * 128:(j + 1) * 128],
                                                                          in_=tmp[:, j * 128:(j + 1) * 128],
                                                                          identity=K.ident_f[:, :]),
                     reads=[tr, R("ident_f")], writes=[R("ps", bank)], tick=(j == nj - 1))
            S.op("dve", lambda e, ot=ot, bank=bank: e.tensor_copy(out=ot[:, 0:W], in_=K.ps[bank][:, 0:W]),
                 reads=[R("ps", bank)], writes=[otr])
            dst = K.y_out[t0:t0 + W, c * 128:(c + 1) * 128].rearrange("(j p) f -> p j f", p=128)
            S.op("pool", lambda e, ot=ot, dst=dst, nj=nj: e.dma_start(out=dst, in_=ot[:, 0:W].rearrange("p (j f) -> p j f", j=nj)),
                 reads=[otr], writes=[R("yout", t0, c)], dma=True)


def mixer_none(K, layer):
    S = K.S
    for tg in range(NTG):
        t0 = tg * G
        for c in range(NCH):
            zc = K.xT[:, c, t0:t0 + G]
            S.op("dve", lambda e, zc=zc: e.tensor_scalar(out=zc, in0=zc, scalar1=ALPHA, scalar2=None, op0=ALU.mult),
                 reads=xr(t0, G, [c]), writes=xr(t0, G, [c]))
            ln_accum(K, t0, G, c)
        ln_finish(K, layer, 0, t0, G, False)


def ffn(K, layer, last):
    S = K.S
    hT = K.big[:, :].rearrange("p (c t) -> p c t", c=64)
    for tg in range(NTG):
        t0 = tg * G
        for ob in range(DFF // 256):
            banks = [2 * (ob % 2), 2 * (ob % 2) + 1]
            dense_block(K, K.ffn_w1[layer], ob, NCH, lambda kc: K.xT[:, kc, t0:t0 + G], lambda kc: xr(t0, G, [kc]), G, banks)
            for o in range(2):
                relu2(K, ob * 2 + o, banks[o], hT)
        for ob in range(D // 256):
            banks = [2 * (ob % 2), 2 * (ob % 2) + 1]
            dense_block(K, K.ffn_w2[layer], ob, 64, lambda kc: hT[:, kc, :], lambda kc: [R("hT", kc)], G, banks)
            for o in range(2):
                resid_ln_accum(K, t0, G, ob * 2 + o, K.ps[banks[o]][:, 0:G], [R("ps", banks[o])])
        ln_finish(K, layer, 1, t0, G, last)


def relu2(K, oc, bank, hT):
    S = K.S
    tmp = K.lnT[oc % 2]
    S.op("act", lambda e: e.activation(out=tmp[:, :], in_=K.ps[bank][:, :], func=AF.Relu),
         reads=[R("ps", bank)], writes=[R("lnT", oc % 2)])
    S.op("act", lambda e: e.activation(out=hT[:, oc, :], in_=tmp[:, :], func=AF.Square),
         reads=[R("lnT", oc % 2)], writes=[R("hT", oc)])


S5_POW = [1, 2, 3, 4, 5, 6, 7, 8, 16, 32, 64, 128, 256, 512, 1024]


def tt(K, eng, out, a, b, op, reads, writes):
    return K.S.op(eng, lambda e: e.tensor_tensor(out=out, in0=a, in1=b, op=op), reads=reads, writes=writes)


def cmul(K, eng, outr, outi, ar, ai, br, bi, t1, t2, rr, ww):
    tr = [R("s5tmp")]
    tt(K, eng, t1, ar, br, ALU.mult, rr, tr)
    tt(K, eng, t2, ai, bi, ALU.mult, rr, tr)
    tt(K, eng, t2, t1, t2, ALU.subtract, tr, tr)
    tt(K, eng, t1, ar, bi, ALU.mult, rr, tr)
    tt(K, eng, outi, ai, br, ALU.mult, rr + tr, ww)
    tt(K, eng, outi, outi, t1, ALU.add, ww + tr, ww)
    K.S.op(eng, lambda e: e.tensor_copy(out=outr, in_=t2), reads=tr, writes=ww)


def cexp_abar(K, eng, ar_t, ai_t, ldt_t, outr, outi, tmps, rr, ww):
    S = K.S
    d, c, s = tmps
    tr = [R("s5tmp2")]
    S.op("act", lambda e: e.activation(out=d, in_=ldt_t, func=AF.Exp), reads=rr, writes=tr)
    tt(K, eng, c, ai_t, d, ALU.mult, rr + tr, tr)
    tt(K, eng, d, ar_t, d, ALU.mult, rr + tr, tr)
    S.op("act", lambda e: e.activation(out=d, in_=d, func=AF.Exp), reads=tr, writes=tr)
    S.op("act", lambda e: e.activation(out=s, in_=c, func=AF.Sin, scale=1.0 / 16), reads=tr, writes=tr)
    S.op("act", lambda e: e.activation(out=c, in_=c, func=AF.Sin, scale=-1.0 / 16, bias=K.halfpi[:, 0:1]), reads=tr + [R("consts5")], writes=tr)
    for _ in range(4):
        tt(K, eng, outr, c, s, ALU.mult, tr, ww)
        tt(K, eng, c, c, c, ALU.mult, tr, tr)
        tt(K, eng, s, s, s, ALU.mult, tr, tr)
        tt(K, eng, c, c, s, ALU.subtract, tr, tr)
        S.op(eng, lambda e: e.tensor_scalar(out=s, in0=outr, scalar1=2.0, scalar2=None, op0=ALU.mult), reads=ww, writes=tr)
    tt(K, eng, outr, c, d, ALU.mult, tr, ww)
    tt(K, eng, outi, s, d, ALU.mult, tr, ww)


def s5_layer(K, layer, ia):
    nc, S = K.nc, K.S
    S.barrier()
    bg = Arena(K, K.big_off, 65536)
    yT = bg.get("yT", [128, NCH, 1024], BF16)
    PW = bg.get("PW", [128, len(S5_POW), 2, 128], F32)
    bT = bg.get("bT", [128, 8, 128], F32)
    H0 = bg.get("H0", [128, NS, 128], F32)
    FINs = bg.get("FINs", [128, NS, 128], F32)
    Hb = [bg.get("Hb%d" % i, [128, 1024], BF16) for i in range(2)]
    sc = Arena(K, K.scr_off, K.scr_size)
    P = sc.get("P", [128, 4, 132], F32)
    Sb = sc.get("Sb", [128, 4, 128], BF16)
    CAR = sc.get("CAR", [128, 128], F32)
    FINp = sc.get("FINp", [128, 128], F32)
    Ct = sc.get("Ct", [128, 8, 128], BF16)
    cst = sc.get("cst", [128, 8, 16], F32)
    rt = [sc.get("rt%d" % i, [128, 64], F32) for i in range(12)]
    tmpm = [sc.get("tmpm%d" % i, [128, 128], F32) for i in range(4)]
    Jsg = sc.get("Jsg", [128, 128], F32)
    rowmask = sc.get("rowmask", [128, 8], F32)
    dcol = sc.get("dcol", [128, NCH], F32)
    K.halfpi = sc.get("halfpi", [128, 1], F32)
    l1 = [K.lnT[0][:, 0:128], K.lnT[0][:, 128:256], K.lnT[0][:, 256:384], K.lnT[1][:, 0:128], K.lnT[1][:, 128:256], K.lnT[1][:, 256:384],
          K.lnT[0][:, 384:512], K.lnT[1][:, 384:512]]
    l1r = [R("lnT", 0), R("lnT", 1)]
    CS = [R("consts5")]

    S.op("pool", lambda e: e.memset(K.halfpi[:, :], float(np.pi / 2)), writes=CS)
    S.op("pool", lambda e: e.memset(Jsg[:, :], 0.0), writes=CS)
    S.op("pool", lambda e: e.affine_select(out=Jsg[:, :], in_=Jsg[:, :], pattern=[[1, 128]], compare_op=ALU.not_equal,
                                            fill=1.0, base=-64, channel_multiplier=-1), reads=CS, writes=CS)
    S.op("pool", lambda e: e.affine_select(out=Jsg[:, :], in_=Jsg[:, :], pattern=[[1, 128]], compare_op=ALU.not_equal,
                                            fill=-1.0, base=64, channel_multiplier=-1), reads=CS, writes=CS)
    S.op("pool", lambda e: e.memset(rowmask[:, :], 1.0), writes=CS)
    S.op("pool", lambda e: e.affine_select(out=rowmask[:, :], in_=rowmask[:, :], pattern=[[-16, 8]], compare_op=ALU.is_ge,
                                            fill=0.0, base=0, channel_multiplier=1), reads=CS, writes=CS)
    S.op("pool", lambda e: e.affine_select(out=rowmask[:, :], in_=rowmask[:, :], pattern=[[16, 8]], compare_op=ALU.is_ge,
                                            fill=0.0, base=15, channel_multiplier=-1), reads=CS, writes=CS)
    S.op("pool", lambda e: e.memset(Ct[:, :, :], 0.0), writes=[R("Ct")])
    S.op("sp", lambda e: e.dma_start(out=dcol[:, :], in_=K.s5_d[ia]), writes=[R("dcol")], dma=True)

    for i in range(3):
        S.op("sp", lambda e, i=i: e.dma_start(out=l1[i], in_=K.s5_l1[ia, i]), writes=l1r, dma=True)

    def pw(n, ri):
        return PW[:, S5_POW.index(n), ri, :]
    PWR = [R("PW")]
    cexp_abar(K, "dve", l1[0], l1[1], l1[2], pw(1, 0), pw(1, 1), [l1[3], l1[4], l1[5]], l1r, PWR)
    for n in range(2, 9):
        cmul(K, "dve", pw(n, 0), pw(n, 1), pw(n - 1, 0), pw(n - 1, 1), pw(1, 0), pw(1, 1), l1[6], l1[7], PWR + l1r, PWR)
    for n in S5_POW[8:]:
        cmul(K, "dve", pw(n, 0), pw(n, 1), pw(n // 2, 0), pw(n // 2, 1), pw(n // 2, 0), pw(n // 2, 1), l1[6], l1[7], PWR + l1r, PWR)

    si = next_stg(K)
    stg = K.stg[si]
    S.op("sp", lambda e: e.dma_start(out=stg[:, 0:NS * 128].rearrange("g (s k) -> g s k", s=NS),
                                     in_=K.s5_h0[ia].rearrange("s g k -> g s k")), writes=[R("stg", si)], dma=True)
    for s in range(NS):
        bank = 4 + s % 2
        S.op("pe", lambda e, s=s, bank=bank: e.transpose(out=K.ps[bank][:, 0:128], in_=stg[:, s * 128:(s + 1) * 128], identity=K.ident_f[:, :]),
             reads=[R("stg", si), R("ident_f")], writes=[R("ps", bank)])
        copy_op(K, evac_engine(K), H0[:, s, :], K.ps[bank][:, 0:128], [R("ps", bank)], [R("H0")])

    passes = [(0, 1024, "A"), (1024, 1024, "B"), (2048, 512, "C")]
    if K.cfg.get("s5_passes"):
        passes = K.cfg["s5_passes"]
    mats_res = [[R("stg", 0)], [R("stg", 1)], [R("stg", 2)], [R("wbf", 0), R("wbf", 1)]]

    def mats(i):
        if i < 3:
            mb = K.stg[i][:, 0:1024].bitcast(BF16).rearrange("p (m c) -> p m c", m=16)
            mf = K.stg[i][:, 1024:2048].rearrange("p (m c) -> p m c", m=8)
        else:
            mb = K.wbf[0][:, :].rearrange("p (m c) -> p m c", m=16)
            mf = K.wbf[1][:, :].bitcast(F32).rearrange("p (m c) -> p m c", m=8)
        return mb, mf

    for (t0, W, pk) in passes:
        nch = W // 8
        ncol = (nch + 1) if pk != "C" else 9
        for fc in range(NCH):
            RT = [R("rt")]
            si = next_stg(K)
            stg2 = K.stg[si]
            S.op("sp", lambda e, stg2=stg2, fc=fc: e.dma_start(out=stg2[:, 0:192].rearrange("p (i k) -> p i k", i=3), in_=K.s5_rows[ia, fc]),
                 writes=[R("stg", si)], dma=True)
            S.op("sp", lambda e, stg2=stg2, fc=fc: e.dma_start(out=stg2[:, 192:320].rearrange("p (i k) -> p i k", i=2), in_=K.s5_bT[ia, fc]),
                 writes=[R("stg", si)], dma=True)
            S.op("sp", lambda e, fc=fc: e.dma_start(out=cst[:, :, :], in_=K.s5_cT[ia, fc]), writes=[R("cst")], dma=True)
            lam_r, lam_i, ldt = stg2[:, 0:64], stg2[:, 64:128], stg2[:, 128:192]
            b_r, b_i = stg2[:, 192:256], stg2[:, 256:320]
            SG = [R("stg", si)]
            abr, abi = rt[0], rt[1]
            cexp_abar(K, "dve", lam_r, lam_i, ldt, abr[:, :], abi[:, :], [rt[2][:, :], rt[3][:, :], rt[4][:, :]], SG, RT)
            nr, den, cr, ci = rt[2][:, :], rt[3][:, :], rt[5][:, :], rt[6][:, :]
            t1, t2 = rt[7][:, :], rt[8][:, :]
            S.op("dve", lambda e: e.tensor_scalar(out=nr, in0=abr[:, :], scalar1=-1.0, scalar2=None, op0=ALU.add), reads=RT, writes=RT)
            tt(K, "dve", den, lam_r, lam_r, ALU.mult, SG, RT)
            tt(K, "dve", t1, lam_i, lam_i, ALU.mult, SG, RT)
            tt(K, "dve", den, den, t1, ALU.add, RT, RT)
            S.op("dve", lambda e: e.reciprocal(out=den, in_=den), reads=RT, writes=RT)
            tt(K, "dve", cr, nr, lam_r, ALU.mult, RT + SG, RT)
            tt(K, "dve", t1, abi[:, :], lam_i, ALU.mult, RT + SG, RT)
            tt(K, "dve", cr, cr, t1, ALU.add, RT, RT)
            tt(K, "dve", cr, cr, den, ALU.mult, RT, RT)
            tt(K, "dve", ci, abi[:, :], lam_r, ALU.mult, RT + SG, RT)
            tt(K, "dve", t1, nr, lam_i, ALU.mult, RT + SG, RT)
            tt(K, "dve", ci, ci, t1, ALU.subtract, RT, RT)
            tt(K, "dve", ci, ci, den, ALU.mult, RT, RT)
            BT = [R("bT")]
            cmul(K, "dve", bT[:, 0, 0:64], bT[:, 0, 64:128], cr, ci, b_r, b_i, t1, t2, RT + SG, BT)
            for tau in range(1, 8):
                cmul(K, "dve", bT[:, tau, 0:64], bT[:, tau, 64:128], bT[:, tau - 1, 0:64], bT[:, tau - 1, 64:128],
                     abr[:, :], abi[:, :], t1, t2, RT + BT, BT)
            for g8 in range(8):
                S.op("pool", lambda e, g8=g8: e.tensor_copy(out=Ct[0:64, g8, 16 * g8:16 * g8 + 16], in_=cst[0:64, g8, :]),
                     reads=[R("cst")], writes=[R("Ct")])
                S.op("pool", lambda e, g8=g8: e.tensor_scalar(out=Ct[64:128, g8, 16 * g8:16 * g8 + 16], in0=cst[64:128, g8, :],
                                                              scalar1=-1.0, scalar2=None, op0=ALU.mult),
                     reads=[R("cst")], writes=[R("Ct")])

            ui = fc % 2
            uD = K.otile[ui][:, :].bitcast(BF16)
            uDv = uD[:, 0:W].rearrange("p (j c) -> p j c", j=8)
            S.op("pool", lambda e, uDv=uDv, fc=fc, t0=t0, W=W: e.tensor_copy(out=uDv, in_=K.xT[:, fc, t0:t0 + W].rearrange("p (c j) -> p j c", j=8)),
                 reads=xr(t0, W, [fc]), writes=[R("otile", ui)])

            def uT(s, uDv=uDv):
                return uDv[:, s, :]
            ures = [R("otile", ui)]
            ybanks = [5, 6]
            for b4 in range(2):
                g0 = fc * 8 + b4 * 4
                for i in range(4):
                    g = g0 + i
                    mb, mf = mats(i)
                    for n in range(1, 9):
                        tm = tmpm[n % 4]
                        S.op("act", lambda e, tm=tm, n=n, g=g: e.activation(out=tm[:, :], in_=Jsg[:, :], func=AF.Copy, scale=pw(n, 1)[:, g:g + 1]),
                             reads=PWR + CS, writes=[R("tmpm", n % 4)])
                        S.op("dve", lambda e, tm=tm, n=n, g=g, mb=mb: e.scalar_tensor_tensor(out=mb[:, n - 1, :], in0=K.ident_f[:, :], scalar=pw(n, 0)[:, g:g + 1],
                                                                                               in1=tm[:, :], op0=ALU.mult, op1=ALU.add),
                             reads=PWR + [R("tmpm", n % 4), R("ident_f")], writes=mats_res[i])
                    for k in range(8):
                        if (1 << k) >= ncol:
                            continue
                        n = 8 << k
                        tm = tmpm[k % 4]
                        S.op("act", lambda e, tm=tm, n=n, g=g: e.activation(out=tm[:, :], in_=Jsg[:, :], func=AF.Copy, scale=pw(n, 1)[:, g:g + 1]),
                             reads=PWR + CS, writes=[R("tmpm", k % 4)])
                        S.op("dve", lambda e, tm=tm, n=n, g=g, mf=mf, k=k: e.scalar_tensor_tensor(out=mf[:, k, :], in0=K.ident_f[:, :], scalar=pw(n, 0)[:, g:g + 1],
                                                                                                    in1=tm[:, :], op0=ALU.mult, op1=ALU.add),
                             reads=PWR + [R("tmpm", k % 4), R("ident_f")], writes=mats_res[i])
                    S.op("act", lambda e, g=g, mb=mb: e.activation(out=mb[:, 8:16, :], in_=bT[:, :, :], func=AF.Copy, scale=rowmask[:, g % 8:g % 8 + 1]),
                         reads=BT + CS, writes=mats_res[i])
                for i in range(4):
                    mb, mf = mats(i)
                    bank = 3 + i // 2
                    off = (i % 2) * 256
                    for s in range(8):
                        u_ap = uT(s)
                        o_ap = K.ps[bank][:, off:off + nch]
                        S.op("pe", lambda e, mb=mb, s=s, u_ap=u_ap, o_ap=o_ap: e.matmul(o_ap, lhsT=mb[:, 8 + 7 - s, :], rhs=u_ap,
                                                                                                 start=(s == 0), stop=(s == 7)),
                             reads=mats_res[i] + ures, writes=[R("ps", bank)], tick=(s == 7))
                for b in range(2):
                    bank = 3 + b
                    src = K.ps[bank][:, :].rearrange("p (i c) -> p i c", i=2)[:, :, 0:nch]
                    if pk != "C":
                        dst = P[:, 2 * b:2 * b + 2, 1:1 + nch]
                    else:
                        dst = P[:, 2 * b:2 * b + 2, 0:72].rearrange("p i (s c) -> p i s c", c=9)[:, :, :, 1:9]
                        src = K.ps[bank][:, :].rearrange("p (i s c) -> p i s c", i=2, c=8)[:, :, 0:8, :]
                    copy_op(K, "act", dst, src, [R("ps", bank)], [R("P")])
                if pk == "A":
                    S.op("pool", lambda e: e.memset(P[:, :, 0:1], 0.0), writes=[R("P")])
                elif pk == "B":
                    S.op("pool", lambda e, g0=g0: e.tensor_copy(out=P[:, :, 0], in_=CAR[:, g0:g0 + 4]), reads=[R("CAR")], writes=[R("P")])
                else:
                    S.op("pool", lambda e, g0=g0: e.tensor_copy(out=P[:, :, 0:72].rearrange("p i (s c) -> p i s c", c=9)[:, :, :, 0],
                                                               in_=H0[:, :, g0:g0 + 4].rearrange("p s i -> p i s")), reads=[R("H0")], writes=[R("P")])
                k = 0
                while (1 << k) < ncol:
                    sh = 1 << k
                    for i in range(4):
                        mb, mf = mats(i)
                        bank = 3 + i // 2
                        off = (i % 2) * 256
                        if pk != "C":
                            rhs = P[:, i, 0:ncol - sh]
                            out = K.ps[bank][:, off:off + ncol - sh]
                        else:
                            rhs = P[:, i, 0:72].rearrange("p (s c) -> p s c", c=9)[:, :, 0:9 - sh]
                            out = K.ps[bank][:, off:off + 8 * (9 - sh)].rearrange("p (s c) -> p s c", s=8)
                        S.op("pe", lambda e, mf=mf, k=k, rhs=rhs, out=out: e.matmul(out, lhsT=mf[:, k, :], rhs=rhs, start=True, stop=True),
                             reads=mats_res[i] + [R("P")], writes=[R("ps", bank)], tick=(i % 2 == 1))
                    for b in range(2):
                        bank = 3 + b
                        if pk != "C":
                            dst = P[:, 2 * b:2 * b + 2, sh:ncol]
                            src = K.ps[bank][:, :].rearrange("p (i c) -> p i c", i=2)[:, :, 0:ncol - sh]
                        else:
                            dst = P[:, 2 * b:2 * b + 2, 0:72].rearrange("p i (s c) -> p i s c", c=9)[:, :, :, sh:9]
                            src = K.ps[bank][:, :].rearrange("p (i x) -> p i x", i=2)[:, :, 0:8 * (9 - sh)].rearrange("p i (s c) -> p i s c", s=8)
                        S.op("dve", lambda e, dst=dst, src=src: e.tensor_tensor(out=dst, in0=dst, in1=src, op=ALU.add),
                             reads=[R("P"), R("ps", bank)], writes=[R("P")])
                    k += 1
                if pk != "C":
                    copy_op(K, "act", Sb[:, :, 0:nch], P[:, :, 0:nch], [R("P")], [R("Sb")])
                    dstc = (CAR if pk == "A" else FINp)[:, g0:g0 + 4]
                    copy_op(K, "pool", dstc, P[:, :, nch], [R("P")], [R("CAR")] if pk == "A" else [R("FINp")])
                else:
                    pv = P[:, :, 0:72].rearrange("p i (s c) -> p i s c", c=9)
                    copy_op(K, "act", Sb[:, :, 0:64].rearrange("p i (s c) -> p i s c", c=8), pv[:, :, :, 0:8], [R("P")], [R("Sb")])
                    copy_op(K, "pool", FINs[:, :, g0:g0 + 4].rearrange("p s i -> p i s"), pv[:, :, :, 8], [R("P")], [R("FINs")])
                for i in range(4):
                    g = g0 + i
                    g8 = g % 8
                    mb, mf = mats(i)
                    hb = Hb[i % 2]
                    hres = [R("Hb", i % 2)]
                    hv = hb[:, 0:W].rearrange("p (j c) -> p j c", j=8)
                    xv = uDv
                    bank2 = [(0, 1), (2, 7)][i % 2]
                    for hlf in range(2):
                        bank = bank2[hlf]
                        pv = K.ps[bank][:, :].rearrange("p (j c) -> p j c", j=4)
                        jlo = 4 * hlf
                        for d in range(jlo + 4):
                            ja = max(jlo, d)
                            rhs_ap = xv[:, ja - d:jlo + 4 - d, :]
                            o_ap = pv[:, ja - jlo:4, 0:nch]
                            S.op("pe", lambda e, mb=mb, d=d, rhs_ap=rhs_ap, o_ap=o_ap: e.matmul(o_ap, lhsT=mb[:, 8 + d, :], rhs=rhs_ap,
                                                                                                     start=(d == 0), stop=False),
                                 reads=mats_res[i] + ures, writes=[R("ps", bank)], tick=False)
                        for j in range(jlo, jlo + 4):
                            o_ap = pv[:, j - jlo, 0:nch]
                            sb_ap = Sb[:, i, 0:nch]
                            S.op("pe", lambda e, mb=mb, j=j, o_ap=o_ap, sb_ap=sb_ap: e.matmul(o_ap, lhsT=mb[:, j, :], rhs=sb_ap,
                                                                                               start=False, stop=True),
                                 reads=mats_res[i] + [R("Sb")], writes=[R("ps", bank)], tick=(j == jlo + 3))
                        copy_op(K, evac_engine(K), hv[:, jlo:jlo + 4, :], pv[:, :, 0:nch], [R("ps", bank)], hres)
                    for tb in range(W // 512):
                        S.op("pe", lambda e, g8=g8, hb=hb, tb=tb: e.matmul(K.ps[ybanks[tb]][:, :], lhsT=Ct[:, g8, :], rhs=hb[:, tb * 512:(tb + 1) * 512],
                                                                          start=(g8 == 0), stop=(g8 == 7)),
                             reads=[R("Ct")] + hres, writes=[R("ps", ybanks[tb])], tick=True)
            for tb in range(W // 512):
                bank = ybanks[tb]
                ua = uD[:, tb * 512:(tb + 1) * 512]
                y1, t2_ = K.lnT[0][:, :], K.lnT[1][:, :]
                S.op("dve", lambda e, ua=ua, bank=bank, fc=fc: e.scalar_tensor_tensor(out=y1, in0=ua, scalar=dcol[:, fc:fc + 1], in1=K.ps[bank][:, :],
                                                                                      op0=ALU.mult, op1=ALU.add),
                     reads=ures + [R("dcol"), R("ps", bank)], writes=[R("lnT", 0)])
                S.op("act", lambda e: e.activation(out=t2_, in_=y1, func=AF.Square), reads=[R("lnT", 0)], writes=[R("lnT", 1)])
                S.op("dve", lambda e: e.tensor_scalar(out=t2_, in0=t2_, scalar1=0.044715, scalar2=1.0, op0=ALU.mult, op1=ALU.add),
                     reads=[R("lnT", 1)], writes=[R("lnT", 1)])
                tt(K, "dve", t2_, t2_, y1, ALU.mult, [R("lnT", 0), R("lnT", 1)], [R("lnT", 1)])
                S.op("act", lambda e: e.activation(out=t2_, in_=t2_, func=AF.Sigmoid, scale=1.5957691216057308), reads=[R("lnT", 1)], writes=[R("lnT", 1)])
                nj = 512 // nch
                yo = yT[:, fc, 0:W].rearrange("p (c j) -> p j c", j=8)[:, tb * nj:(tb + 1) * nj, :]
                tt(K, "dve", yo, y1.rearrange("p (j c) -> p j c", j=nj), t2_.rearrange("p (j c) -> p j c", j=nj), ALU.mult,
                   [R("lnT", 0), R("lnT", 1)], [R("yT", fc)])
        for tb in range(W // 512):
            tg0 = t0 + tb * 512
            for ob in range(D // 256):
                def rhs(kc, tb=tb):
                    return yT[:, kc, tb * 512:(tb + 1) * 512]
                dense_block(K, K.s5_wo[ia], ob, NCH, rhs, lambda kc: [R("yT", kc)], 512, [0, 1])
                dense_block(K, K.s5_wg[ia], ob, NCH, rhs, lambda kc: [R("yT", kc)], 512, [2, 3])
                for o in range(2):
                    sg = K.lnT[o][:, :]
                    S.op("act", lambda e, sg=sg, o=o: e.activation(out=sg, in_=K.ps[2 + o][:, :], func=AF.Sigmoid), reads=[R("ps", 2 + o)], writes=[R("lnT", o)])
                    tt(K, "dve", sg, sg, K.ps[o][:, :], ALU.mult, [R("lnT", o), R("ps", o)], [R("lnT", o)])
                    resid_ln_accum(K, tg0, 512, ob * 2 + o, sg, [R("lnT", o)])
            ln_finish(K, layer, 0, tg0, 512, False)
    def out_state(src_ap, src_res, dst):
        bank = 4
        S.op("pe", lambda e: e.transpose(out=K.ps[bank][:, 0:128], in_=src_ap, identity=K.ident_f[:, :]),
             reads=src_res + [R("ident_f")], writes=[R("ps", bank)])
        ot = K.otile[0]
        copy_op(K, "dve", ot[:, 0:128], K.ps[bank][:, 0:128], [R("ps", bank)], [R("otile", 0)])
        S.op("pool", lambda e: e.dma_start(out=dst, in_=ot[:, 0:128]), reads=[R("otile", 0)], writes=[R("sout")], dma=True)
    if any(p[2] == "B" for p in passes):
        out_state(FINp[:, :], [R("FINp")], K.ssm_p[ia])
    if any(p[2] == "C" for p in passes):
        for s in range(NS):
            out_state(FINs[:, s, :], [R("FINs")], K.ssm_s[ia, s])
    S.barrier()


AG = 256


def dense_block_tm(K, w2d, ob, t0, banks):
    S = K.S
    for kt in range(2):
        wi = load_w_tile(K, wtile(w2d, kt, ob))
        wbf = K.wbf[wi]
        for t in range(2):
            for k in range(8):
                kc = kt * 8 + k
                first = (kt == 0 and k == 0)
                last = (kt == 1 and k == 7)
                lhs = K.xT[:, kc, t0 + t * 128:t0 + (t + 1) * 128]
                o_ap = K.ps[banks[t]][:, 0:256]
                S.op("pe", lambda e, lhs=lhs, wbf=wbf, k=k, o_ap=o_ap, first=first, last=last:
                     e.matmul(o_ap, lhsT=lhs, rhs=wbf[:, k * 256:(k + 1) * 256], start=first, stop=last),
                     reads=[R("wbf", wi)] + xr(t0 + t * 128, 128, [kc]), writes=[R("ps", banks[t])], tick=(last or (t == 1 and k == 7)))


def attn_layer(K, layer):
    nc, S = K.nc, K.S
    S.barrier()
    bg = Arena(K, K.big_off, 65536)
    KT = bg.get("KT", [128, 16, 768], BF16)
    V = bg.get("V", [128, 6, 2048], BF16)
    oT = bg.get("oT", [128, 16, AG], BF16)
    toep = bg.get("toep", [128, 16, 2, 128], BF16)
    sc = Arena(K, K.scr_off, K.scr_size)
    qT = sc.get("qT", [128, 16, AG], BF16)
    PT = [sc.get("PT%d" % i, [128, 640], BF16) for i in range(2)]
    Ssb = sc.get("Ssb", [128, 256], F32)
    otok = [sc.get("otok%d" % i, [128, 128], BF16) for i in range(2)]
    farcol = sc.get("farcol", [128, 16], F32)
    rc = [sc.get("rc%d" % i, [128, 1], F32) for i in range(2)]
    wqkv = K.at_wqkv
    scale = float(128 ** -0.5)

    si = next_stg(K)
    stg = K.stg[si]
    for half in range(2):
        S.op("sp", lambda e, half=half, stg=stg: e.dma_start(out=stg[:, :].rearrange("p (h r q) -> p h r q", h=8, r=2), in_=K.at_toep[:, half * 8:(half + 1) * 8]),
             writes=[R("stg", si)], dma=True)
        S.op("pool", lambda e, half=half, stg=stg: e.tensor_copy(out=toep[:, half * 8:(half + 1) * 8, :, :], in_=stg[:, :].rearrange("p (h r q) -> p h r q", h=8, r=2)),
             reads=[R("stg", si)], writes=[R("toep")])
    S.op("sp", lambda e: e.dma_start(out=farcol[:, :], in_=K.at_far), writes=[R("farcol")], dma=True)
    cnt = [0]

    def attend(h, nq, q_ap, tiles, sample_i, o_dst, o_dst_res):
        u = cnt[0] % 2
        cnt[0] += 1
        pt = PT[u]
        ptr = [R("PT", u)]
        far = [t for t in tiles if t[5] == "far"]
        near = [t for t in tiles if t[5] != "far"]
        for (r, kt_ap, kt_res, v_ap, v_res, kind) in tiles:
            bank = 4 if kind == "far" else 5
            col = (r if kind == "far" else r - 3) * nq
            o_ap = K.ps[bank][:, col:col + nq]
            S.op("pe", lambda e, o_ap=o_ap, kt_ap=kt_ap: e.matmul(o_ap, lhsT=kt_ap, rhs=q_ap, start=True, stop=True),
                 reads=kt_res + [R("qT")], writes=[R("ps", bank)], tick=True)
        if far:
            r0 = far[0][0]
            src = K.ps[4][:, r0 * nq:3 * nq]
            dst = pt[:, r0 * nq:3 * nq]
            S.op("act", lambda e: e.activation(out=dst, in_=src, func=AF.Exp, bias=farcol[:, h:h + 1], scale=1.0),
                 reads=[R("ps", 4), R("farcol")], writes=ptr)
        r0n = near[0][0]
        for (r, kt_ap, kt_res, v_ap, v_res, kind) in near:
            col = (r - 3) * nq
            if sample_i is None:
                bias_ap = toep[:, h, r - 3, :]
            elif r == 3:
                bias_ap = toep[:, h, 0, 0:64]
            else:
                bias_ap = toep[:, h, 1, (sample_i % 2) * 64:(sample_i % 2) * 64 + 64]
            s_ap = Ssb[:, col:col + nq]
            p_ap = K.ps[5][:, col:col + nq]
            S.op("dve", lambda e, s_ap=s_ap, p_ap=p_ap, bias_ap=bias_ap: e.tensor_tensor(out=s_ap, in0=p_ap, in1=bias_ap, op=ALU.add),
                 reads=[R("ps", 5), R("toep")], writes=[R("Ssb")])
        srcn = Ssb[:, (r0n - 3) * nq:2 * nq]
        dstn = pt[:, r0n * nq:5 * nq]
        S.op("act", lambda e: e.activation(out=dstn, in_=srcn, func=AF.Exp), reads=[R("Ssb")], writes=ptr)
        if sample_i is None:
            if far and far[0][0] == 0:
                S.op("pool", lambda e: e.memset(pt[0:64, 64:128], 0.0), writes=ptr)
            S.op("pool", lambda e: e.memset(pt[64:128, 4 * 128:4 * 128 + 64], 0.0), writes=ptr)
        else:
            oh = (1 - sample_i % 2) * 64
            S.op("pool", lambda e: e.memset(pt[oh:oh + 64, 4 * nq:5 * nq], 0.0), writes=ptr)
        ob_ = K.ps[6]
        for idx, (r, kt_ap, kt_res, v_ap, v_res, kind) in enumerate(tiles):
            l_ap = pt[:, r * nq:(r + 1) * nq]
            S.op("pe", lambda e, l_ap=l_ap, v_ap=v_ap, idx=idx: e.matmul(ob_[0:nq, 0:128], lhsT=l_ap, rhs=v_ap, start=(idx == 0), stop=(idx == len(tiles) - 1)),
                 reads=ptr + v_res, writes=[R("ps", 6)], tick=False)
        for idx, (r, kt_ap, kt_res, v_ap, v_res, kind) in enumerate(tiles):
            l_ap = pt[:, r * nq:(r + 1) * nq]
            S.op("pe", lambda e, l_ap=l_ap, idx=idx: e.matmul(ob_[0:nq, 128:129], lhsT=l_ap, rhs=K.ones_b[:, 0:1], start=(idx == 0), stop=(idx == len(tiles) - 1)),
                 reads=ptr + [R("ones_b")], writes=[R("ps", 6)], tick=(idx == len(tiles) - 1))
        rcu = rc[u]
        S.op("dve", lambda e: e.reciprocal(out=rcu[0:nq, :], in_=ob_[0:nq, 128:129]), reads=[R("ps", 6)], writes=[R("rc", u)])
        ot = otok[u]
        S.op("act", lambda e: e.activation(out=ot[0:nq, :], in_=ob_[0:nq, 0:128], func=AF.Copy, scale=rcu[0:nq, 0:1]),
             reads=[R("ps", 6), R("rc", u)], writes=[R("otok", u)])
        tp = K.ps[7][:, :].bitcast(BF16)
        S.op("pe", lambda e: e.transpose(out=tp[:, 0:nq], in_=ot[0:nq, :], identity=K.ident_b[0:nq, 0:nq]),
             reads=[R("otok", u), R("ident_b")], writes=[R("ps", 7)])
        copy_op(K, "dve", o_dst, tp[:, 0:nq], [R("ps", 7)], o_dst_res)

    nag = T // AG
    ags = K.cfg.get("attn_ags", list(range(nag)))
    for a in ags:
        t0 = a * AG
        is_s = a >= SEQ // AG
        want_out = (is_s or (t0 >= SEQ - 512)) and not K.cfg.get("no_out")
        if not is_s:
            kcol = (a % 3) * 256
        else:
            kcol = 512
        for ob in range(8):
            banks = [2 * (ob % 2), 2 * (ob % 2) + 1]
            dense_block(K, wqkv, 8 + ob, NCH, lambda kc: K.xT[:, kc, t0:t0 + AG], lambda kc: xr(t0, AG, [kc]), AG, banks)
            for o in range(2):
                hh = ob * 2 + o
                copy_op(K, evac_engine(K), KT[:, hh, kcol:kcol + AG], K.ps[banks[o]][:, 0:AG], [R("ps", banks[o])], [R("KT", kcol // 256)])
        for which in (K.cfg.get("whichs", [1, 2]) if want_out else [2]):
            for ob in range(8):
                banks = [2 * (ob % 2), 2 * (ob % 2) + 1]
                dense_block_tm(K, wqkv, which * 8 + ob, t0, banks)
                for t in range(2):
                    if not is_s:
                        slot = (2 * a + t) % 6
                    else:
                        slot = 4 + t
                    src = K.ps[banks[t]][:, 0:256]
                    if not want_out:
                        copy_op(K, "act", V[:, slot, ob * 256:(ob + 1) * 256], src, [R("ps", banks[t])], [R("V", slot)])
                    else:
                        ot = K.otile[t]
                        copy_op(K, "dve", ot[:, 0:256], src, [R("ps", banks[t])], [R("otile", t)])
                        if which == 2:
                            copy_op(K, "act", V[:, slot, ob * 256:(ob + 1) * 256], ot[:, 0:256], [R("otile", t)], [R("V", slot)])
                        if is_s:
                            dr = (a - SEQ // AG) * AG + t * 128
                            dst = (K.at_ks if which == 1 else K.at_vs)[dr:dr + 128, ob * 256:(ob + 1) * 256]
                        else:
                            dr = t0 - (SEQ - 512) + t * 128
                            dst = (K.at_kp if which == 1 else K.at_vp)[dr:dr + 128, ob * 256:(ob + 1) * 256]
                        if not K.cfg.get("no_dma"):
                            S.op("sp", lambda e, dst=dst, ot=ot: e.dma_start(out=dst, in_=ot[:, 0:256]), reads=[R("otile", t)], writes=[R("kvout")], dma=True)
        for ob in range(8):
            banks = [2 * (ob % 2), 2 * (ob % 2) + 1]
            dense_block(K, wqkv, ob, NCH, lambda kc: K.xT[:, kc, t0:t0 + AG], lambda kc: xr(t0, AG, [kc]), AG, banks)
            for o in range(2):
                hh = ob * 2 + o
                src = K.ps[banks[o]][:, 0:AG]
                dst = qT[:, hh, :]
                S.op("act", lambda e, src=src, dst=dst: e.activation(out=dst, in_=src, func=AF.Copy, scale=scale), reads=[R("ps", banks[o])], writes=[R("qT")])
        stage = K.cfg.get("attn_stage", 3)
        if stage < 2:
            continue
        if not is_s:
            for h in range(16):
                for qi in range(2):
                    qt = 2 * a + qi
                    tiles = []
                    for r in range(5):
                        j = qt - 4 + r
                        if j < 0:
                            continue
                        blk = (j // 2) % 3
                        col = blk * 256 + (j % 2) * 128
                        tiles.append((r, KT[:, h, col:col + 128], [R("KT", blk)], V[:, j % 6, h * 128:(h + 1) * 128], [R("V", j % 6)],
                                      "far" if r < 3 else "near"))
                    attend(h, 128, qT[:, h, qi * 128:(qi + 1) * 128], tiles, None, oT[:, h, qi * 128:(qi + 1) * 128], [R("oT", h)])
        else:
            sa = a - SEQ // AG
            for i in range(4):
                sq = sa * 4 + i
                for jt in range(4):
                    si = next_stg(K)
                    stg = K.stg[si]
                    S.op("sp", lambda e, stg=stg, sq=sq, jt=jt: e.dma_start(out=stg[:, :], in_=K.at_ck[sq, jt * 128:(jt + 1) * 128, :]), writes=[R("stg", si)], dma=True)
                    for cb in range(4):
                        bank = cb % 4
                        for jj in range(4):
                            c = cb * 4 + jj
                            o_ap = K.ps[bank][:, jj * 128:(jj + 1) * 128]
                            S.op("pe", lambda e, stg=stg, c=c, o_ap=o_ap: e.transpose(out=o_ap, in_=stg[:, c * 128:(c + 1) * 128], identity=K.ident_f[:, :]),
                                 reads=[R("stg", si), R("ident_f")], writes=[R("ps", bank)], tick=(jj == 3))
                        copy_op(K, evac_engine(K), KT[:, cb * 4:(cb + 1) * 4, jt * 128:(jt + 1) * 128],
                                K.ps[bank][:, :].rearrange("p (j t) -> p j t", j=4), [R("ps", bank)], [R("KT", jt // 2)])
                    si = next_stg(K)
                    stg = K.stg[si]
                    S.op("sp", lambda e, stg=stg, sq=sq, jt=jt: e.dma_start(out=stg[:, :], in_=K.at_cv[sq, jt * 128:(jt + 1) * 128, :]), writes=[R("stg", si)], dma=True)
                    copy_op(K, "act" if jt % 2 else "dve", V[:, jt, :], stg[:, :], [R("stg", si)], [R("V", jt)])
                for h in range(16):
                    tiles = []
                    for r in range(4):
                        tiles.append((r, KT[:, h, r * 128:(r + 1) * 128], [R("KT", r // 2)], V[:, r, h * 128:(h + 1) * 128], [R("V", r)],
                                      "far" if r < 3 else "near"))
                    pc = 512 + (i // 2) * 128
                    tiles.append((4, KT[:, h, pc:pc + 128], [R("KT", 2)], V[:, 4 + i // 2, h * 128:(h + 1) * 128], [R("V", 4 + i // 2)], "near"))
                    attend(h, 64, qT[:, h, i * 64:(i + 1) * 64], tiles, i, oT[:, h, i * 64:(i + 1) * 64], [R("oT", h)])
        if stage < 3:
            continue
        for ob in range(D // 256):
            banks = [2 * (ob % 2), 2 * (ob % 2) + 1]
            dense_block(K, K.at_wo, ob, NCH, lambda kc: oT[:, kc, :], lambda kc: [R("oT", kc)], AG, banks)
            for o in range(2):
                resid_ln_accum(K, t0, AG, ob * 2 + o, K.ps[banks[o]][:, 0:AG], [R("ps", banks[o])])
        ln_finish(K, layer, 0, t0, AG, False)
    S.barrier()


HG = 256
RMS_EPS = 1e-6


def dense_block_tm64(K, w2d, ob, t0, banks):
    S = K.S
    wis = [load_w_tile(K, wtile(w2d, kt, ob)) for kt in range(2)]
    for c in range(4):
        o_ap = K.ps[banks[c // 2]][0:64, (c % 2) * 256:(c % 2) * 256 + 256]
        for kt in range(2):
            wbf = K.wbf[wis[kt]]
            for k in range(8):
                kc = kt * 8 + k
                first = (kt == 0 and k == 0)
                last = (kt == 1 and k == 7)
                lhs = K.xT[:, kc, t0 + c * 64:t0 + (c + 1) * 64]
                S.op("pe", lambda e, lhs=lhs, wbf=wbf, k=k, o_ap=o_ap, first=first, last=last:
                     e.matmul(o_ap, lhsT=lhs, rhs=wbf[:, k * 256:(k + 1) * 256], start=first, stop=last),
                     reads=[R("wbf", wis[kt])] + xr(t0 + c * 64, 64, [kc]), writes=[R("ps", banks[c // 2])], tick=(k == 7))


def hgrn_layer(K, layer):
    nc, S = K.nc, K.S
    S.barrier()
    bg = Arena(K, K.big_off, 65536)
    qhT = bg.get("qhT", [128, 16, HG], BF16)
    ktT = bg.get("ktT", [128, 16, HG], BF16)
    mT = bg.get("mT", [128, 16, HG], BF16)
    GsT = bg.get("GsT", [128, 16, HG], BF16)
    Vc = bg.get("Vc", [64, 4, 2048], BF16)
    Kc2 = [bg.get("Kc%d" % i, [64, 2048], BF16) for i in range(2)]
    Sbf = bg.get("Sbf", [128, 16, 128], BF16)
    ATm = [bg.get("ATm%d" % i, [64, 4, 64], BF16) for i in range(2)]
    omb = [bg.get("omb%d" % i, [64, 4, 128], BF16) for i in range(2)]
    sc = Arena(K, K.scr_off, K.scr_size)
    Sm = sc.get("Sm", [128, 16, 128], F32)
    ft = [K.lnT[0][:, 0:256], K.lnT[0][:, 256:512], K.lnT[1][:, 0:256], K.lnT[1][:, 256:512]]
    ebuf = [K.otile[0][:, 0:256], K.otile[1][:, 0:256]]
    sq = K.lnA[0:64, :]
    eL = sc.get("eL", [128, 16, 4], F32)
    ss = sc.get("ss", [64, 16], F32)
    ones64 = sc.get("ones64", [128, 64], F32)
    M64 = sc.get("M64", [64, 4, 64], F32)
    lbt = sc.get("lbt", [128, 16], F32)
    oml = sc.get("oml", [128, 16], F32)
    ngt = sc.get("ngt", [128, 1], F32)
    lg = sc.get("lg", [128, 4, 16], F32)
    den = sc.get("den", [128, 16], F32)
    epsr = sc.get("epsr", [64, 1], F32)
    w_in = K.hg_win
    C = [R("hgc")]

    S.op("pool", lambda e: e.memset(ones64[:, :], 1.0), writes=C)
    S.op("pool", lambda e: e.memset(epsr[:, :], RMS_EPS), writes=C)
    S.op("pool", lambda e: e.memset(M64[:, :, :], 1.0), writes=C)
    S.op("pool", lambda e: e.affine_select(out=M64[:, :, :], in_=M64[:, :, :], pattern=[[0, 4], [1, 64]], compare_op=ALU.is_ge,
                                            fill=0.0, base=0, channel_multiplier=-1), reads=C, writes=C)
    S.op("sp", lambda e: e.dma_start(out=ngt[:, :], in_=K.hg_ng), writes=C, dma=True)
    S.op("sp", lambda e: e.dma_start(out=lg[:, :, :], in_=K.hg_lb), writes=C, dma=True)
    S.op("act", lambda e: e.activation(out=lg[:, :, :], in_=lg[:, :, :], func=AF.Exp), reads=C, writes=C)
    tt(K, "dve", den[:, :], lg[:, 0, :], lg[:, 1, :], ALU.add, C, C)
    tt(K, "dve", den[:, :], den[:, :], lg[:, 2, :], ALU.add, C, C)
    tt(K, "dve", den[:, :], den[:, :], lg[:, 3, :], ALU.add, C, C)
    S.op("dve", lambda e: e.reciprocal(out=den[:, :], in_=den[:, :]), reads=C, writes=C)
    S.op("pool", lambda e: e.memset(lbt[:, :], 0.0), writes=C)
    for l in range(1, layer + 1):
        tt(K, "dve", lbt[:, :], lbt[:, :], lg[:, l, :], ALU.add, C, C)
    tt(K, "dve", lbt[:, :], lbt[:, :], den[:, :], ALU.mult, C, C)
    S.op("dve", lambda e: e.tensor_scalar(out=oml[:, :], in0=lbt[:, :], scalar1=-1.0, scalar2=1.0, op0=ALU.mult, op1=ALU.add), reads=C, writes=C)
    S.op("pool", lambda e: e.memset(Sm[:, :, :], 0.0), writes=[R("Sm", h) for h in range(16)])
    S.op("pool", lambda e: e.memset(Sbf[:, :, :], 0.0), writes=[R("Sbf", h) for h in range(16)])

    ntg = T // HG
    tgs = K.cfg.get("hgrn_tgs", list(range(ntg)))
    ucnt = [0]
    for a in tgs:
        t0 = a * HG
        is_s = a >= SEQ // HG
        for ob in range(8):
            banks = [2 * (ob % 2), 2 * (ob % 2) + 1]
            dense_block_tm64(K, w_in, 2 * 8 + ob, t0, banks)
            for b in range(2):
                src = K.ps[banks[b]][0:64, :].rearrange("p (c n) -> p c n", c=2)
                dst = Vc[:, 2 * b:2 * b + 2, ob * 256:(ob + 1) * 256]
                copy_op(K, "act", dst, src, [R("ps", banks[b])], [R("Vc")])
        for hp in range(8):
            banks = [2 * (hp % 2), 2 * (hp % 2) + 1]
            dense_block(K, w_in, 24 + hp, NCH, lambda kc: K.xT[:, kc, t0:t0 + HG], lambda kc: xr(t0, HG, [kc]), HG, banks)
            for o in range(2):
                h = hp * 2 + o
                tmp = K.lnT[o][:, 0:HG]
                psg = K.ps[banks[o]][:, 0:HG]
                S.op("act", lambda e, tmp=tmp, psg=psg: e.activation(out=tmp, in_=psg, func=AF.Sigmoid), reads=[R("ps", banks[o])], writes=[R("lnT", o)])
                tt(K, "dve", GsT[:, h, :], tmp, psg, ALU.mult, [R("lnT", o), R("ps", banks[o])], [R("GsT", h)])
        for hp in range(8):
            dense_block(K, w_in, 8 + hp, NCH, lambda kc: K.xT[:, kc, t0:t0 + HG], lambda kc: xr(t0, HG, [kc]), HG, [0, 1])
            for o in range(2):
                h = hp * 2 + o
                psf = K.ps[o][:, 0:HG]
                f_, omf, b_, en = ft[0], ft[1], ft[2], ft[3]
                eb = ebuf[o]
                FT = [R("lnT", 0), R("lnT", 1)]
                S.op("act", lambda e, psf=psf: e.activation(out=f_, in_=psf, func=AF.Sigmoid), reads=[R("ps", o)], writes=FT)
                S.op("dve", lambda e, h=h: e.tensor_scalar(out=f_, in0=f_, scalar1=oml[:, h:h + 1], scalar2=lbt[:, h:h + 1], op0=ALU.mult, op1=ALU.add),
                     reads=FT + C, writes=FT)
                S.op("dve", lambda e: e.tensor_scalar(out=omf, in0=f_, scalar1=-1.0, scalar2=1.0, op0=ALU.mult, op1=ALU.add), reads=FT, writes=FT)
                S.op("act", lambda e: e.activation(out=f_, in_=f_, func=AF.Ln), reads=FT, writes=FT)
                for c in range(4):
                    S.op("dve", lambda e, c=c: e.tensor_tensor_scan(out=b_[:, c * 64:(c + 1) * 64], data0=ones64[:, :], data1=f_[:, c * 64:(c + 1) * 64],
                                                                    initial=0.0, op0=ALU.mult, op1=ALU.add), reads=FT + C, writes=FT)
                S.op("act", lambda e, eb=eb: e.activation(out=eb, in_=b_, func=AF.Exp), reads=FT, writes=[R("otile", o)])
                S.op("act", lambda e: e.activation(out=en, in_=b_, func=AF.Exp, scale=-1.0), reads=FT, writes=FT)
                tt(K, "dve", ktT[:, h, :], omf, en, ALU.mult, FT, [R("ktT", h)])
                copy_op(K, "pool", eL[:, h, :], ebuf[o][:, 63:HG:64], [R("otile", o)], [R("eL")])
            dense_block(K, w_in, hp, NCH, lambda kc: K.xT[:, kc, t0:t0 + HG], lambda kc: xr(t0, HG, [kc]), HG, [2, 3])
            for o in range(2):
                h = hp * 2 + o
                tt(K, "dve", qhT[:, h, :], K.ps[2 + o][:, 0:HG], ebuf[o], ALU.mult, [R("ps", 2 + o), R("otile", o)], [R("qhT", h)])
        for c in range(4):
            if is_s:
                sq_i = (a - SEQ // HG) * 4 + c
                S.op("sp", lambda e, sq_i=sq_i: e.dma_start(out=Sm[:, :, :], in_=K.hg_s0[sq_i].rearrange("h k v -> k h v")),
                     writes=[R("Sm", h) for h in range(16)], dma=True)
                copy_op(K, "act", Sbf[:, :, :], Sm[:, :, :], [R("Sm", h) for h in range(16)], [R("Sbf", h) for h in range(16)])
            csl = slice(c * 64, (c + 1) * 64)
            Kc = Kc2[c % 2]
            KR = [R("Kc", c % 2)]
            for hg in range(4):
                tp = K.ps[7][:, :].bitcast(BF16)
                for hh in range(4):
                    h = hg * 4 + hh
                    i_ap = ktT[:, h, c * 64:(c + 1) * 64]
                    o_ap = tp[0:64, hh * 128:(hh + 1) * 128]
                    S.op("pe", lambda e, i_ap=i_ap, o_ap=o_ap: e.transpose(out=o_ap, in_=i_ap, identity=K.ident_b[:, :]),
                         reads=[R("ktT", h), R("ident_b")], writes=[R("ps", 7)], tick=(hh == 3))
                copy_op(K, evac_engine(K), Kc[:, hg * 512:(hg + 1) * 512], tp[0:64, 0:512], [R("ps", 7)], KR)
            for hg in range(4):
                u = ucnt[0] % 2
                ucnt[0] += 1
                at, ob_ = ATm[u], omb[u]
                for hh in range(4):
                    h = hg * 4 + hh
                    o_ap = K.ps[4][0:64, hh * 64:(hh + 1) * 64]
                    l_ap, r_ap = ktT[:, h, csl], qhT[:, h, csl]
                    S.op("pe", lambda e, o_ap=o_ap, l_ap=l_ap, r_ap=r_ap: e.matmul(o_ap, lhsT=l_ap, rhs=r_ap, start=True, stop=True),
                         reads=[R("ktT", h), R("qhT", h)], writes=[R("ps", 4)], tick=(hh == 3))
                tt(K, "dve", at[:, :, :], K.ps[4][0:64, 0:256].rearrange("p (h t) -> p h t", h=4), M64[:, :, :], ALU.mult, [R("ps", 4)] + C, [R("ATm", u)])
                for hh in range(4):
                    h = hg * 4 + hh
                    o_ap = K.ps[5][0:64, hh * 128:(hh + 1) * 128]
                    v_ap = Vc[:, c, h * 128:(h + 1) * 128]
                    a_ap = at[:, hh, :]
                    q_ap = qhT[:, h, csl]
                    s_ap = Sbf[:, h, :]
                    S.op("pe", lambda e, o_ap=o_ap, a_ap=a_ap, v_ap=v_ap: e.matmul(o_ap, lhsT=a_ap, rhs=v_ap, start=True, stop=False),
                         reads=[R("ATm", u), R("Vc")], writes=[R("ps", 5)], tick=False)
                    S.op("pe", lambda e, o_ap=o_ap, q_ap=q_ap, s_ap=s_ap: e.matmul(o_ap, lhsT=q_ap, rhs=s_ap, start=False, stop=True),
                         reads=[R("qhT", h), R("Sbf", h)], writes=[R("ps", 5)], tick=(hh == 3))
                for hh in range(4):
                    h = hg * 4 + hh
                    o_ap = K.ps[6][:, hh * 128:(hh + 1) * 128]
                    k_ap = Kc[:, h * 128:(h + 1) * 128]
                    v_ap = Vc[:, c, h * 128:(h + 1) * 128]
                    S.op("pe", lambda e, o_ap=o_ap, k_ap=k_ap, v_ap=v_ap: e.matmul(o_ap, lhsT=k_ap, rhs=v_ap, start=True, stop=True),
                         reads=KR + [R("Vc")], writes=[R("ps", 6)], tick=(hh == 3))
                for hh in range(4):
                    h = hg * 4 + hh
                    sm = Sm[:, h, :]
                    e_ap = eL[:, h, c:c + 1]
                    S.op("dve", lambda e, sm=sm, e_ap=e_ap: e.tensor_scalar(out=sm, in0=sm, scalar1=e_ap, scalar2=None, op0=ALU.mult),
                         reads=[R("Sm", h), R("eL")], writes=[R("Sm", h)])
                    p_ap = K.ps[6][:, hh * 128:(hh + 1) * 128]
                    S.op("dve", lambda e, sm=sm, e_ap=e_ap, p_ap=p_ap: e.scalar_tensor_tensor(out=sm, in0=p_ap, scalar=e_ap, in1=sm, op0=ALU.mult, op1=ALU.add),
                         reads=[R("Sm", h), R("eL"), R("ps", 6)], writes=[R("Sm", h)])
                    copy_op(K, "act", Sbf[:, h, :], sm, [R("Sm", h)], [R("Sbf", h)])
                S.op("act", lambda e: e.activation(out=sq[:, :], in_=K.ps[5][0:64, :], func=AF.Square), reads=[R("ps", 5)], writes=[R("lnA")])
                ssg = ss[:, hg * 4:(hg + 1) * 4]
                S.op("dve", lambda e, ssg=ssg: e.tensor_reduce(out=ssg, in_=sq[:, :].rearrange("p (h v) -> p h v", h=4), axis=AX.X, op=ALU.add),
                     reads=[R("lnA")], writes=[R("ss")])
                S.op("act", lambda e, ssg=ssg: e.activation(out=ssg, in_=ssg, func=AF.Sqrt, bias=epsr[:, 0:1], scale=1.0 / 128), reads=[R("ss")] + C, writes=[R("ss")])
                S.op("dve", lambda e, ssg=ssg: e.reciprocal(out=ssg, in_=ssg), reads=[R("ss")], writes=[R("ss")])
                for hh in range(4):
                    h = hg * 4 + hh
                    p_ap = K.ps[5][0:64, hh * 128:(hh + 1) * 128]
                    r_ap = ss[:, h:h + 1]
                    d_ap = ob_[:, hh, :]
                    S.op("act", lambda e, p_ap=p_ap, r_ap=r_ap, d_ap=d_ap: e.activation(out=d_ap, in_=p_ap, func=AF.Copy, scale=r_ap),
                         reads=[R("ps", 5), R("ss")], writes=[R("omb", u)])
                tp = K.ps[7][:, :].bitcast(BF16)
                for hh in range(4):
                    i_ap = ob_[:, hh, :]
                    o_ap = tp[:, hh * 64:(hh + 1) * 64]
                    S.op("pe", lambda e, i_ap=i_ap, o_ap=o_ap: e.transpose(out=o_ap, in_=i_ap, identity=K.ident_b[0:64, 0:64]),
                         reads=[R("omb", u), R("ident_b")], writes=[R("ps", 7)], tick=(hh == 3))
                m_dst = mT[:, hg * 4:(hg + 1) * 4, csl]
                g_src = GsT[:, hg * 4:(hg + 1) * 4, csl]
                t_src = tp[:, 0:256].rearrange("p (h t) -> p h t", h=4)
                S.op("dve", lambda e, m_dst=m_dst, g_src=g_src, t_src=t_src: e.scalar_tensor_tensor(out=m_dst, in0=t_src, scalar=ngt[:, 0:1], in1=g_src, op0=ALU.mult, op1=ALU.mult),
                     reads=[R("ps", 7)] + C + [R("GsT", hg * 4 + hh) for hh in range(4)], writes=[R("mT", hg * 4 + hh) for hh in range(4)])
            last_prompt = (not is_s) and (a == SEQ // HG - 1) and c == 3
            if is_s or last_prompt:
                dst = K.hg_ss[(a - SEQ // HG) * 4 + c] if is_s else K.hg_sp
                S.op("sp", lambda e, dst=dst: e.dma_start(out=dst.rearrange("h k v -> k h v"), in_=Sm[:, :, :]),
                     reads=[R("Sm", h) for h in range(16)], writes=[R("hgout")], dma=True)
        for ob in range(D // 256):
            banks = [2 * (ob % 2), 2 * (ob % 2) + 1]
            dense_block(K, K.hg_wo, ob, NCH, lambda kc: mT[:, kc, :], lambda kc: [R("mT", kc)], HG, banks)
            for o in range(2):
                resid_ln_accum(K, t0, HG, ob * 2 + o, K.ps[banks[o]][:, 0:HG], [R("ps", banks[o])])
        ln_finish(K, layer, 0, t0, HG, False)
    S.barrier()


def s5_host_layout(inp, b):
    a_re, a_im, ldt = inp["ssm_a_re"], inp["ssm_a_im"], inp["ssm_log_dt"]
    na = a_re.shape[0]
    out = {}
    are_t = a_re.transpose(0, 2, 1)
    aim_t = a_im.transpose(0, 2, 1)
    l1 = np.stack([np.concatenate([are_t, are_t], 1), np.concatenate([aim_t, aim_t], 1),
                   np.broadcast_to(ldt[:, None, :], (na, 128, 128))], 1)
    out["s5_l1"] = l1
    rows = np.stack([np.repeat(a_re, 16, axis=1), np.repeat(a_im, 16, axis=1),
                     np.broadcast_to(np.repeat(ldt, 16, axis=1)[:, :, None], (na, D, 64))], 2)
    out["s5_rows"] = rows.reshape(na, NCH, 128, 3, 64)
    bre = inp["ssm_b_re"].transpose(0, 1, 3, 2).reshape(na, D, 64)
    bim = inp["ssm_b_im"].transpose(0, 1, 3, 2).reshape(na, D, 64)
    out["s5_bT"] = np.stack([bre, bim], 2).reshape(na, NCH, 128, 2, 64)
    cre = inp["ssm_c_re"].transpose(0, 1, 3, 2)
    cim = inp["ssm_c_im"].transpose(0, 1, 3, 2)
    cc = np.concatenate([cre, cim], 2)
    out["s5_cT"] = cc.reshape(na, NCH, 8, 128, 16).transpose(0, 1, 3, 2, 4)
    out["s5_d"] = inp["ssm_d"].reshape(na, NCH, 128).transpose(0, 2, 1)
    h0 = np.concatenate([inp["state_ssm_re"][:, 8 * b:8 * b + 8], inp["state_ssm_im"][:, 8 * b:8 * b + 8]], -1)
    out["s5_h0"] = h0
    out["s5_wo"] = inp["ssm_w_out"]
    out["s5_wg"] = inp["ssm_w_gate"]
    return {k: np.ascontiguousarray(v, dtype=np.float32) for k, v in out.items()}


def attn_host_layout(inp, b):
    tab = inp["attn_rel_bias"][0]
    kp = np.arange(128)[:, None, None]
    r = np.array([-1, 0])[None, :, None]
    q = np.arange(128)[None, None, :]
    idx = np.clip(q - (128 * r + kp), -128, 128) + 128
    toep = tab[:, idx].transpose(1, 0, 2, 3)
    far = np.broadcast_to(tab[None, :, 256], (128, 16))
    out = {"at_wqkv": inp["attn_w_qkv"][0], "at_wo": inp["attn_w_o"][0], "at_toep": toep, "at_far": far,
           "at_ck": inp["cache_attn_k"][0, 8 * b:8 * b + 8].reshape(NS, 512, D),
           "at_cv": inp["cache_attn_v"][0, 8 * b:8 * b + 8].reshape(NS, 512, D)}
    return {k: np.ascontiguousarray(v, dtype=np.float32) for k, v in out.items()}


def hgrn_host_layout(inp, b):
    out = {"hg_win": inp["hgrn_w_in"][0], "hg_wo": inp["hgrn_w_o"][0],
           "hg_lb": inp["hgrn_lb_logits"].reshape(4, NCH, 128).transpose(2, 0, 1),
           "hg_ng": inp["hgrn_norm_g"][0].reshape(128, 1),
           "hg_s0": inp["state_hgrn"][0, 8 * b:8 * b + 8]}
    return {k: np.ascontiguousarray(v, dtype=np.float32) for k, v in out.items()}


def make_core_inputs(inp, c):
    b = c % 4
    x_in = np.concatenate([inp["x_prompt"][b], inp["x_sample"][8 * b:8 * b + 8].reshape(NS * DS, D)], axis=0)
    lnp = np.stack([inp["ln_mix_g"], inp["ln_mix_b"], inp["ln_ffn_g"], inp["ln_ffn_b"]], 0)
    lnp = lnp.reshape(4, DEPTH, NCH, 128).transpose(3, 0, 1, 2).reshape(128, 4 * DEPTH * NCH)
    m = {
        "x_in": np.ascontiguousarray(x_in, dtype=np.float32),
        "ffn_w1": np.ascontiguousarray(inp["ffn_w1"], dtype=np.float32),
        "ffn_w2": np.ascontiguousarray(inp["ffn_w2"], dtype=np.float32),
        "lnp": np.ascontiguousarray(lnp, dtype=np.float32),
    }
    if "ssm_a_re" in inp:
        m.update(s5_host_layout(inp, b))
    if "attn_w_qkv" in inp:
        m.update(attn_host_layout(inp, b))
    if "hgrn_w_in" in inp:
        m.update(hgrn_host_layout(inp, b))
    return m


_NC_CACHE = {}


def kernel(**inp):
    inp = {k: np.asarray(v) for k, v in inp.items()}
    if "nc" not in _NC_CACHE:
        _NC_CACHE["nc"] = build({})
    nc = _NC_CACHE["nc"]
    maps4 = [make_core_inputs(inp, c) for c in range(4)]
    in_maps = [maps4[c % 4] for c in range(8)]
    res = run_bass_kernel_spmd(nc, in_maps, core_ids=list(range(8)))
    r = res.results
    f32 = np.float32
    yp = np.stack([r[c]["y_out"][:SEQ] for c in range(4)]).astype(f32)
    ys = np.concatenate([r[c]["y_out"][SEQ:].reshape(NS, DS, D) for c in range(4)]).astype(f32)
    ssm_p = np.stack([r[c]["ssm_p"] for c in range(4)], 1)
    ssm_s = np.concatenate([r[c]["ssm_s"] for c in range(4)], 1)
    kp = np.stack([r[c]["at_kp"] for c in range(4)]).reshape(1, 4, 512, 16, 128)
    vp = np.stack([r[c]["at_vp"] for c in range(4)]).reshape(1, 4, 512, 16, 128)
    ks = np.concatenate([r[c]["at_ks"].reshape(NS, DS, 16, 128) for c in range(4)])[None]
    vs = np.concatenate([r[c]["at_vs"].reshape(NS, DS, 16, 128) for c in range(4)])[None]
    hp = np.stack([r[c]["hg_sp"] for c in range(4)])[None]
    hs = np.concatenate([r[c]["hg_ss"] for c in range(4)])[None]
    out = (yp, ys, ssm_p[..., :64], ssm_p[..., 64:], kp, vp, hp, ssm_s[..., :64], ssm_s[..., 64:], ks, vs, hs)
    return tuple(np.ascontiguousarray(o, dtype=f32) for o in out)
```

```python
import numpy as np
import concourse.bass as bass
import concourse.mybir as mybir
from concourse.bass_utils import run_bass_kernel_spmd

F32 = mybir.dt.float32
BF16 = mybir.dt.bfloat16
AF = mybir.ActivationFunctionType
ALU = mybir.AluOpType
AX = mybir.AxisListType

D = 2048
NCH = 16
SEQ = 2048
NS = 8
DS = 64
T = SEQ + NS * DS
G = 512
NTG = T // G
DFF = 8192
DEPTH = 4
ALPHA = (2.0 * DEPTH) ** 0.25
LN_EPS = 1e-5

SAME_ENGINE_SYNC = True


class Sched:
    ENGS = ("pe", "act", "dve", "pool", "sp")

    def __init__(self, nc, ndma=10):
        self.nc = nc
        self.q = {e: [] for e in self.ENGS}
        self.ticks = {e: 0 for e in self.ENGS}
        self.pending = {e: False for e in self.ENGS}
        self.last_w = {}
        self.readers = {}
        self.waited = {e: {} for e in self.ENGS}
        self.ndma = ndma
        self.dma_val = {}
        self.dma_rr = {e: 0 for e in self.ENGS}
        self.all_dma = []

    def _need(self, eng, tk, waits):
        if tk is None:
            return
        if tk[0] == "e":
            _, e2, n = tk
            if e2 == eng and (eng == "pe" or not SAME_ENGINE_SYNC):
                return
            key = ("e", e2)
            val = n
        else:
            _, qn, idx, val = tk
            key = ("d", qn, idx)
        if self.waited[eng].get(key, 0) >= val:
            return
        cur = waits.get(key, 0)
        if val > cur:
            waits[key] = val

    def op(self, eng, fn, reads=(), writes=(), tick=True, dma=False):
        waits = {}
        for r in reads:
            self._need(eng, self.last_w.get(r), waits)
        for w in writes:
            self._need(eng, self.last_w.get(w), waits)
            for tk in self.readers.get(w, ()):
                self._need(eng, tk, waits)
        if dma:
            idx = self.dma_rr[eng]
            self.dma_rr[eng] = (idx + 1) % self.ndma
            prev = self.dma_val.get((eng, idx), 0)
            if prev:
                self._need(eng, ("d", eng, idx, prev), waits)
            val = prev + 16
            self.dma_val[(eng, idx)] = val
            tk = ("d", eng, idx, val)
            inc = ("d", eng, idx)
        else:
            if tick:
                self.ticks[eng] += 1
                tk = ("e", eng, self.ticks[eng])
                inc = ("e", eng)
                self.pending[eng] = False
            else:
                tk = ("e", eng, self.ticks[eng] + 1)
                inc = None
                self.pending[eng] = True
        for key, val in waits.items():
            self.waited[eng][key] = val
        self.q[eng].append((list(waits.items()), fn, inc))
        for w in writes:
            self.last_w[w] = tk
            self.readers[w] = []
        for r in reads:
            lst = self.readers.setdefault(r, [])
            if tk[0] == "e":
                lst[:] = [x for x in lst if not (x[0] == "e" and x[1] == tk[1])]
            lst.append(tk)
        return tk

    def barrier(self):
        for e in self.ENGS:
            waits = {}
            for e2 in self.ENGS:
                if e2 != e and self.ticks[e2] > 0:
                    self._need(e, ("e", e2, self.ticks[e2]), waits)
            for (qn, idx), val in self.dma_val.items():
                self._need(e, ("d", qn, idx, val), waits)
            for key, val in waits.items():
                self.waited[e][key] = val
            self.q[e].append((list(waits.items()), None, None))

    def finish(self, eng="sp"):
        waits = {}
        for (qn, idx), val in self.dma_val.items():
            self._need(eng, ("d", qn, idx, val), waits)
        self.q[eng].append((list(waits.items()), None, None))

    def simulate(self):
        sem = {}
        pc = {e: 0 for e in self.ENGS}
        progress = True
        while progress:
            progress = False
            for e in self.ENGS:
                while pc[e] < len(self.q[e]):
                    waits, fn, inc = self.q[e][pc[e]]
                    ok = all(sem.get(key, 0) >= val for key, val in waits)
                    if not ok:
                        break
                    if inc is not None:
                        k = ("e", inc[1]) if inc[0] == "e" else ("d", inc[1], inc[2])
                        sem[k] = sem.get(k, 0) + (1 if inc[0] == "e" else 16)
                    pc[e] += 1
                    progress = True
        stuck = {e: (pc[e], len(self.q[e])) for e in self.ENGS if pc[e] < len(self.q[e])}
        if stuck:
            msg = []
            for e, (p, n) in stuck.items():
                waits, fn, inc = self.q[e][p]
                msg.append("%s stuck at %d/%d waits=%s have=%s" % (e, p, n, waits, [sem.get(k, 0) for k, v in waits]))
            raise RuntimeError("DEADLOCK: " + " | ".join(msg))

    def emit(self):
        self.simulate()
        nc = self.nc
        for e in self.ENGS:
            assert not self.pending[e], e
            assert self.ticks[e] < 60000, (e, self.ticks[e])
        import contextlib
        with contextlib.ExitStack() as st:
            esem = {e: st.enter_context(nc.semaphore("se_" + e)) for e in self.ENGS}
            dsem = {}
            for (qn, idx) in self.dma_val:
                dsem[(qn, idx)] = st.enter_context(nc.semaphore("sd_%s_%d" % (qn, idx)))
            block = st.enter_context(nc.Block())

            def run(eng_name):
                def body(eng):
                    for waits, fn, inc in self.q[eng_name]:
                        for key, val in waits:
                            s = esem[key[1]] if key[0] == "e" else dsem[(key[1], key[2])]
                            eng.wait_ge(s, val)
                        if fn is None:
                            continue
                        ins = fn(eng)
                        if inc is not None:
                            if inc[0] == "e":
                                ins.then_inc(esem[inc[1]], 1)
                            else:
                                ins.then_inc(dsem[(inc[1], inc[2])], 16)
                return body

            block.tensor(run("pe"))
            block.scalar(run("act"))
            block.vector(run("dve"))
            block.gpsimd(run("pool"))
            block.sync(run("sp"))


class Ctx:
    pass


def R(name, *idx):
    return (name,) + idx


def xr(t0, W, cs=None):
    cs = range(NCH) if cs is None else cs
    return [R("xT", b, c) for b in range(t0 // 256, (t0 + W + 255) // 256) for c in cs]


def build(cfg):
    nc = bass.Bass("TRN2", target_bir_lowering=False)
    S = Sched(nc)
    K = Ctx()
    K.nc, K.S = nc, S
    K.cfg = cfg
    depth = cfg.get("depth", DEPTH)
    kinds = cfg.get("kinds", [l % 3 for l in range(depth)])
    K.na = max(1, sum(1 for k in kinds if k == 0))

    def din(name, shape):
        return nc.dram_tensor(name, list(shape), F32, kind="ExternalInput").ap()

    def dout(name, shape):
        return nc.dram_tensor(name, list(shape), F32, kind="ExternalOutput").ap()

    K.x_in = din("x_in", [T, D])
    K.ffn_w1 = din("ffn_w1", [cfg.get("wdepth", DEPTH), D, DFF])
    K.ffn_w2 = din("ffn_w2", [cfg.get("wdepth", DEPTH), DFF, D])
    K.lnp = din("lnp", [128, 4 * DEPTH * NCH])
    K.y_out = dout("y_out", [T, D])
    if 0 in kinds:
        na = K.na
        K.s5_l1 = din("s5_l1", [na, 3, 128, 128])
        K.s5_rows = din("s5_rows", [na, NCH, 128, 3, 64])
        K.s5_bT = din("s5_bT", [na, NCH, 128, 2, 64])
        K.s5_cT = din("s5_cT", [na, NCH, 128, 8, 16])
        K.s5_d = din("s5_d", [na, 128, NCH])
        K.s5_h0 = din("s5_h0", [na, NS, 128, 128])
        K.s5_wo = din("s5_wo", [na, D, D])
        K.s5_wg = din("s5_wg", [na, D, D])
        K.ssm_p = dout("ssm_p", [na, 128, 128])
        K.ssm_s = dout("ssm_s", [na, NS, 128, 128])

    if 1 in kinds:
        K.at_wqkv = din("at_wqkv", [D, 3 * D])
        K.at_wo = din("at_wo", [D, D])
        K.at_toep = din("at_toep", [128, 16, 2, 128])
        K.at_far = din("at_far", [128, 16])
        K.at_ck = din("at_ck", [NS, 512, D])
        K.at_cv = din("at_cv", [NS, 512, D])
        K.at_kp = dout("at_kp", [512, D])
        K.at_vp = dout("at_vp", [512, D])
        K.at_ks = dout("at_ks", [NS * DS, D])
        K.at_vs = dout("at_vs", [NS * DS, D])
    if 2 in kinds:
        K.hg_win = din("hg_win", [D, 4 * D])
        K.hg_wo = din("hg_wo", [D, D])
        K.hg_lb = din("hg_lb", [128, 4, NCH])
        K.hg_ng = din("hg_ng", [128, 1])
        K.hg_s0 = din("hg_s0", [NS, 16, 128, 128])
        K.hg_sp = dout("hg_sp", [16, 128, 128])
        K.hg_ss = dout("hg_ss", [NS, 16, 128, 128])
    sb = nc.alloc_sbuf_tensor
    K.xT = sb("xT", [128, NCH, T], BF16)
    big0, big1 = nc.bump_sbuf(65536)
    K.big_off = big0
    K.big = nc.alloc_sbuf_tensor_at("big", [128, 32768], BF16, offset=big0)
    K.stg = [sb("stg%d" % i, [128, 2048], F32) for i in range(3)]
    K.wbf = [sb("wbf%d" % i, [128, 2048], BF16) for i in range(2)]
    K.lnp_sb = sb("lnp_sb", [128, 4 * DEPTH * NCH], F32)
    K.ident_f = sb("ident_f", [128, 128], F32)
    K.ident_b = sb("ident_b", [128, 128], BF16)
    K.ones_b = sb("ones_b", [128, 128], BF16)
    K.zsq = [sb("zsq%d" % i, [128, G], BF16) for i in range(2)]
    K.lnA = sb("lnA", [128, G], F32)
    K.lnB = sb("lnB", [128, G], F32)
    K.lnT = [sb("lnT%d" % i, [128, G], F32) for i in range(2)]
    K.otile = [sb("otile%d" % i, [128, 512], F32) for i in range(2)]
    K.eps_t = sb("eps_t", [128, 1], F32)
    sc0, sc1 = nc.bump_sbuf(14336)
    K.scr_off = sc0
    K.scr_size = 14336
    K.ps = [nc.alloc_psum_tensor("ps%d" % i, [128, 512], F32) for i in range(8)]

    K.stg_rr = 0
    K.wbf_rr = 0
    K.ev_rr = 0
    K.uid = 0

    setup_consts(K)
    load_input(K)
    ia = 0
    for li in range(depth):
        kind = kinds[li]
        layer = li + cfg.get("layer0", 0)
        if kind == 0:
            s5_layer(K, layer, ia)
            ia += 1
        elif kind == 1:
            attn_layer(K, layer)
        elif kind == 2:
            hgrn_layer(K, layer)
        else:
            mixer_none(K, layer)
        if cfg.get("ffn", True):
            ffn(K, layer, last=(li == depth - 1) and not cfg.get("dbg_xT"))
    if cfg.get("dbg_xT"):
        dbg = nc.dram_tensor("dbg", [128, NCH * T], BF16, kind="ExternalOutput").ap()
        S.op("sp", lambda e: e.dma_start(out=dbg, in_=K.xT[:, :, :].rearrange("p c t -> p (c t)")),
             reads=xr(0, T), dma=True)
    if cfg.get("dbg_big"):
        dbg2 = nc.dram_tensor("dbg2", [128, 32768], BF16, kind="ExternalOutput").ap()
        S.op("sp", lambda e: e.dma_start(out=dbg2, in_=K.big[:, :]),
             reads=[R("hT", c) for c in range(64)] + [R("yT", c) for c in range(16)], dma=True)
    S.finish("sp")
    S.emit()
    return nc


def alloc_at(K, name, shape, dtype, off):
    K.uid += 1
    return K.nc.alloc_sbuf_tensor_at("%s_%d" % (name, K.uid), list(shape), dtype, offset=off)


class Arena:
    def __init__(self, K, base, size):
        self.K, self.base, self.size, self.cur = K, base, size, 0

    def get(self, name, shape, dtype):
        n = 1
        for d in shape[1:]:
            n *= d
        nbytes = n * (4 if dtype == F32 else 2)
        nbytes = (nbytes + 31) // 32 * 32
        assert self.cur + nbytes <= self.size, (name, self.cur, nbytes, self.size)
        t = alloc_at(self.K, name, shape, dtype, self.base + self.cur)
        self.cur += nbytes
        return t


def setup_consts(K):
    nc, S = K.nc, K.S
    S.op("pool", lambda e: e.memset(K.ident_f[:, :], 0.0), writes=[R("ident_f")])
    S.op("pool", lambda e: e.affine_select(out=K.ident_f[:, :], in_=K.ident_f[:, :], pattern=[[1, 128]],
                                            compare_op=ALU.not_equal, fill=1.0, base=0, channel_multiplier=-1),
         reads=[R("ident_f")], writes=[R("ident_f")])
    S.op("pool", lambda e: e.tensor_copy(out=K.ident_b[:, :], in_=K.ident_f[:, :]), reads=[R("ident_f")], writes=[R("ident_b")])
    S.op("pool", lambda e: e.memset(K.ones_b[:, :], 1.0), writes=[R("ones_b")])
    S.op("pool", lambda e: e.memset(K.eps_t[:, :], LN_EPS), writes=[R("eps_t")])
    S.op("sp", lambda e: e.dma_start(out=K.lnp_sb[:, :], in_=K.lnp), writes=[R("lnp")], dma=True)


def next_stg(K):
    i = K.stg_rr
    K.stg_rr = (i + 1) % len(K.stg)
    return i


def next_wbf(K):
    i = K.wbf_rr
    K.wbf_rr = (i + 1) % len(K.wbf)
    return i


def evac_engine(K):
    K.ev_rr ^= 1
    return "act" if K.ev_rr else "dve"


def copy_op(K, eng, out, in_, reads, writes):
    if eng == "act":
        return K.S.op("act", lambda e: e.activation(out=out, in_=in_, func=AF.Copy), reads=reads, writes=writes)
    return K.S.op(eng, lambda e: e.tensor_copy(out=out, in_=in_), reads=reads, writes=writes)


def load_input(K):
    S = K.S
    for tt in range(T // 128):
        si = next_stg(K)
        stg = K.stg[si]
        S.op("sp", lambda e, stg=stg, tt=tt: e.dma_start(out=stg[:, :], in_=K.x_in[tt * 128:(tt + 1) * 128, :]),
             writes=[R("stg", si)], dma=True)
        for cb in range(4):
            bank = 4 + (tt * 4 + cb) % 4
            ps = K.ps[bank]
            for j in range(4):
                c = cb * 4 + j
                S.op("pe", lambda e, ps=ps, stg=stg, c=c, j=j: e.transpose(out=ps[:, j * 128:(j + 1) * 128],
                                                                       in_=stg[:, c * 128:(c + 1) * 128],
                                                                       identity=K.ident_f[:, :]),
                     reads=[R("stg", si), R("ident_f")], writes=[R("ps", bank)], tick=(j == 3))
            dst = K.xT[:, cb * 4:(cb + 1) * 4, tt * 128:(tt + 1) * 128]
            src = ps[:, :].rearrange("p (j t) -> p j t", j=4)
            copy_op(K, evac_engine(K), dst, src, [R("ps", bank)], xr(tt * 128, 128, range(cb * 4, cb * 4 + 4)))


def load_w_tile(K, dram_ap):
    S = K.S
    si = next_stg(K)
    wi = next_wbf(K)
    stg = K.stg[si]
    wbf = K.wbf[wi]
    S.op("sp", lambda e: e.dma_start(out=stg[:, :].rearrange("p (k n) -> p k n", k=8), in_=dram_ap), writes=[R("stg", si)], dma=True)
    K.cast_rr = getattr(K, "cast_rr", 0) + 1
    copy_op(K, "act" if K.cast_rr % 2 else "dve", wbf[:, :], stg[:, :], [R("stg", si)], [R("wbf", wi)])
    return wi


def wtile(w2d, kt, ob):
    return w2d[kt * 1024:(kt + 1) * 1024, ob * 256:(ob + 1) * 256].rearrange("(k p) n -> p k n", p=128)


def dense_block(K, w2d, ob, KC, rhs_fn, rhs_res, ncols, banks):
    S = K.S
    nkt = KC // 8
    for kt in range(nkt):
        wi = load_w_tile(K, wtile(w2d, kt, ob))
        wbf = K.wbf[wi]
        for o in range(2):
            ps = K.ps[banks[o]]
            for k in range(8):
                kc = kt * 8 + k
                first = (kt == 0 and k == 0)
                last = (kt == nkt - 1 and k == 7)
                rhs_ap = rhs_fn(kc)
                S.op("pe", lambda e, ps=ps, wbf=wbf, k=k, o=o, rhs_ap=rhs_ap, first=first, last=last:
                     e.matmul(ps[:, 0:ncols], lhsT=wbf[:, k * 256 + o * 128:k * 256 + (o + 1) * 128], rhs=rhs_ap,
                              start=first, stop=last),
                     reads=[R("wbf", wi)] + rhs_res(kc), writes=[R("ps", banks[o])], tick=(last or (o == 1 and k == 7)))


def ln_accum(K, t0, W, c):
    S = K.S
    zc = K.xT[:, c, t0:t0 + W]
    zq = K.zsq[c % 2]
    S.op("act", lambda e: e.activation(out=zq[:, 0:W], in_=zc, func=AF.Square), reads=xr(t0, W, [c]), writes=[R("zsq", c % 2)])
    S.op("pe", lambda e: e.matmul(K.ps[6][:, 0:W], lhsT=K.ones_b[:, :], rhs=zc, start=(c == 0), stop=(c == NCH - 1)),
         reads=xr(t0, W, [c]) + [R("ones_b")], writes=[R("ps", 6)], tick=(c == NCH - 1))
    S.op("pe", lambda e: e.matmul(K.ps[7][:, 0:W], lhsT=K.ones_b[:, :], rhs=zq[:, 0:W], start=(c == 0), stop=(c == NCH - 1)),
         reads=[R("zsq", c % 2), R("ones_b")], writes=[R("ps", 7)], tick=True)


def resid_ln_accum(K, t0, W, c, m_ap, m_res):
    S = K.S
    zc = K.xT[:, c, t0:t0 + W]
    S.op("dve", lambda e: e.scalar_tensor_tensor(out=zc, in0=zc, scalar=ALPHA, in1=m_ap, op0=ALU.mult, op1=ALU.add),
         reads=xr(t0, W, [c]) + m_res, writes=xr(t0, W, [c]))
    ln_accum(K, t0, W, c)


def ln_finish(K, layer, kind, t0, W, last):
    S = K.S
    inv = 1.0 / D
    A, B = K.lnA[:, 0:W], K.lnB[:, 0:W]
    S.op("dve", lambda e: e.tensor_scalar(out=A, in0=K.ps[6][:, 0:W], scalar1=inv, scalar2=None, op0=ALU.mult),
         reads=[R("ps", 6)], writes=[R("lnA")])
    S.op("dve", lambda e: e.tensor_tensor(out=B, in0=A, in1=A, op=ALU.mult), reads=[R("lnA")], writes=[R("lnB")])
    S.op("dve", lambda e: e.scalar_tensor_tensor(out=B, in0=K.ps[7][:, 0:W], scalar=inv, in1=B, op0=ALU.mult, op1=ALU.subtract),
         reads=[R("ps", 7), R("lnB")], writes=[R("lnB")])
    S.op("act", lambda e: e.activation(out=B, in_=B, func=AF.Sqrt, bias=K.eps_t[:, 0:1], scale=1.0),
         reads=[R("lnB"), R("eps_t")], writes=[R("lnB")])
    S.op("dve", lambda e: e.reciprocal(out=B, in_=B), reads=[R("lnB")], writes=[R("lnB")])
    S.op("dve", lambda e: e.tensor_tensor(out=A, in0=A, in1=B, op=ALU.mult), reads=[R("lnA"), R("lnB")], writes=[R("lnA")])
    gi = (2 * kind) * DEPTH * NCH + layer * NCH
    bi = (2 * kind + 1) * DEPTH * NCH + layer * NCH
    for c in range(NCH):
        tmp = K.lnT[c % 2][:, 0:W]
        tr = R("lnT", c % 2)
        zc = K.xT[:, c, t0:t0 + W]
        S.op("dve", lambda e, tmp=tmp, zc=zc: e.tensor_tensor(out=tmp, in0=zc, in1=B, op=ALU.mult),
             reads=xr(t0, W, [c]) + [R("lnB")], writes=[tr])
        S.op("dve", lambda e, tmp=tmp: e.tensor_tensor(out=tmp, in0=tmp, in1=A, op=ALU.subtract),
             reads=[tr, R("lnA")], writes=[tr])
        S.op("act", lambda e, tmp=tmp, zc=zc, c=c: e.activation(out=zc, in_=tmp, func=AF.Identity,
                                                                bias=K.lnp_sb[:, bi + c:bi + c + 1],
                                                                scale=K.lnp_sb[:, gi + c:gi + c + 1]),
             reads=[tr, R("lnp")], writes=xr(t0, W, [c]))
        if last:
            ot = K.otile[c % 2]
            otr = R("otile", c % 2)
            S.op("act", lambda e, tmp=tmp, c=c: e.activation(out=tmp, in_=tmp, func=AF.Identity,
                                                          bias=K.lnp_sb[:, bi + c:bi + c + 1],
                                                          scale=K.lnp_sb[:, gi + c:gi + c + 1]),
                 reads=[tr, R("lnp")], writes=[tr])
            bank = 4 + c % 2
            nj = W // 128
            for j in range(nj):
                S.op("pe", lambda e, tmp=tmp, j=j, bank=bank: e.transpose(out=K.ps[bank][:, j * 128:(j + 1) * 128],
                                                                          in_=tmp[:, j * 128:(j + 1) * 128],
                                                                          identity=K.ident_f[:, :]),
                     reads=[tr, R("ident_f")], writes=[R("ps", bank)], tick=(j == nj - 1))
            S.op("dve", lambda e, ot=ot, bank=bank: e.tensor_copy(out=ot[:, 0:W], in_=K.ps[bank][:, 0:W]),
                 reads=[R("ps", bank)], writes=[otr])
            dst = K.y_out[t0:t0 + W, c * 128:(c + 1) * 128].rearrange("(j p) f -> p j f", p=128)
            S.op("pool", lambda e, ot=ot, dst=dst, nj=nj: e.dma_start(out=dst, in_=ot[:, 0:W].rearrange("p (j f) -> p j f", j=nj)),
                 reads=[otr], writes=[R("yout", t0, c)], dma=True)


def mixer_none(K, layer):
    S = K.S
    for tg in range(NTG):
        t0 = tg * G
        for c in range(NCH):
            zc = K.xT[:, c, t0:t0 + G]
            S.op("dve", lambda e, zc=zc: e.tensor_scalar(out=zc, in0=zc, scalar1=ALPHA, scalar2=None, op0=ALU.mult),
                 reads=xr(t0, G, [c]), writes=xr(t0, G, [c]))
            ln_accum(K, t0, G, c)
        ln_finish(K, layer, 0, t0, G, False)


def ffn(K, layer, last):
    S = K.S
    hT = K.big[:, :].rearrange("p (c t) -> p c t", c=64)
    for tg in range(NTG):
        t0 = tg * G
        for ob in range(DFF // 256):
            banks = [2 * (ob % 2), 2 * (ob % 2) + 1]
            dense_block(K, K.ffn_w1[layer], ob, NCH, lambda kc: K.xT[:, kc, t0:t0 + G], lambda kc: xr(t0, G, [kc]), G, banks)
            for o in range(2):
                relu2(K, ob * 2 + o, banks[o], hT)
        for ob in range(D // 256):
            banks = [2 * (ob % 2), 2 * (ob % 2) + 1]
            dense_block(K, K.ffn_w2[layer], ob, 64, lambda kc: hT[:, kc, :], lambda kc: [R("hT", kc)], G, banks)
            for o in range(2):
                resid_ln_accum(K, t0, G, ob * 2 + o, K.ps[banks[o]][:, 0:G], [R("ps", banks[o])])
        ln_finish(K, layer, 1, t0, G, last)


def relu2(K, oc, bank, hT):
    S = K.S
    tmp = K.lnT[oc % 2]
    S.op("act", lambda e: e.activation(out=tmp[:, :], in_=K.ps[bank][:, :], func=AF.Relu),
         reads=[R("ps", bank)], writes=[R("lnT", oc % 2)])
    S.op("act", lambda e: e.activation(out=hT[:, oc, :], in_=tmp[:, :], func=AF.Square),
         reads=[R("lnT", oc % 2)], writes=[R("hT", oc)])


S5_POW = [1, 2, 3, 4, 5, 6, 7, 8, 16, 32, 64, 128, 256, 512, 1024]


def tt(K, eng, out, a, b, op, reads, writes):
    return K.S.op(eng, lambda e: e.tensor_tensor(out=out, in0=a, in1=b, op=op), reads=reads, writes=writes)


def cmul(K, eng, outr, outi, ar, ai, br, bi, t1, t2, rr, ww):
    tr = [R("s5tmp")]
    tt(K, eng, t1, ar, br, ALU.mult, rr, tr)
    tt(K, eng, t2, ai, bi, ALU.mult, rr, tr)
    tt(K, eng, t2, t1, t2, ALU.subtract, tr, tr)
    tt(K, eng, t1, ar, bi, ALU.mult, rr, tr)
    tt(K, eng, outi, ai, br, ALU.mult, rr + tr, ww)
    tt(K, eng, outi, outi, t1, ALU.add, ww + tr, ww)
    K.S.op(eng, lambda e: e.tensor_copy(out=outr, in_=t2), reads=tr, writes=ww)


def cexp_abar(K, eng, ar_t, ai_t, ldt_t, outr, outi, tmps, rr, ww):
    S = K.S
    d, c, s = tmps
    tr = [R("s5tmp2")]
    S.op("act", lambda e: e.activation(out=d, in_=ldt_t, func=AF.Exp), reads=rr, writes=tr)
    tt(K, eng, c, ai_t, d, ALU.mult, rr + tr, tr)
    tt(K, eng, d, ar_t, d, ALU.mult, rr + tr, tr)
    S.op("act", lambda e: e.activation(out=d, in_=d, func=AF.Exp), reads=tr, writes=tr)
    S.op("act", lambda e: e.activation(out=s, in_=c, func=AF.Sin, scale=1.0 / 16), reads=tr, writes=tr)
    S.op("act", lambda e: e.activation(out=c, in_=c, func=AF.Sin, scale=-1.0 / 16, bias=K.halfpi[:, 0:1]), reads=tr + [R("consts5")], writes=tr)
    for _ in range(4):
        tt(K, eng, outr, c, s, ALU.mult, tr, ww)
        tt(K, eng, c, c, c, ALU.mult, tr, tr)
        tt(K, eng, s, s, s, ALU.mult, tr, tr)
        tt(K, eng, c, c, s, ALU.subtract, tr, tr)
        S.op(eng, lambda e: e.tensor_scalar(out=s, in0=outr, scalar1=2.0, scalar2=None, op0=ALU.mult), reads=ww, writes=tr)
    tt(K, eng, outr, c, d, ALU.mult, tr, ww)
    tt(K, eng, outi, s, d, ALU.mult, tr, ww)


def s5_layer(K, layer, ia):
    nc, S = K.nc, K.S
    S.barrier()
    bg = Arena(K, K.big_off, 65536)
    yT = bg.get("yT", [128, NCH, 1024], BF16)
    PW = bg.get("PW", [128, len(S5_POW), 2, 128], F32)
    bT = bg.get("bT", [128, 8, 128], F32)
    H0 = bg.get("H0", [128, NS, 128], F32)
    FINs = bg.get("FINs", [128, NS, 128], F32)
    Hb = [bg.get("Hb%d" % i, [128, 1024], BF16) for i in range(2)]
    sc = Arena(K, K.scr_off, K.scr_size)
    P = sc.get("P", [128, 4, 132], F32)
    Sb = sc.get("Sb", [128, 4, 128], BF16)
    CAR = sc.get("CAR", [128, 128], F32)
    FINp = sc.get("FINp", [128, 128], F32)
    Ct = sc.get("Ct", [128, 8, 128], BF16)
    cst = sc.get("cst", [128, 8, 16], F32)
    rt = [sc.get("rt%d" % i, [128, 64], F32) for i in range(12)]
    NTM = 7
    tmpm = [sc.get("tmpm%d" % i, [128, 128], F32) for i in range(NTM)]
    tm_rr = [0]
    Jsg = sc.get("Jsg", [128, 128], F32)
    rowmask = sc.get("rowmask", [128, 8], F32)
    dcol = sc.get("dcol", [128, NCH], F32)
    K.halfpi = sc.get("halfpi", [128, 1], F32)
    l1 = [K.lnT[0][:, 0:128], K.lnT[0][:, 128:256], K.lnT[0][:, 256:384], K.lnT[1][:, 0:128], K.lnT[1][:, 128:256], K.lnT[1][:, 256:384],
          K.lnT[0][:, 384:512], K.lnT[1][:, 384:512]]
    l1r = [R("lnT", 0), R("lnT", 1)]
    CS = [R("consts5")]

    S.op("pool", lambda e: e.memset(K.halfpi[:, :], float(np.pi / 2)), writes=CS)
    S.op("pool", lambda e: e.memset(Jsg[:, :], 0.0), writes=CS)
    S.op("pool", lambda e: e.affine_select(out=Jsg[:, :], in_=Jsg[:, :], pattern=[[1, 128]], compare_op=ALU.not_equal,
                                            fill=1.0, base=-64, channel_multiplier=-1), reads=CS, writes=CS)
    S.op("pool", lambda e: e.affine_select(out=Jsg[:, :], in_=Jsg[:, :], pattern=[[1, 128]], compare_op=ALU.not_equal,
                                            fill=-1.0, base=64, channel_multiplier=-1), reads=CS, writes=CS)
    S.op("pool", lambda e: e.memset(rowmask[:, :], 1.0), writes=CS)
    S.op("pool", lambda e: e.affine_select(out=rowmask[:, :], in_=rowmask[:, :], pattern=[[-16, 8]], compare_op=ALU.is_ge,
                                            fill=0.0, base=0, channel_multiplier=1), reads=CS, writes=CS)
    S.op("pool", lambda e: e.affine_select(out=rowmask[:, :], in_=rowmask[:, :], pattern=[[16, 8]], compare_op=ALU.is_ge,
                                            fill=0.0, base=15, channel_multiplier=-1), reads=CS, writes=CS)
    S.op("pool", lambda e: e.memset(Ct[:, :, :], 0.0), writes=[R("Ct")])
    S.op("sp", lambda e: e.dma_start(out=dcol[:, :], in_=K.s5_d[ia]), writes=[R("dcol")], dma=True)

    for i in range(3):
        S.op("sp", lambda e, i=i: e.dma_start(out=l1[i], in_=K.s5_l1[ia, i]), writes=l1r, dma=True)

    def pw(n, ri):
        return PW[:, S5_POW.index(n), ri, :]
    PWR = [R("PW")]
    cexp_abar(K, "dve", l1[0], l1[1], l1[2], pw(1, 0), pw(1, 1), [l1[3], l1[4], l1[5]], l1r, PWR)
    for n in range(2, 9):
        cmul(K, "dve", pw(n, 0), pw(n, 1), pw(n - 1, 0), pw(n - 1, 1), pw(1, 0), pw(1, 1), l1[6], l1[7], PWR + l1r, PWR)
    for n in S5_POW[8:]:
        cmul(K, "dve", pw(n, 0), pw(n, 1), pw(n // 2, 0), pw(n // 2, 1), pw(n // 2, 0), pw(n // 2, 1), l1[6], l1[7], PWR + l1r, PWR)

    si = next_stg(K)
    stg = K.stg[si]
    S.op("sp", lambda e: e.dma_start(out=stg[:, 0:NS * 128].rearrange("g (s k) -> g s k", s=NS),
                                     in_=K.s5_h0[ia].rearrange("s g k -> g s k")), writes=[R("stg", si)], dma=True)
    for s in range(NS):
        bank = 4 + s % 2
        S.op("pe", lambda e, s=s, bank=bank: e.transpose(out=K.ps[bank][:, 0:128], in_=stg[:, s * 128:(s + 1) * 128], identity=K.ident_f[:, :]),
             reads=[R("stg", si), R("ident_f")], writes=[R("ps", bank)])
        copy_op(K, evac_engine(K), H0[:, s, :], K.ps[bank][:, 0:128], [R("ps", bank)], [R("H0")])

    passes = [(0, 1024, "A"), (1024, 1024, "B"), (2048, 512, "C")]
    if K.cfg.get("s5_passes"):
        passes = K.cfg["s5_passes"]
    mats_res = [[R("stg", 0)], [R("stg", 1)], [R("stg", 2)], [R("wbf", 0), R("wbf", 1)]]

    def mats(i):
        if i < 3:
            mb = K.stg[i][:, 0:1024].bitcast(BF16).rearrange("p (m c) -> p m c", m=16)
            mf = K.stg[i][:, 1024:2048].rearrange("p (m c) -> p m c", m=8)
        else:
            mb = K.wbf[0][:, :].rearrange("p (m c) -> p m c", m=16)
            mf = K.wbf[1][:, :].bitcast(F32).rearrange("p (m c) -> p m c", m=8)
        return mb, mf

    for (t0, W, pk) in passes:
        nch = W // 8
        ncol = (nch + 1) if pk != "C" else 9
        for fc in range(NCH):
            RT = [R("rt")]
            si = next_stg(K)
            stg2 = K.stg[si]
            S.op("sp", lambda e, stg2=stg2, fc=fc: e.dma_start(out=stg2[:, 0:192].rearrange("p (i k) -> p i k", i=3), in_=K.s5_rows[ia, fc]),
                 writes=[R("stg", si)], dma=True)
            S.op("sp", lambda e, stg2=stg2, fc=fc: e.dma_start(out=stg2[:, 192:320].rearrange("p (i k) -> p i k", i=2), in_=K.s5_bT[ia, fc]),
                 writes=[R("stg", si)], dma=True)
            S.op("sp", lambda e, fc=fc: e.dma_start(out=cst[:, :, :], in_=K.s5_cT[ia, fc]), writes=[R("cst")], dma=True)
            lam_r, lam_i, ldt = stg2[:, 0:64], stg2[:, 64:128], stg2[:, 128:192]
            b_r, b_i = stg2[:, 192:256], stg2[:, 256:320]
            SG = [R("stg", si)]
            abr, abi = rt[0], rt[1]
            cexp_abar(K, "dve", lam_r, lam_i, ldt, abr[:, :], abi[:, :], [rt[2][:, :], rt[3][:, :], rt[4][:, :]], SG, RT)
            nr, den, cr, ci = rt[2][:, :], rt[3][:, :], rt[5][:, :], rt[6][:, :]
            t1, t2 = rt[7][:, :], rt[8][:, :]
            S.op("dve", lambda e: e.tensor_scalar(out=nr, in0=abr[:, :], scalar1=-1.0, scalar2=None, op0=ALU.add), reads=RT, writes=RT)
            tt(K, "dve", den, lam_r, lam_r, ALU.mult, SG, RT)
            tt(K, "dve", t1, lam_i, lam_i, ALU.mult, SG, RT)
            tt(K, "dve", den, den, t1, ALU.add, RT, RT)
            S.op("dve", lambda e: e.reciprocal(out=den, in_=den), reads=RT, writes=RT)
            tt(K, "dve", cr, nr, lam_r, ALU.mult, RT + SG, RT)
            tt(K, "dve", t1, abi[:, :], lam_i, ALU.mult, RT + SG, RT)
            tt(K, "dve", cr, cr, t1, ALU.add, RT, RT)
            tt(K, "dve", cr, cr, den, ALU.mult, RT, RT)
            tt(K, "dve", ci, abi[:, :], lam_r, ALU.mult, RT + SG, RT)
            tt(K, "dve", t1, nr, lam_i, ALU.mult, RT + SG, RT)
            tt(K, "dve", ci, ci, t1, ALU.subtract, RT, RT)
            tt(K, "dve", ci, ci, den, ALU.mult, RT, RT)
            BT = [R("bT")]
            cmul(K, "dve", bT[:, 0, 0:64], bT[:, 0, 64:128], cr, ci, b_r, b_i, t1, t2, RT + SG, BT)
            for tau in range(1, 8):
                cmul(K, "dve", bT[:, tau, 0:64], bT[:, tau, 64:128], bT[:, tau - 1, 0:64], bT[:, tau - 1, 64:128],
                     abr[:, :], abi[:, :], t1, t2, RT + BT, BT)
            for g8 in range(8):
                S.op("pool", lambda e, g8=g8: e.tensor_copy(out=Ct[0:64, g8, 16 * g8:16 * g8 + 16], in_=cst[0:64, g8, :]),
                     reads=[R("cst")], writes=[R("Ct")])
                S.op("pool", lambda e, g8=g8: e.tensor_scalar(out=Ct[64:128, g8, 16 * g8:16 * g8 + 16], in0=cst[64:128, g8, :],
                                                              scalar1=-1.0, scalar2=None, op0=ALU.mult),
                     reads=[R("cst")], writes=[R("Ct")])

            ui = fc % 2
            uD = K.otile[ui][:, :].bitcast(BF16)
            uDv = uD[:, 0:W].rearrange("p (j c) -> p j c", j=8)
            S.op("pool", lambda e, uDv=uDv, fc=fc, t0=t0, W=W: e.tensor_copy(out=uDv, in_=K.xT[:, fc, t0:t0 + W].rearrange("p (c j) -> p j c", j=8)),
                 reads=xr(t0, W, [fc]), writes=[R("otile", ui)])

            def uT(s, uDv=uDv):
                return uDv[:, s, :]
            ures = [R("otile", ui)]
            ybanks = [5, 6]
            for b4 in range(2):
                g0 = fc * 8 + b4 * 4
                for i in range(4):
                    g = g0 + i
                    mb, mf = mats(i)
                    for n in range(1, 9):
                        ti = tm_rr[0] % NTM
                        tm_rr[0] += 1
                        tm = tmpm[ti]
                        S.op("act", lambda e, tm=tm, n=n, g=g: e.activation(out=tm[:, :], in_=Jsg[:, :], func=AF.Copy, scale=pw(n, 1)[:, g:g + 1]),
                             reads=PWR + CS, writes=[R("tmpm", ti)])
                        S.op("dve", lambda e, tm=tm, n=n, g=g, mb=mb: e.scalar_tensor_tensor(out=mb[:, n - 1, :], in0=K.ident_f[:, :], scalar=pw(n, 0)[:, g:g + 1],
                                                                                               in1=tm[:, :], op0=ALU.mult, op1=ALU.add),
                             reads=PWR + [R("tmpm", ti), R("ident_f")], writes=mats_res[i] + ([R("matsG", i)] if n == 1 else []))
                    for k in range(8):
                        if (1 << k) >= ncol:
                            continue
                        n = 8 << k
                        ti = tm_rr[0] % NTM
                        tm_rr[0] += 1
                        tm = tmpm[ti]
                        S.op("act", lambda e, tm=tm, n=n, g=g: e.activation(out=tm[:, :], in_=Jsg[:, :], func=AF.Copy, scale=pw(n, 1)[:, g:g + 1]),
                             reads=PWR + CS, writes=[R("tmpm", ti)])
                        S.op("dve", lambda e, tm=tm, n=n, g=g, mf=mf, k=k: e.scalar_tensor_tensor(out=mf[:, k, :], in0=K.ident_f[:, :], scalar=pw(n, 0)[:, g:g + 1],
                                                                                                    in1=tm[:, :], op0=ALU.mult, op1=ALU.add),
                             reads=PWR + [R("tmpm", ti), R("ident_f")], writes=mats_res[i])
                    S.op("act", lambda e, g=g, mb=mb: e.activation(out=mb[:, 8:16, :], in_=bT[:, :, :], func=AF.Copy, scale=rowmask[:, g % 8:g % 8 + 1]),
                         reads=BT + CS + [R("matsG", i)], writes=[R("matsL", i)])
                for i in range(4):
                    mb, mf = mats(i)
                    bank = 3 + i // 2
                    off = (i % 2) * 256
                    for s in range(8):
                        u_ap = uT(s)
                        o_ap = K.ps[bank][:, off:off + nch]
                        S.op("pe", lambda e, mb=mb, s=s, u_ap=u_ap, o_ap=o_ap: e.matmul(o_ap, lhsT=mb[:, 8 + 7 - s, :], rhs=u_ap,
                                                                                                 start=(s == 0), stop=(s == 7)),
                             reads=mats_res[i] + [R("matsL", i)] + ures, writes=[R("ps", bank)], tick=(s == 7))
                for b in range(2):
                    bank = 3 + b
                    src = K.ps[bank][:, :].rearrange("p (i c) -> p i c", i=2)[:, :, 0:nch]
                    if pk != "C":
                        dst = P[:, 2 * b:2 * b + 2, 1:1 + nch]
                    else:
                        dst = P[:, 2 * b:2 * b + 2, 0:72].rearrange("p i (s c) -> p i s c", c=9)[:, :, :, 1:9]
                        src = K.ps[bank][:, :].rearrange("p (i s c) -> p i s c", i=2, c=8)[:, :, 0:8, :]
                    copy_op(K, "act", dst, src, [R("ps", bank)], [R("P")])
                if pk == "A":
                    S.op("pool", lambda e: e.memset(P[:, :, 0:1], 0.0), writes=[R("P")])
                elif pk == "B":
                    S.op("pool", lambda e, g0=g0: e.tensor_copy(out=P[:, :, 0], in_=CAR[:, g0:g0 + 4]), reads=[R("CAR")], writes=[R("P")])
                else:
                    S.op("pool", lambda e, g0=g0: e.tensor_copy(out=P[:, :, 0:72].rearrange("p i (s c) -> p i s c", c=9)[:, :, :, 0],
                                                               in_=H0[:, :, g0:g0 + 4].rearrange("p s i -> p i s")), reads=[R("H0")], writes=[R("P")])
                k = 0
                while (1 << k) < ncol:
                    sh = 1 << k
                    for i in range(4):
                        mb, mf = mats(i)
                        bank = 3 + i // 2
                        off = (i % 2) * 256
                        if pk != "C":
                            rhs = P[:, i, 0:ncol - sh]
                            out = K.ps[bank][:, off:off + ncol - sh]
                        else:
                            rhs = P[:, i, 0:72].rearrange("p (s c) -> p s c", c=9)[:, :, 0:9 - sh]
                            out = K.ps[bank][:, off:off + 8 * (9 - sh)].rearrange("p (s c) -> p s c", s=8)
                        S.op("pe", lambda e, mf=mf, k=k, rhs=rhs, out=out: e.matmul(out, lhsT=mf[:, k, :], rhs=rhs, start=True, stop=True),
                             reads=mats_res[i] + [R("P")], writes=[R("ps", bank)], tick=(i % 2 == 1))
                    for b in range(2):
                        bank = 3 + b
                        if pk != "C":
                            dst = P[:, 2 * b:2 * b + 2, sh:ncol]
                            src = K.ps[bank][:, :].rearrange("p (i c) -> p i c", i=2)[:, :, 0:ncol - sh]
                        else:
                            dst = P[:, 2 * b:2 * b + 2, 0:72].rearrange("p i (s c) -> p i s c", c=9)[:, :, :, sh:9]
                            src = K.ps[bank][:, :].rearrange("p (i x) -> p i x", i=2)[:, :, 0:8 * (9 - sh)].rearrange("p i (s c) -> p i s c", s=8)
                        S.op("dve", lambda e, dst=dst, src=src: e.tensor_tensor(out=dst, in0=dst, in1=src, op=ALU.add),
                             reads=[R("P"), R("ps", bank)], writes=[R("P")])
                    k += 1
                if pk != "C":
                    copy_op(K, "act", Sb[:, :, 0:nch], P[:, :, 0:nch], [R("P")], [R("Sb")])
                    dstc = (CAR if pk == "A" else FINp)[:, g0:g0 + 4]
                    copy_op(K, "pool", dstc, P[:, :, nch], [R("P")], [R("CAR")] if pk == "A" else [R("FINp")])
                else:
                    pv = P[:, :, 0:72].rearrange("p i (s c) -> p i s c", c=9)
                    copy_op(K, "act", Sb[:, :, 0:64].rearrange("p i (s c) -> p i s c", c=8), pv[:, :, :, 0:8], [R("P")], [R("Sb")])
                    copy_op(K, "pool", FINs[:, :, g0:g0 + 4].rearrange("p s i -> p i s"), pv[:, :, :, 8], [R("P")], [R("FINs")])
                for i in range(4):
                    g = g0 + i
                    g8 = g % 8
                    mb, mf = mats(i)
                    hb = Hb[i % 2]
                    hres = [R("Hb", i % 2)]
                    hv = hb[:, 0:W].rearrange("p (j c) -> p j c", j=8)
                    xv = uDv
                    bank2 = [(0, 1), (2, 7)][i % 2]
                    for hlf in range(2):
                        bank = bank2[hlf]
                        pv = K.ps[bank][:, :].rearrange("p (j c) -> p j c", j=4)
                        jlo = 4 * hlf
                        for d in range(jlo + 4):
                            ja = max(jlo, d)
                            rhs_ap = xv[:, ja - d:jlo + 4 - d, :]
                            o_ap = pv[:, ja - jlo:4, 0:nch]
                            S.op("pe", lambda e, mb=mb, d=d, rhs_ap=rhs_ap, o_ap=o_ap: e.matmul(o_ap, lhsT=mb[:, 8 + d, :], rhs=rhs_ap,
                                                                                                     start=(d == 0), stop=False),
                                 reads=mats_res[i] + [R("matsL", i)] + ures, writes=[R("ps", bank)], tick=False)
                        for j in range(jlo, jlo + 4):
                            o_ap = pv[:, j - jlo, 0:nch]
                            sb_ap = Sb[:, i, 0:nch]
                            S.op("pe", lambda e, mb=mb, j=j, o_ap=o_ap, sb_ap=sb_ap: e.matmul(o_ap, lhsT=mb[:, j, :], rhs=sb_ap,
                                                                                               start=False, stop=True),
                                 reads=mats_res[i] + [R("Sb")], writes=[R("ps", bank)], tick=(j == jlo + 3))
                        copy_op(K, evac_engine(K), hv[:, jlo:jlo + 4, :], pv[:, :, 0:nch], [R("ps", bank)], hres)
                    for tb in range(W // 512):
                        S.op("pe", lambda e, g8=g8, hb=hb, tb=tb: e.matmul(K.ps[ybanks[tb]][:, :], lhsT=Ct[:, g8, :], rhs=hb[:, tb * 512:(tb + 1) * 512],
                                                                          start=(g8 == 0), stop=(g8 == 7)),
                             reads=[R("Ct")] + hres, writes=[R("ps", ybanks[tb])], tick=True)
            for tb in range(W // 512):
                bank = ybanks[tb]
                ua = uD[:, tb * 512:(tb + 1) * 512]
                y1, t2_ = K.lnT[0][:, :], K.lnT[1][:, :]
                S.op("dve", lambda e, ua=ua, bank=bank, fc=fc: e.scalar_tensor_tensor(out=y1, in0=ua, scalar=dcol[:, fc:fc + 1], in1=K.ps[bank][:, :],
                                                                                      op0=ALU.mult, op1=ALU.add),
                     reads=ures + [R("dcol"), R("ps", bank)], writes=[R("lnT", 0)])
                S.op("act", lambda e: e.activation(out=t2_, in_=y1, func=AF.Square), reads=[R("lnT", 0)], writes=[R("lnT", 1)])
                S.op("dve", lambda e: e.tensor_scalar(out=t2_, in0=t2_, scalar1=0.044715, scalar2=1.0, op0=ALU.mult, op1=ALU.add),
                     reads=[R("lnT", 1)], writes=[R("lnT", 1)])
                tt(K, "dve", t2_, t2_, y1, ALU.mult, [R("lnT", 0), R("lnT", 1)], [R("lnT", 1)])
                S.op("act", lambda e: e.activation(out=t2_, in_=t2_, func=AF.Sigmoid, scale=1.5957691216057308), reads=[R("lnT", 1)], writes=[R("lnT", 1)])
                nj = 512 // nch
                yo = yT[:, fc, 0:W].rearrange("p (c j) -> p j c", j=8)[:, tb * nj:(tb + 1) * nj, :]
                tt(K, "dve", yo, y1.rearrange("p (j c) -> p j c", j=nj), t2_.rearrange("p (j c) -> p j c", j=nj), ALU.mult,
                   [R("lnT", 0), R("lnT", 1)], [R("yT", fc)])
        for tb in range(W // 512):
            tg0 = t0 + tb * 512
            for ob in range(D // 256):
                def rhs(kc, tb=tb):
                    return yT[:, kc, tb * 512:(tb + 1) * 512]
                dense_block(K, K.s5_wo[ia], ob, NCH, rhs, lambda kc: [R("yT", kc)], 512, [0, 1])
                dense_block(K, K.s5_wg[ia], ob, NCH, rhs, lambda kc: [R("yT", kc)], 512, [2, 3])
                for o in range(2):
                    sg = K.lnT[o][:, :]
                    S.op("act", lambda e, sg=sg, o=o: e.activation(out=sg, in_=K.ps[2 + o][:, :], func=AF.Sigmoid), reads=[R("ps", 2 + o)], writes=[R("lnT", o)])
                    tt(K, "dve", sg, sg, K.ps[o][:, :], ALU.mult, [R("lnT", o), R("ps", o)], [R("lnT", o)])
                    resid_ln_accum(K, tg0, 512, ob * 2 + o, sg, [R("lnT", o)])
            ln_finish(K, layer, 0, tg0, 512, False)
    def out_state(src_ap, src_res, dst):
        bank = 4
        S.op("pe", lambda e: e.transpose(out=K.ps[bank][:, 0:128], in_=src_ap, identity=K.ident_f[:, :]),
             reads=src_res + [R("ident_f")], writes=[R("ps", bank)])
        ot = K.otile[0]
        copy_op(K, "dve", ot[:, 0:128], K.ps[bank][:, 0:128], [R("ps", bank)], [R("otile", 0)])
        S.op("pool", lambda e: e.dma_start(out=dst, in_=ot[:, 0:128]), reads=[R("otile", 0)], writes=[R("sout")], dma=True)
    if any(p[2] == "B" for p in passes):
        out_state(FINp[:, :], [R("FINp")], K.ssm_p[ia])
    if any(p[2] == "C" for p in passes):
        for s in range(NS):
            out_state(FINs[:, s, :], [R("FINs")], K.ssm_s[ia, s])
    S.barrier()


AG = 256


def dense_block_tm(K, w2d, ob, t0, banks):
    S = K.S
    for kt in range(2):
        wi = load_w_tile(K, wtile(w2d, kt, ob))
        wbf = K.wbf[wi]
        for t in range(2):
            for k in range(8):
                kc = kt * 8 + k
                first = (kt == 0 and k == 0)
                last = (kt == 1 and k == 7)
                lhs = K.xT[:, kc, t0 + t * 128:t0 + (t + 1) * 128]
                o_ap = K.ps[banks[t]][:, 0:256]
                S.op("pe", lambda e, lhs=lhs, wbf=wbf, k=k, o_ap=o_ap, first=first, last=last:
                     e.matmul(o_ap, lhsT=lhs, rhs=wbf[:, k * 256:(k + 1) * 256], start=first, stop=last),
                     reads=[R("wbf", wi)] + xr(t0 + t * 128, 128, [kc]), writes=[R("ps", banks[t])], tick=(last or (t == 1 and k == 7)))


def attn_layer(K, layer):
    nc, S = K.nc, K.S
    S.barrier()
    bg = Arena(K, K.big_off, 65536)
    KT = bg.get("KT", [128, 16, 768], BF16)
    V = bg.get("V", [128, 6, 2048], BF16)
    oT = bg.get("oT", [128, 16, AG], BF16)
    toep = bg.get("toep", [128, 16, 2, 128], BF16)
    sc = Arena(K, K.scr_off, K.scr_size)
    qT = sc.get("qT", [128, 16, AG], BF16)
    PT = [sc.get("PT%d" % i, [128, 640], BF16) for i in range(2)]
    Ssb = sc.get("Ssb", [128, 256], F32)
    otok = [sc.get("otok%d" % i, [128, 128], BF16) for i in range(2)]
    farcol = sc.get("farcol", [128, 16], F32)
    rc = [sc.get("rc%d" % i, [128, 1], F32) for i in range(2)]
    wqkv = K.at_wqkv
    scale = float(128 ** -0.5)

    si = next_stg(K)
    stg = K.stg[si]
    for half in range(2):
        S.op("sp", lambda e, half=half, stg=stg: e.dma_start(out=stg[:, :].rearrange("p (h r q) -> p h r q", h=8, r=2), in_=K.at_toep[:, half * 8:(half + 1) * 8]),
             writes=[R("stg", si)], dma=True)
        S.op("pool", lambda e, half=half, stg=stg: e.tensor_copy(out=toep[:, half * 8:(half + 1) * 8, :, :], in_=stg[:, :].rearrange("p (h r q) -> p h r q", h=8, r=2)),
             reads=[R("stg", si)], writes=[R("toep")])
    S.op("sp", lambda e: e.dma_start(out=farcol[:, :], in_=K.at_far), writes=[R("farcol")], dma=True)
    cnt = [0]

    def attend(h, nq, q_ap, tiles, sample_i, o_dst, o_dst_res):
        u = cnt[0] % 2
        cnt[0] += 1
        pt = PT[u]
        ptr = [R("PT", u)]
        far = [t for t in tiles if t[5] == "far"]
        near = [t for t in tiles if t[5] != "far"]
        for (r, kt_ap, kt_res, v_ap, v_res, kind) in tiles:
            bank = 4 if kind == "far" else 5
            col = (r if kind == "far" else r - 3) * nq
            o_ap = K.ps[bank][:, col:col + nq]
            S.op("pe", lambda e, o_ap=o_ap, kt_ap=kt_ap: e.matmul(o_ap, lhsT=kt_ap, rhs=q_ap, start=True, stop=True),
                 reads=kt_res + [R("qT")], writes=[R("ps", bank)], tick=True)
        if far:
            r0 = far[0][0]
            src = K.ps[4][:, r0 * nq:3 * nq]
            dst = pt[:, r0 * nq:3 * nq]
            S.op("act", lambda e: e.activation(out=dst, in_=src, func=AF.Exp, bias=farcol[:, h:h + 1], scale=1.0),
                 reads=[R("ps", 4), R("farcol")], writes=ptr)
        r0n = near[0][0]
        for (r, kt_ap, kt_res, v_ap, v_res, kind) in near:
            col = (r - 3) * nq
            if sample_i is None:
                bias_ap = toep[:, h, r - 3, :]
            elif r == 3:
                bias_ap = toep[:, h, 0, 0:64]
            else:
                bias_ap = toep[:, h, 1, (sample_i % 2) * 64:(sample_i % 2) * 64 + 64]
            s_ap = Ssb[:, col:col + nq]
            p_ap = K.ps[5][:, col:col + nq]
            S.op("dve", lambda e, s_ap=s_ap, p_ap=p_ap, bias_ap=bias_ap: e.tensor_tensor(out=s_ap, in0=p_ap, in1=bias_ap, op=ALU.add),
                 reads=[R("ps", 5), R("toep")], writes=[R("Ssb")])
        srcn = Ssb[:, (r0n - 3) * nq:2 * nq]
        dstn = pt[:, r0n * nq:5 * nq]
        S.op("act", lambda e: e.activation(out=dstn, in_=srcn, func=AF.Exp), reads=[R("Ssb")], writes=ptr)
        if sample_i is None:
            if far and far[0][0] == 0:
                S.op("pool", lambda e: e.memset(pt[0:64, 64:128], 0.0), writes=ptr)
            S.op("pool", lambda e: e.memset(pt[64:128, 4 * 128:4 * 128 + 64], 0.0), writes=ptr)
        else:
            oh = (1 - sample_i % 2) * 64
            S.op("pool", lambda e: e.memset(pt[oh:oh + 64, 4 * nq:5 * nq], 0.0), writes=ptr)
        ob_ = K.ps[6]
        for idx, (r, kt_ap, kt_res, v_ap, v_res, kind) in enumerate(tiles):
            l_ap = pt[:, r * nq:(r + 1) * nq]
            S.op("pe", lambda e, l_ap=l_ap, v_ap=v_ap, idx=idx: e.matmul(ob_[0:nq, 0:128], lhsT=l_ap, rhs=v_ap, start=(idx == 0), stop=(idx == len(tiles) - 1)),
                 reads=ptr + v_res, writes=[R("ps", 6)], tick=False)
        for idx, (r, kt_ap, kt_res, v_ap, v_res, kind) in enumerate(tiles):
            l_ap = pt[:, r * nq:(r + 1) * nq]
            S.op("pe", lambda e, l_ap=l_ap, idx=idx: e.matmul(ob_[0:nq, 128:129], lhsT=l_ap, rhs=K.ones_b[:, 0:1], start=(idx == 0), stop=(idx == len(tiles) - 1)),
                 reads=ptr + [R("ones_b")], writes=[R("ps", 6)], tick=(idx == len(tiles) - 1))
        rcu = rc[u]
        S.op("dve", lambda e: e.reciprocal(out=rcu[0:nq, :], in_=ob_[0:nq, 128:129]), reads=[R("ps", 6)], writes=[R("rc", u)])
        ot = otok[u]
        S.op("act", lambda e: e.activation(out=ot[0:nq, :], in_=ob_[0:nq, 0:128], func=AF.Copy, scale=rcu[0:nq, 0:1]),
             reads=[R("ps", 6), R("rc", u)], writes=[R("otok", u)])
        tp = K.ps[7][:, :].bitcast(BF16)
        S.op("pe", lambda e: e.transpose(out=tp[:, 0:nq], in_=ot[0:nq, :], identity=K.ident_b[0:nq, 0:nq]),
             reads=[R("otok", u), R("ident_b")], writes=[R("ps", 7)])
        copy_op(K, "dve", o_dst, tp[:, 0:nq], [R("ps", 7)], o_dst_res)

    nag = T // AG
    ags = K.cfg.get("attn_ags", list(range(nag)))
    for a in ags:
        t0 = a * AG
        is_s = a >= SEQ // AG
        want_out = (is_s or (t0 >= SEQ - 512)) and not K.cfg.get("no_out")
        if not is_s:
            kcol = (a % 3) * 256
        else:
            kcol = 512
        for ob in range(8):
            banks = [2 * (ob % 2), 2 * (ob % 2) + 1]
            dense_block(K, wqkv, 8 + ob, NCH, lambda kc: K.xT[:, kc, t0:t0 + AG], lambda kc: xr(t0, AG, [kc]), AG, banks)
            for o in range(2):
                hh = ob * 2 + o
                copy_op(K, evac_engine(K), KT[:, hh, kcol:kcol + AG], K.ps[banks[o]][:, 0:AG], [R("ps", banks[o])], [R("KT", kcol // 256)])
        for which in (K.cfg.get("whichs", [1, 2]) if want_out else [2]):
            for ob in range(8):
                banks = [2 * (ob % 2), 2 * (ob % 2) + 1]
                dense_block_tm(K, wqkv, which * 8 + ob, t0, banks)
                for t in range(2):
                    if not is_s:
                        slot = (2 * a + t) % 6
                    else:
                        slot = 4 + t
                    src = K.ps[banks[t]][:, 0:256]
                    if not want_out:
                        copy_op(K, "act", V[:, slot, ob * 256:(ob + 1) * 256], src, [R("ps", banks[t])], [R("V", slot)])
                    else:
                        ot = K.otile[t]
                        copy_op(K, "dve", ot[:, 0:256], src, [R("ps", banks[t])], [R("otile", t)])
                        if which == 2:
                            copy_op(K, "act", V[:, slot, ob * 256:(ob + 1) * 256], ot[:, 0:256], [R("otile", t)], [R("V", slot)])
                        if is_s:
                            dr = (a - SEQ // AG) * AG + t * 128
                            dst = (K.at_ks if which == 1 else K.at_vs)[dr:dr + 128, ob * 256:(ob + 1) * 256]
                        else:
                            dr = t0 - (SEQ - 512) + t * 128
                            dst = (K.at_kp if which == 1 else K.at_vp)[dr:dr + 128, ob * 256:(ob + 1) * 256]
                        if not K.cfg.get("no_dma"):
                            S.op("sp", lambda e, dst=dst, ot=ot: e.dma_start(out=dst, in_=ot[:, 0:256]), reads=[R("otile", t)], writes=[R("kvout")], dma=True)
        for ob in range(8):
            banks = [2 * (ob % 2), 2 * (ob % 2) + 1]
            dense_block(K, wqkv, ob, NCH, lambda kc: K.xT[:, kc, t0:t0 + AG], lambda kc: xr(t0, AG, [kc]), AG, banks)
            for o in range(2):
                hh = ob * 2 + o
                src = K.ps[banks[o]][:, 0:AG]
                dst = qT[:, hh, :]
                S.op("act", lambda e, src=src, dst=dst: e.activation(out=dst, in_=src, func=AF.Copy, scale=scale), reads=[R("ps", banks[o])], writes=[R("qT")])
        stage = K.cfg.get("attn_stage", 3)
        if stage < 2:
            continue
        if not is_s:
            for h in range(16):
                for qi in range(2):
                    qt = 2 * a + qi
                    tiles = []
                    for r in range(5):
                        j = qt - 4 + r
                        if j < 0:
                            continue
                        blk = (j // 2) % 3
                        col = blk * 256 + (j % 2) * 128
                        tiles.append((r, KT[:, h, col:col + 128], [R("KT", blk)], V[:, j % 6, h * 128:(h + 1) * 128], [R("V", j % 6)],
                                      "far" if r < 3 else "near"))
                    attend(h, 128, qT[:, h, qi * 128:(qi + 1) * 128], tiles, None, oT[:, h, qi * 128:(qi + 1) * 128], [R("oT", h)])
        else:
            sa = a - SEQ // AG
            for i in range(4):
                sq = sa * 4 + i
                for jt in range(4):
                    si = next_stg(K)
                    stg = K.stg[si]
                    S.op("sp", lambda e, stg=stg, sq=sq, jt=jt: e.dma_start(out=stg[:, :], in_=K.at_ck[sq, jt * 128:(jt + 1) * 128, :]), writes=[R("stg", si)], dma=True)
                    for cb in range(4):
                        bank = cb % 4
                        for jj in range(4):
                            c = cb * 4 + jj
                            o_ap = K.ps[bank][:, jj * 128:(jj + 1) * 128]
                            S.op("pe", lambda e, stg=stg, c=c, o_ap=o_ap: e.transpose(out=o_ap, in_=stg[:, c * 128:(c + 1) * 128], identity=K.ident_f[:, :]),
                                 reads=[R("stg", si), R("ident_f")], writes=[R("ps", bank)], tick=(jj == 3))
                        copy_op(K, evac_engine(K), KT[:, cb * 4:(cb + 1) * 4, jt * 128:(jt + 1) * 128],
                                K.ps[bank][:, :].rearrange("p (j t) -> p j t", j=4), [R("ps", bank)], [R("KT", jt // 2)])
                    si = next_stg(K)
                    stg = K.stg[si]
                    S.op("sp", lambda e, stg=stg, sq=sq, jt=jt: e.dma_start(out=stg[:, :], in_=K.at_cv[sq, jt * 128:(jt + 1) * 128, :]), writes=[R("stg", si)], dma=True)
                    copy_op(K, "act" if jt % 2 else "dve", V[:, jt, :], stg[:, :], [R("stg", si)], [R("V", jt)])
                for h in range(16):
                    tiles = []
                    for r in range(4):
                        tiles.append((r, KT[:, h, r * 128:(r + 1) * 128], [R("KT", r // 2)], V[:, r, h * 128:(h + 1) * 128], [R("V", r)],
                                      "far" if r < 3 else "near"))
                    pc = 512 + (i // 2) * 128
                    tiles.append((4, KT[:, h, pc:pc + 128], [R("KT", 2)], V[:, 4 + i // 2, h * 128:(h + 1) * 128], [R("V", 4 + i // 2)], "near"))
                    attend(h, 64, qT[:, h, i * 64:(i + 1) * 64], tiles, i, oT[:, h, i * 64:(i + 1) * 64], [R("oT", h)])
        if stage < 3:
            continue
        for ob in range(D // 256):
            banks = [2 * (ob % 2), 2 * (ob % 2) + 1]
            dense_block(K, K.at_wo, ob, NCH, lambda kc: oT[:, kc, :], lambda kc: [R("oT", kc)], AG, banks)
            for o in range(2):
                resid_ln_accum(K, t0, AG, ob * 2 + o, K.ps[banks[o]][:, 0:AG], [R("ps", banks[o])])
        ln_finish(K, layer, 0, t0, AG, False)
    S.barrier()


HG = 256
RMS_EPS = 1e-6


def dense_block_tm64(K, w2d, ob, t0, banks):
    S = K.S
    wis = [load_w_tile(K, wtile(w2d, kt, ob)) for kt in range(2)]
    for c in range(4):
        o_ap = K.ps[banks[c // 2]][0:64, (c % 2) * 256:(c % 2) * 256 + 256]
        for kt in range(2):
            wbf = K.wbf[wis[kt]]
            for k in range(8):
                kc = kt * 8 + k
                first = (kt == 0 and k == 0)
                last = (kt == 1 and k == 7)
                lhs = K.xT[:, kc, t0 + c * 64:t0 + (c + 1) * 64]
                S.op("pe", lambda e, lhs=lhs, wbf=wbf, k=k, o_ap=o_ap, first=first, last=last:
                     e.matmul(o_ap, lhsT=lhs, rhs=wbf[:, k * 256:(k + 1) * 256], start=first, stop=last),
                     reads=[R("wbf", wis[kt])] + xr(t0 + c * 64, 64, [kc]), writes=[R("ps", banks[c // 2])], tick=(k == 7))


def hgrn_layer(K, layer):
    nc, S = K.nc, K.S
    S.barrier()
    bg = Arena(K, K.big_off, 65536)
    qhT = bg.get("qhT", [128, 16, HG], BF16)
    ktT = bg.get("ktT", [128, 16, HG], BF16)
    mT = bg.get("mT", [128, 16, HG], BF16)
    GsT = bg.get("GsT", [128, 16, HG], BF16)
    Vc = bg.get("Vc", [64, 4, 2048], BF16)
    Kc2 = [bg.get("Kc%d" % i, [64, 2048], BF16) for i in range(2)]
    Sbf = bg.get("Sbf", [128, 16, 128], BF16)
    ATm = [bg.get("ATm%d" % i, [64, 4, 64], BF16) for i in range(2)]
    omb = [bg.get("omb%d" % i, [64, 4, 128], BF16) for i in range(2)]
    sc = Arena(K, K.scr_off, K.scr_size)
    Sm = sc.get("Sm", [128, 16, 128], F32)
    ft = [K.lnT[0][:, 0:256], K.lnT[0][:, 256:512], K.lnT[1][:, 0:256], K.lnT[1][:, 256:512]]
    ebuf = [K.otile[0][:, 0:256], K.otile[1][:, 0:256]]
    sq = K.lnA[0:64, :]
    eL = sc.get("eL", [128, 16, 4], F32)
    ss = sc.get("ss", [64, 16], F32)
    ones64 = sc.get("ones64", [128, 64], F32)
    M64 = sc.get("M64", [64, 4, 64], F32)
    lbt = sc.get("lbt", [128, 16], F32)
    oml = sc.get("oml", [128, 16], F32)
    ngt = sc.get("ngt", [128, 1], F32)
    lg = sc.get("lg", [128, 4, 16], F32)
    den = sc.get("den", [128, 16], F32)
    epsr = sc.get("epsr", [64, 1], F32)
    w_in = K.hg_win
    C = [R("hgc")]

    S.op("pool", lambda e: e.memset(ones64[:, :], 1.0), writes=C)
    S.op("pool", lambda e: e.memset(epsr[:, :], RMS_EPS), writes=C)
    S.op("pool", lambda e: e.memset(M64[:, :, :], 1.0), writes=C)
    S.op("pool", lambda e: e.affine_select(out=M64[:, :, :], in_=M64[:, :, :], pattern=[[0, 4], [1, 64]], compare_op=ALU.is_ge,
                                            fill=0.0, base=0, channel_multiplier=-1), reads=C, writes=C)
    S.op("sp", lambda e: e.dma_start(out=ngt[:, :], in_=K.hg_ng), writes=C, dma=True)
    S.op("sp", lambda e: e.dma_start(out=lg[:, :, :], in_=K.hg_lb), writes=C, dma=True)
    S.op("act", lambda e: e.activation(out=lg[:, :, :], in_=lg[:, :, :], func=AF.Exp), reads=C, writes=C)
    tt(K, "dve", den[:, :], lg[:, 0, :], lg[:, 1, :], ALU.add, C, C)
    tt(K, "dve", den[:, :], den[:, :], lg[:, 2, :], ALU.add, C, C)
    tt(K, "dve", den[:, :], den[:, :], lg[:, 3, :], ALU.add, C, C)
    S.op("dve", lambda e: e.reciprocal(out=den[:, :], in_=den[:, :]), reads=C, writes=C)
    S.op("pool", lambda e: e.memset(lbt[:, :], 0.0), writes=C)
    for l in range(1, layer + 1):
        tt(K, "dve", lbt[:, :], lbt[:, :], lg[:, l, :], ALU.add, C, C)
    tt(K, "dve", lbt[:, :], lbt[:, :], den[:, :], ALU.mult, C, C)
    S.op("dve", lambda e: e.tensor_scalar(out=oml[:, :], in0=lbt[:, :], scalar1=-1.0, scalar2=1.0, op0=ALU.mult, op1=ALU.add), reads=C, writes=C)
    S.op("pool", lambda e: e.memset(Sm[:, :, :], 0.0), writes=[R("Sm", h) for h in range(16)])
    S.op("pool", lambda e: e.memset(Sbf[:, :, :], 0.0), writes=[R("Sbf", h) for h in range(16)])

    ntg = T // HG
    tgs = K.cfg.get("hgrn_tgs", list(range(ntg)))
    ucnt = [0]
    for a in tgs:
        t0 = a * HG
        is_s = a >= SEQ // HG
        for ob in range(8):
            banks = [2 * (ob % 2), 2 * (ob % 2) + 1]
            dense_block_tm64(K, w_in, 2 * 8 + ob, t0, banks)
            for b in range(2):
                src = K.ps[banks[b]][0:64, :].rearrange("p (c n) -> p c n", c=2)
                dst = Vc[:, 2 * b:2 * b + 2, ob * 256:(ob + 1) * 256]
                copy_op(K, "act", dst, src, [R("ps", banks[b])], [R("Vc")])
        for hp in range(8):
            banks = [2 * (hp % 2), 2 * (hp % 2) + 1]
            dense_block(K, w_in, 24 + hp, NCH, lambda kc: K.xT[:, kc, t0:t0 + HG], lambda kc: xr(t0, HG, [kc]), HG, banks)
            for o in range(2):
                h = hp * 2 + o
                tmp = K.lnT[o][:, 0:HG]
                psg = K.ps[banks[o]][:, 0:HG]
                S.op("act", lambda e, tmp=tmp, psg=psg: e.activation(out=tmp, in_=psg, func=AF.Sigmoid), reads=[R("ps", banks[o])], writes=[R("lnT", o)])
                tt(K, "dve", GsT[:, h, :], tmp, psg, ALU.mult, [R("lnT", o), R("ps", banks[o])], [R("GsT", h)])
        for hp in range(8):
            dense_block(K, w_in, 8 + hp, NCH, lambda kc: K.xT[:, kc, t0:t0 + HG], lambda kc: xr(t0, HG, [kc]), HG, [0, 1])
            for o in range(2):
                h = hp * 2 + o
                psf = K.ps[o][:, 0:HG]
                f_, omf, b_, en = ft[0], ft[1], ft[2], ft[3]
                eb = ebuf[o]
                FT = [R("lnT", 0), R("lnT", 1)]
                S.op("act", lambda e, psf=psf: e.activation(out=f_, in_=psf, func=AF.Sigmoid), reads=[R("ps", o)], writes=FT)
                S.op("dve", lambda e, h=h: e.tensor_scalar(out=f_, in0=f_, scalar1=oml[:, h:h + 1], scalar2=lbt[:, h:h + 1], op0=ALU.mult, op1=ALU.add),
                     reads=FT + C, writes=FT)
                S.op("dve", lambda e: e.tensor_scalar(out=omf, in0=f_, scalar1=-1.0, scalar2=1.0, op0=ALU.mult, op1=ALU.add), reads=FT, writes=FT)
                S.op("act", lambda e: e.activation(out=f_, in_=f_, func=AF.Ln), reads=FT, writes=FT)
                for c in range(4):
                    S.op("dve", lambda e, c=c: e.tensor_tensor_scan(out=b_[:, c * 64:(c + 1) * 64], data0=ones64[:, :], data1=f_[:, c * 64:(c + 1) * 64],
                                                                    initial=0.0, op0=ALU.mult, op1=ALU.add), reads=FT + C, writes=FT)
                S.op("act", lambda e, eb=eb: e.activation(out=eb, in_=b_, func=AF.Exp), reads=FT, writes=[R("otile", o)])
                S.op("act", lambda e: e.activation(out=en, in_=b_, func=AF.Exp, scale=-1.0), reads=FT, writes=FT)
                tt(K, "dve", ktT[:, h, :], omf, en, ALU.mult, FT, [R("ktT", h)])
                copy_op(K, "pool", eL[:, h, :], ebuf[o][:, 63:HG:64], [R("otile", o)], [R("eL")])
            dense_block(K, w_in, hp, NCH, lambda kc: K.xT[:, kc, t0:t0 + HG], lambda kc: xr(t0, HG, [kc]), HG, [2, 3])
            for o in range(2):
                h = hp * 2 + o
                tt(K, "dve", qhT[:, h, :], K.ps[2 + o][:, 0:HG], ebuf[o], ALU.mult, [R("ps", 2 + o), R("otile", o)], [R("qhT", h)])
        for c in range(4):
            if is_s:
                sq_i = (a - SEQ // HG) * 4 + c
                S.op("sp", lambda e, sq_i=sq_i: e.dma_start(out=Sm[:, :, :], in_=K.hg_s0[sq_i].rearrange("h k v -> k h v")),
                     writes=[R("Sm", h) for h in range(16)], dma=True)
                copy_op(K, "act", Sbf[:, :, :], Sm[:, :, :], [R("Sm", h) for h in range(16)], [R("Sbf", h) for h in range(16)])
            csl = slice(c * 64, (c + 1) * 64)
            Kc = Kc2[c % 2]
            KR = [R("Kc", c % 2)]
            for hg in range(4):
                tp = K.ps[7][:, :].bitcast(BF16)
                for hh in range(4):
                    h = hg * 4 + hh
                    i_ap = ktT[:, h, c * 64:(c + 1) * 64]
                    o_ap = tp[0:64, hh * 128:(hh + 1) * 128]
                    S.op("pe", lambda e, i_ap=i_ap, o_ap=o_ap: e.transpose(out=o_ap, in_=i_ap, identity=K.ident_b[:, :]),
                         reads=[R("ktT", h), R("ident_b")], writes=[R("ps", 7)], tick=(hh == 3))
                copy_op(K, evac_engine(K), Kc[:, hg * 512:(hg + 1) * 512], tp[0:64, 0:512], [R("ps", 7)], KR)
            for hg in range(4):
                u = ucnt[0] % 2
                ucnt[0] += 1
                at, ob_ = ATm[u], omb[u]
                for hh in range(4):
                    h = hg * 4 + hh
                    o_ap = K.ps[4][0:64, hh * 64:(hh + 1) * 64]
                    l_ap, r_ap = ktT[:, h, csl], qhT[:, h, csl]
                    S.op("pe", lambda e, o_ap=o_ap, l_ap=l_ap, r_ap=r_ap: e.matmul(o_ap, lhsT=l_ap, rhs=r_ap, start=True, stop=True),
                         reads=[R("ktT", h), R("qhT", h)], writes=[R("ps", 4)], tick=(hh == 3))
                tt(K, "dve", at[:, :, :], K.ps[4][0:64, 0:256].rearrange("p (h t) -> p h t", h=4), M64[:, :, :], ALU.mult, [R("ps", 4)] + C, [R("ATm", u)])
                for hh in range(4):
                    h = hg * 4 + hh
                    o_ap = K.ps[5][0:64, hh * 128:(hh + 1) * 128]
                    v_ap = Vc[:, c, h * 128:(h + 1) * 128]
                    a_ap = at[:, hh, :]
                    q_ap = qhT[:, h, csl]
                    s_ap = Sbf[:, h, :]
                    S.op("pe", lambda e, o_ap=o_ap, a_ap=a_ap, v_ap=v_ap: e.matmul(o_ap, lhsT=a_ap, rhs=v_ap, start=True, stop=False),
                         reads=[R("ATm", u), R("Vc")], writes=[R("ps", 5)], tick=False)
                    S.op("pe", lambda e, o_ap=o_ap, q_ap=q_ap, s_ap=s_ap: e.matmul(o_ap, lhsT=q_ap, rhs=s_ap, start=False, stop=True),
                         reads=[R("qhT", h), R("Sbf", h)], writes=[R("ps", 5)], tick=(hh == 3))
                for hh in range(4):
                    h = hg * 4 + hh
                    o_ap = K.ps[6][:, hh * 128:(hh + 1) * 128]
                    k_ap = Kc[:, h * 128:(h + 1) * 128]
                    v_ap = Vc[:, c, h * 128:(h + 1) * 128]
                    S.op("pe", lambda e, o_ap=o_ap, k_ap=k_ap, v_ap=v_ap: e.matmul(o_ap, lhsT=k_ap, rhs=v_ap, start=True, stop=True),
                         reads=KR + [R("Vc")], writes=[R("ps", 6)], tick=(hh == 3))
                for hh in range(4):
                    h = hg * 4 + hh
                    sm = Sm[:, h, :]
                    e_ap = eL[:, h, c:c + 1]
                    S.op("dve", lambda e, sm=sm, e_ap=e_ap: e.tensor_scalar(out=sm, in0=sm, scalar1=e_ap, scalar2=None, op0=ALU.mult),
                         reads=[R("Sm", h), R("eL")], writes=[R("Sm", h)])
                    p_ap = K.ps[6][:, hh * 128:(hh + 1) * 128]
                    S.op("dve", lambda e, sm=sm, e_ap=e_ap, p_ap=p_ap: e.scalar_tensor_tensor(out=sm, in0=p_ap, scalar=e_ap, in1=sm, op0=ALU.mult, op1=ALU.add),
                         reads=[R("Sm", h), R("eL"), R("ps", 6)], writes=[R("Sm", h)])
                    copy_op(K, "act", Sbf[:, h, :], sm, [R("Sm", h)], [R("Sbf", h)])
                S.op("act", lambda e: e.activation(out=sq[:, :], in_=K.ps[5][0:64, :], func=AF.Square), reads=[R("ps", 5)], writes=[R("lnA")])
                ssg = ss[:, hg * 4:(hg + 1) * 4]
                S.op("dve", lambda e, ssg=ssg: e.tensor_reduce(out=ssg, in_=sq[:, :].rearrange("p (h v) -> p h v", h=4), axis=AX.X, op=ALU.add),
                     reads=[R("lnA")], writes=[R("ss")])
                S.op("act", lambda e, ssg=ssg: e.activation(out=ssg, in_=ssg, func=AF.Sqrt, bias=epsr[:, 0:1], scale=1.0 / 128), reads=[R("ss")] + C, writes=[R("ss")])
                S.op("dve", lambda e, ssg=ssg: e.reciprocal(out=ssg, in_=ssg), reads=[R("ss")], writes=[R("ss")])
                for hh in range(4):
                    h = hg * 4 + hh
                    p_ap = K.ps[5][0:64, hh * 128:(hh + 1) * 128]
                    r_ap = ss[:, h:h + 1]
                    d_ap = ob_[:, hh, :]
                    S.op("act", lambda e, p_ap=p_ap, r_ap=r_ap, d_ap=d_ap: e.activation(out=d_ap, in_=p_ap, func=AF.Copy, scale=r_ap),
                         reads=[R("ps", 5), R("ss")], writes=[R("omb", u)])
                tp = K.ps[7][:, :].bitcast(BF16)
                for hh in range(4):
                    i_ap = ob_[:, hh, :]
                    o_ap = tp[:, hh * 64:(hh + 1) * 64]
                    S.op("pe", lambda e, i_ap=i_ap, o_ap=o_ap: e.transpose(out=o_ap, in_=i_ap, identity=K.ident_b[0:64, 0:64]),
                         reads=[R("omb", u), R("ident_b")], writes=[R("ps", 7)], tick=(hh == 3))
                m_dst = mT[:, hg * 4:(hg + 1) * 4, csl]
                g_src = GsT[:, hg * 4:(hg + 1) * 4, csl]
                t_src = tp[:, 0:256].rearrange("p (h t) -> p h t", h=4)
                S.op("dve", lambda e, m_dst=m_dst, g_src=g_src, t_src=t_src: e.scalar_tensor_tensor(out=m_dst, in0=t_src, scalar=ngt[:, 0:1], in1=g_src, op0=ALU.mult, op1=ALU.mult),
                     reads=[R("ps", 7)] + C + [R("GsT", hg * 4 + hh) for hh in range(4)], writes=[R("mT", hg * 4 + hh) for hh in range(4)])
            last_prompt = (not is_s) and (a == SEQ // HG - 1) and c == 3
            if is_s or last_prompt:
                dst = K.hg_ss[(a - SEQ // HG) * 4 + c] if is_s else K.hg_sp
                S.op("sp", lambda e, dst=dst: e.dma_start(out=dst.rearrange("h k v -> k h v"), in_=Sm[:, :, :]),
                     reads=[R("Sm", h) for h in range(16)], writes=[R("hgout")], dma=True)
        for ob in range(D // 256):
            banks = [2 * (ob % 2), 2 * (ob % 2) + 1]
            dense_block(K, K.hg_wo, ob, NCH, lambda kc: mT[:, kc, :], lambda kc: [R("mT", kc)], HG, banks)
            for o in range(2):
                resid_ln_accum(K, t0, HG, ob * 2 + o, K.ps[banks[o]][:, 0:HG], [R("ps", banks[o])])
        ln_finish(K, layer, 0, t0, HG, False)
    S.barrier()


def s5_host_layout(inp, b):
    a_re, a_im, ldt = inp["ssm_a_re"], inp["ssm_a_im"], inp["ssm_log_dt"]
    na = a_re.shape[0]
    out = {}
    are_t = a_re.transpose(0, 2, 1)
    aim_t = a_im.transpose(0, 2, 1)
    l1 = np.stack([np.concatenate([are_t, are_t], 1), np.concatenate([aim_t, aim_t], 1),
                   np.broadcast_to(ldt[:, None, :], (na, 128, 128))], 1)
    out["s5_l1"] = l1
    rows = np.stack([np.repeat(a_re, 16, axis=1), np.repeat(a_im, 16, axis=1),
                     np.broadcast_to(np.repeat(ldt, 16, axis=1)[:, :, None], (na, D, 64))], 2)
    out["s5_rows"] = rows.reshape(na, NCH, 128, 3, 64)
    bre = inp["ssm_b_re"].transpose(0, 1, 3, 2).reshape(na, D, 64)
    bim = inp["ssm_b_im"].transpose(0, 1, 3, 2).reshape(na, D, 64)
    out["s5_bT"] = np.stack([bre, bim], 2).reshape(na, NCH, 128, 2, 64)
    cre = inp["ssm_c_re"].transpose(0, 1, 3, 2)
    cim = inp["ssm_c_im"].transpose(0, 1, 3, 2)
    cc = np.concatenate([cre, cim], 2)
    out["s5_cT"] = cc.reshape(na, NCH, 8, 128, 16).transpose(0, 1, 3, 2, 4)
    out["s5_d"] = inp["ssm_d"].reshape(na, NCH, 128).transpose(0, 2, 1)
    h0 = np.concatenate([inp["state_ssm_re"][:, 8 * b:8 * b + 8], inp["state_ssm_im"][:, 8 * b:8 * b + 8]], -1)
    out["s5_h0"] = h0
    out["s5_wo"] = inp["ssm_w_out"]
    out["s5_wg"] = inp["ssm_w_gate"]
    return {k: np.ascontiguousarray(v, dtype=np.float32) for k, v in out.items()}


def attn_host_layout(inp, b):
    tab = inp["attn_rel_bias"][0]
    kp = np.arange(128)[:, None, None]
    r = np.array([-1, 0])[None, :, None]
    q = np.arange(128)[None, None, :]
    idx = np.clip(q - (128 * r + kp), -128, 128) + 128
    toep = tab[:, idx].transpose(1, 0, 2, 3)
    far = np.broadcast_to(tab[None, :, 256], (128, 16))
    out = {"at_wqkv": inp["attn_w_qkv"][0], "at_wo": inp["attn_w_o"][0], "at_toep": toep, "at_far": far,
           "at_ck": inp["cache_attn_k"][0, 8 * b:8 * b + 8].reshape(NS, 512, D),
           "at_cv": inp["cache_attn_v"][0, 8 * b:8 * b + 8].reshape(NS, 512, D)}
    return {k: np.ascontiguousarray(v, dtype=np.float32) for k, v in out.items()}


def hgrn_host_layout(inp, b):
    out = {"hg_win": inp["hgrn_w_in"][0], "hg_wo": inp["hgrn_w_o"][0],
           "hg_lb": inp["hgrn_lb_logits"].reshape(4, NCH, 128).transpose(2, 0, 1),
           "hg_ng": inp["hgrn_norm_g"][0].reshape(128, 1),
           "hg_s0": inp["state_hgrn"][0, 8 * b:8 * b + 8]}
    return {k: np.ascontiguousarray(v, dtype=np.float32) for k, v in out.items()}


def make_core_inputs(inp, c):
    b = c % 4
    x_in = np.concatenate([inp["x_prompt"][b], inp["x_sample"][8 * b:8 * b + 8].reshape(NS * DS, D)], axis=0)
    lnp = np.stack([inp["ln_mix_g"], inp["ln_mix_b"], inp["ln_ffn_g"], inp["ln_ffn_b"]], 0)
    lnp = lnp.reshape(4, DEPTH, NCH, 128).transpose(3, 0, 1, 2).reshape(128, 4 * DEPTH * NCH)
    m = {
        "x_in": np.ascontiguousarray(x_in, dtype=np.float32),
        "ffn_w1": np.ascontiguousarray(inp["ffn_w1"], dtype=np.float32),
        "ffn_w2": np.ascontiguousarray(inp["ffn_w2"], dtype=np.float32),
        "lnp": np.ascontiguousarray(lnp, dtype=np.float32),
    }
    if "ssm_a_re" in inp:
        m.update(s5_host_layout(inp, b))
    if "attn_w_qkv" in inp:
        m.update(attn_host_layout(inp, b))
    if "hgrn_w_in" in inp:
        m.update(hgrn_host_layout(inp, b))
    return m


_NC_CACHE = {}


def kernel(**inp):
    inp = {k: np.asarray(v) for k, v in inp.items()}
    if "nc" not in _NC_CACHE:
        _NC_CACHE["nc"] = build({})
    nc = _NC_CACHE["nc"]
    maps4 = [make_core_inputs(inp, c) for c in range(4)]
    in_maps = [maps4[c % 4] for c in range(8)]
    res = run_bass_kernel_spmd(nc, in_maps, core_ids=list(range(8)))
    r = res.results
    f32 = np.float32
    yp = np.stack([r[c]["y_out"][:SEQ] for c in range(4)]).astype(f32)
    ys = np.concatenate([r[c]["y_out"][SEQ:].reshape(NS, DS, D) for c in range(4)]).astype(f32)
    ssm_p = np.stack([r[c]["ssm_p"] for c in range(4)], 1)
    ssm_s = np.concatenate([r[c]["ssm_s"] for c in range(4)], 1)
    kp = np.stack([r[c]["at_kp"] for c in range(4)]).reshape(1, 4, 512, 16, 128)
    vp = np.stack([r[c]["at_vp"] for c in range(4)]).reshape(1, 4, 512, 16, 128)
    ks = np.concatenate([r[c]["at_ks"].reshape(NS, DS, 16, 128) for c in range(4)])[None]
    vs = np.concatenate([r[c]["at_vs"].reshape(NS, DS, 16, 128) for c in range(4)])[None]
    hp = np.stack([r[c]["hg_sp"] for c in range(4)])[None]
    hs = np.concatenate([r[c]["hg_ss"] for c in range(4)])[None]
    out = (yp, ys, ssm_p[..., :64], ssm_p[..., 64:], kp, vp, hp, ssm_s[..., :64], ssm_s[..., 64:], ks, vs, hs)
    return tuple(np.ascontiguousarray(o, dtype=f32) for o in out)
```
